# Optimizing a Trainium2 kernel written in Bass

```python
import math
import jax
import jax.numpy as jnp
from jax import lax
import numpy as np

D_MODEL = 1024
BATCH = 4
SEQ = 4096
DEPTH = 4
DEC_BATCH = 32
DEC_SEQ = 32
PAST_LEN = 2048

CHUNK = 64
Q_BLOCK = 128
GATHER_ROWS = 128
N_MIXERS = 3
HEAD_DIM = 64
N_HEADS_A = D_MODEL // HEAD_DIM
N_HEADS_B = D_MODEL // (2 * HEAD_DIM)
N_HEADS_C = D_MODEL // HEAD_DIM
N_IDX_HEADS = 8
IDX_DIM = 64
TOPK_MAX = 256
D_FF = 4 * D_MODEL
ROPE_THETA = 10000.0
NORM_EPS = 1e-6
SUBLN_EPS = 1e-5
ATTN_SCALE = HEAD_DIM ** -0.5
IDX_SCALE = IDX_DIM ** -0.5
IDX_HEAD_SCALE = N_IDX_HEADS ** -0.5
FORGET_BIAS_LO = 2.0
FORGET_BIAS_HI = 6.0
N_A = len(range(0, DEPTH, N_MIXERS))
N_B = len(range(1, DEPTH, N_MIXERS))
N_C = len(range(2, DEPTH, N_MIXERS))

kernel_name = 'hybrid_fox_diff_dsa_stream_step'


def rms_norm(x, g, eps=NORM_EPS):
    xf = x.astype(jnp.float32)
    y = xf * lax.rsqrt(jnp.mean(xf * xf, axis=-1, keepdims=True) + eps)
    return (y * g.astype(jnp.float32)).astype(x.dtype)


def rope(x, pos):
    half = x.shape[-1] // 2
    inv = ROPE_THETA ** (-jnp.arange(half, dtype=jnp.float32) / half)
    ang = pos.astype(jnp.float32)[:, None] * inv[None, :]
    cos = jnp.cos(ang)[None, :, None, :]
    sin = jnp.sin(ang)[None, :, None, :]
    xf = x.astype(jnp.float32)
    x1, x2 = xf[..., :half], xf[..., half:]
    return jnp.concatenate([x1 * cos - x2 * sin, x2 * cos + x1 * sin], axis=-1).astype(x.dtype)


def sweep_query_blocks(fn, q_pos, *q_args):
    t = q_pos.shape[0]
    qb = Q_BLOCK if t % Q_BLOCK == 0 else t
    nb = t // qb
    blocked = tuple(jnp.moveaxis(a.reshape(a.shape[0], nb, qb, *a.shape[2:]), 1, 0) for a in q_args)
    out = lax.map(lambda xs: fn(xs[0], *xs[1]), (q_pos.reshape(nb, qb), blocked))
    out = jnp.moveaxis(out, 0, 1)
    return out.reshape(out.shape[0], t, *out.shape[3:])


def fox_mixer(h, pos, past, w_qkv, w_f, b_f, w_o):
    b, t, _ = h.shape
    q, k, v = jnp.split(h @ w_qkv, 3, axis=-1)
    q = q.reshape(b, t, N_HEADS_A, HEAD_DIM)
    k = k.reshape(b, t, N_HEADS_A, HEAD_DIM)
    v = v.reshape(b, t, N_HEADS_A, HEAD_DIM)
    logf = jax.nn.log_sigmoid((h @ w_f + b_f).astype(jnp.float32))
    if past is None:
        kk, vv, lf, k_pos = k, v, logf, pos
    else:
        pk, pv, plf, ppos = past
        kk = jnp.concatenate([pk.astype(k.dtype), k], axis=1)
        vv = jnp.concatenate([pv.astype(v.dtype), v], axis=1)
        lf = jnp.concatenate([plf.astype(jnp.float32), logf], axis=1)
        k_pos = jnp.concatenate([ppos, pos])
    cum = jnp.cumsum(lf, axis=1)
    cum_k = jnp.moveaxis(cum, 1, 2)

    def block(qpos, qb, cqb):
        logits = jnp.einsum('bqhd,bshd->bhqs', qb, kk).astype(jnp.float32) * ATTN_SCALE
        logits = logits + jnp.moveaxis(cqb, 1, 2)[..., None] - cum_k[:, :, None, :]
        mask = k_pos[None, :] <= qpos[:, None]
        p = jax.nn.softmax(jnp.where(mask, logits, -jnp.inf), axis=-1).astype(vv.dtype)
        return jnp.einsum('bhqs,bshd->bqhd', p, vv)

    o = sweep_query_blocks(block, pos, q, cum[:, -t:])
    y = o.reshape(b, t, D_MODEL) @ w_o
    return y, k, v, logf.astype(h.dtype)


def diff_lambda_init(layer):
    return 0.8 - 0.6 * math.exp(-0.3 * layer)


def diff_mixer(h, pos, past, layer, w_qkv, lam_q1, lam_k1, lam_q2, lam_k2, subln_g, w_o):
    b, t, _ = h.shape
    q, k, v = jnp.split(h @ w_qkv, 3, axis=-1)
    q = rope(q.reshape(b, t, 2 * N_HEADS_B, HEAD_DIM), pos)
    k = rope(k.reshape(b, t, 2 * N_HEADS_B, HEAD_DIM), pos)
    v = v.reshape(b, t, N_HEADS_B, 2 * HEAD_DIM)
    if past is None:
        kk, vv, k_pos = k, v, pos
    else:
        pk, pv, ppos = past
        kk = jnp.concatenate([pk.astype(k.dtype), k], axis=1)
        vv = jnp.concatenate([pv.astype(v.dtype), v], axis=1)
        k_pos = jnp.concatenate([ppos, pos])
    lam_init = diff_lambda_init(layer)
    f32 = jnp.float32
    lam = (jnp.exp(jnp.sum(lam_q1.astype(f32) * lam_k1.astype(f32)))
           - jnp.exp(jnp.sum(lam_q2.astype(f32) * lam_k2.astype(f32))) + lam_init)
    k_chunk = k_pos // CHUNK

    def block(qpos, qb):
        nq = qb.shape[1]
        logits = jnp.einsum('bqgd,bsgd->bgqs', qb, kk).astype(f32) * ATTN_SCALE
        mask = k_chunk[None, :] <= (qpos // CHUNK)[:, None]
        p = jax.nn.softmax(jnp.where(mask, logits, -jnp.inf), axis=-1)
        p = p.reshape(b, N_HEADS_B, 2, nq, -1)
        a = (p[:, :, 0] - lam * p[:, :, 1]).astype(vv.dtype)
        return jnp.einsum('bhqs,bshe->bqhe', a, vv)

    o = sweep_query_blocks(block, pos, q)
    o = rms_norm(o, subln_g, SUBLN_EPS) * (1.0 - lam_init)
    return o.reshape(b, t, D_MODEL) @ w_o, k, v


def dsa_select(qi, wi, ki_all, q_pos, k_pos, top_k):
    k_chunk = k_pos // CHUNK

    def block(qpos, qib, wib):
        s = jnp.einsum('bqid,bsd->bqis', qib, ki_all).astype(jnp.float32) * IDX_SCALE
        score = jnp.einsum('bqis,bqi->bqs', jax.nn.relu(s), wib.astype(jnp.float32))
        adm = k_chunk[None, :] <= (qpos // CHUNK)[:, None]
        score = jnp.where(adm[None], score, -jnp.inf)
        _, idx = lax.top_k(score, top_k)
        return idx

    idx = sweep_query_blocks(block, q_pos, qi, wi)
    ok = k_chunk[idx] <= (q_pos // CHUNK)[None, :, None]
    return idx, ok


def dsa_attend(q, kk, vv, idx, ok):
    b, t, nh, dh = q.shape
    length = kk.shape[1]
    top_k = idx.shape[-1]
    n = b * t
    rows = math.gcd(n, GATHER_ROWS)
    flat_idx = (idx + (jnp.arange(b, dtype=jnp.int32) * length)[:, None, None]).reshape(n // rows, rows, top_k)
    k_flat = kk.reshape(b * length, nh, dh)
    v_flat = vv.reshape(b * length, nh, dh)

    def block(args):
        qb, ib, okb = args
        kg = k_flat[ib]
        vg = v_flat[ib]
        logits = jnp.einsum('rhd,rkhd->rhk', qb, kg).astype(jnp.float32) * ATTN_SCALE
        p = jax.nn.softmax(jnp.where(okb[:, None, :], logits, -jnp.inf), axis=-1).astype(vg.dtype)
        return jnp.einsum('rhk,rkhd->rhd', p, vg)

    out = lax.map(block, (q.reshape(n // rows, rows, nh, dh), flat_idx, ok.reshape(n // rows, rows, top_k)))
    return out.reshape(b, t, nh, dh)


def dsa_mixer(h, pos, past, w_qkv, w_qidx, w_kidx, w_widx, w_o):
    b, t, _ = h.shape
    q, k, v = jnp.split(h @ w_qkv, 3, axis=-1)
    q = rope(q.reshape(b, t, N_HEADS_C, HEAD_DIM), pos)
    k = rope(k.reshape(b, t, N_HEADS_C, HEAD_DIM), pos)
    v = v.reshape(b, t, N_HEADS_C, HEAD_DIM)
    qi = rope((h @ w_qidx).reshape(b, t, N_IDX_HEADS, IDX_DIM), pos)
    ki = rope((h @ w_kidx)[:, :, None, :], pos)[:, :, 0]
    wi = (h @ w_widx) * IDX_HEAD_SCALE
    if past is None:
        kk, vv, kki, k_pos = k, v, ki, pos
    else:
        pk, pv, pki, ppos = past
        kk = jnp.concatenate([pk.astype(k.dtype), k], axis=1)
        vv = jnp.concatenate([pv.astype(v.dtype), v], axis=1)
        kki = jnp.concatenate([pki.astype(ki.dtype), ki], axis=1)
        k_pos = jnp.concatenate([ppos, pos])
    top_k = min(TOPK_MAX, kk.shape[1] // 4)
    idx, ok = dsa_select(qi, wi, kki, pos, k_pos, top_k)
    o = dsa_attend(q, kk, vv, idx, ok)
    return o.reshape(b, t, D_MODEL) @ w_o, k, v, ki


def sq_relu_ffn(h, w_up, w_down):
    return jnp.square(jax.nn.relu(h @ w_up)) @ w_down


def setup_inputs(seed: int = 0) -> dict:
    key = jax.random.key(seed)
    ks = iter(jax.random.split(key, 40))
    f32 = jnp.float32

    def nrm(shape, scale=1.0):
        return jax.random.normal(next(ks), shape, f32) * scale

    def gain(shape):
        return 1.0 + nrm(shape, 0.01)

    d = D_MODEL
    dsc = d ** -0.5
    inputs = {
        'x_prompt': nrm((BATCH, SEQ, d)),
        'x_sample': nrm((DEC_BATCH, DEC_SEQ, d)),
        'cache_a_k': nrm((N_A, DEC_BATCH, PAST_LEN, N_HEADS_A, HEAD_DIM)),
        'cache_a_v': nrm((N_A, DEC_BATCH, PAST_LEN, N_HEADS_A, HEAD_DIM)),
        'cache_a_logf': jax.nn.log_sigmoid(nrm((N_A, DEC_BATCH, PAST_LEN, N_HEADS_A)) + 0.5 * (FORGET_BIAS_LO + FORGET_BIAS_HI)),
        'cache_b_k': nrm((N_B, DEC_BATCH, PAST_LEN, 2 * N_HEADS_B, HEAD_DIM)),
        'cache_b_v': nrm((N_B, DEC_BATCH, PAST_LEN, N_HEADS_B, 2 * HEAD_DIM)),
        'cache_c_k': nrm((N_C, DEC_BATCH, PAST_LEN, N_HEADS_C, HEAD_DIM)),
        'cache_c_v': nrm((N_C, DEC_BATCH, PAST_LEN, N_HEADS_C, HEAD_DIM)),
        'cache_c_kidx': nrm((N_C, DEC_BATCH, PAST_LEN, IDX_DIM)),
        'norm_mix_g': gain((DEPTH, d)),
        'norm_ffn_g': gain((DEPTH, d)),
        'norm_final_g': gain((d,)),
        'a_w_qkv': nrm((N_A, d, 3 * d), dsc),
        'a_w_f': nrm((N_A, d, N_HEADS_A), dsc),
        'a_b_f': jax.random.uniform(next(ks), (N_A, N_HEADS_A), f32, FORGET_BIAS_LO, FORGET_BIAS_HI),
        'a_w_o': nrm((N_A, d, d), dsc),
        'b_w_qkv': nrm((N_B, d, 3 * d), dsc),
        'b_lam_q1': nrm((N_B, HEAD_DIM), 0.1),
        'b_lam_k1': nrm((N_B, HEAD_DIM), 0.1),
        'b_lam_q2': nrm((N_B, HEAD_DIM), 0.1),
        'b_lam_k2': nrm((N_B, HEAD_DIM), 0.1),
        'b_subln_g': gain((N_B, 2 * HEAD_DIM)),
        'b_w_o': nrm((N_B, d, d), dsc),
        'c_w_qkv': nrm((N_C, d, 3 * d), dsc),
        'c_w_qidx': nrm((N_C, d, N_IDX_HEADS * IDX_DIM), dsc),
        'c_w_kidx': nrm((N_C, d, IDX_DIM), dsc),
        'c_w_widx': nrm((N_C, d, N_IDX_HEADS), dsc),
        'c_w_o': nrm((N_C, d, d), dsc),
        'ffn_w_up': nrm((DEPTH, d, D_FF), dsc),
        'ffn_w_down': nrm((DEPTH, D_FF, d), D_FF ** -0.5),
    }
    return inputs


def reference(x_prompt, x_sample, cache_a_k, cache_a_v, cache_a_logf, cache_b_k, cache_b_v,
              cache_c_k, cache_c_v, cache_c_kidx, norm_mix_g, norm_ffn_g, norm_final_g,
              a_w_qkv, a_w_f, a_b_f, a_w_o, b_w_qkv, b_lam_q1, b_lam_k1, b_lam_q2, b_lam_k2,
              b_subln_g, b_w_o, c_w_qkv, c_w_qidx, c_w_kidx, c_w_widx, c_w_o, ffn_w_up, ffn_w_down):
    t_p = x_prompt.shape[1]
    t_s = x_sample.shape[1]
    past_len = cache_a_k.shape[2]
    pos_p = jnp.arange(t_p, dtype=jnp.int32)
    pos_s = past_len + jnp.arange(t_s, dtype=jnp.int32)
    past_pos = jnp.arange(past_len, dtype=jnp.int32)

    hp, hs = x_prompt, x_sample
    a_prompt, a_sample, b_prompt, b_sample, c_prompt, c_sample = [], [], [], [], [], []
    for i in range(DEPTH):
        kind = i % N_MIXERS
        j = i // N_MIXERS
        hp_n = rms_norm(hp, norm_mix_g[i])
        hs_n = rms_norm(hs, norm_mix_g[i])
        if kind == 0:
            wts = (a_w_qkv[j], a_w_f[j], a_b_f[j], a_w_o[j])
            yp, *sp = fox_mixer(hp_n, pos_p, None, *wts)
            ys, *ss = fox_mixer(hs_n, pos_s, (cache_a_k[j], cache_a_v[j], cache_a_logf[j], past_pos), *wts)
            a_prompt.append(sp)
            a_sample.append(ss)
        elif kind == 1:
            wts = (b_w_qkv[j], b_lam_q1[j], b_lam_k1[j], b_lam_q2[j], b_lam_k2[j], b_subln_g[j], b_w_o[j])
            yp, *sp = diff_mixer(hp_n, pos_p, None, i, *wts)
            ys, *ss = diff_mixer(hs_n, pos_s, (cache_b_k[j], cache_b_v[j], past_pos), i, *wts)
            b_prompt.append(sp)
            b_sample.append(ss)
        else:
            wts = (c_w_qkv[j], c_w_qidx[j], c_w_kidx[j], c_w_widx[j], c_w_o[j])
            yp, *sp = dsa_mixer(hp_n, pos_p, None, *wts)
            ys, *ss = dsa_mixer(hs_n, pos_s, (cache_c_k[j], cache_c_v[j], cache_c_kidx[j], past_pos), *wts)
            c_prompt.append(sp)
            c_sample.append(ss)
        hp = hp + yp
        hs = hs + ys
        hp = hp + sq_relu_ffn(rms_norm(hp, norm_ffn_g[i]), ffn_w_up[i], ffn_w_down[i])
        hs = hs + sq_relu_ffn(rms_norm(hs, norm_ffn_g[i]), ffn_w_up[i], ffn_w_down[i])

    y_prompt = rms_norm(hp, norm_final_g)
    y_sample = rms_norm(hs, norm_final_g)

    new_a_k_p = jnp.stack([s[0] for s in a_prompt])
    new_a_v_p = jnp.stack([s[1] for s in a_prompt])
    new_a_logf_p = jnp.stack([s[2] for s in a_prompt])
    new_b_k_p = jnp.stack([s[0] for s in b_prompt])
    new_b_v_p = jnp.stack([s[1] for s in b_prompt])
    new_c_k_p = jnp.stack([s[0] for s in c_prompt])
    new_c_v_p = jnp.stack([s[1] for s in c_prompt])
    new_c_kidx_p = jnp.stack([s[2] for s in c_prompt])
    new_a_k_s = jnp.stack([s[0] for s in a_sample])
    new_a_v_s = jnp.stack([s[1] for s in a_sample])
    new_a_logf_s = jnp.stack([s[2] for s in a_sample])
    new_b_k_s = jnp.stack([s[0] for s in b_sample])
    new_b_v_s = jnp.stack([s[1] for s in b_sample])
    new_c_k_s = jnp.stack([s[0] for s in c_sample])
    new_c_v_s = jnp.stack([s[1] for s in c_sample])
    new_c_kidx_s = jnp.stack([s[2] for s in c_sample])
    return (y_prompt, y_sample,
            new_a_k_p, new_a_v_p, new_a_logf_p, new_b_k_p, new_b_v_p, new_c_k_p, new_c_v_p, new_c_kidx_p,
            new_a_k_s, new_a_v_s, new_a_logf_s, new_b_k_s, new_b_v_s, new_c_k_s, new_c_v_s, new_c_kidx_s)
```

```python
import os
import math
from contextlib import ExitStack

import numpy as np
import ml_dtypes
import concourse.bass as bass
import concourse.mybir as mybir
from concourse.bass_utils import run_bass_kernel_spmd

F32 = mybir.dt.float32
BF16 = mybir.dt.bfloat16
ALU = mybir.AluOpType
AF = mybir.ActivationFunctionType
AX = mybir.AxisListType

ENGS = ("pe", "dve", "act", "pool", "sp")

D = 1024
NH = 16
HD = 64
NTP = int(os.environ.get("MK_NTP", "32"))
NT = NTP + 1
TOK = NT * 128
PTOK = NTP * 128
NSEQ = 4
TS = 32
PAST = 2048
NKP = PAST // 128
DEPTH = int(os.environ.get("MK_DEPTH", "4"))
STOP = os.environ.get("MK_STOP", "")
NEG = -29952.0
EPS = 1e-6
SUBLN_EPS = 1e-5
NBIS = 16
BIS_RANGE = 16.0


class SemGroup:
    __slots__ = ("name", "dsem", "dcount", "uid")
    _n = 0

    def __init__(self, name):
        self.name = name
        self.dsem = None
        self.dcount = 0
        SemGroup._n += 1
        self.uid = SemGroup._n


class Buf:
    __slots__ = ("name", "last_w", "readers", "grp")

    def __init__(self, name, grp=None):
        self.name = name
        self.last_w = None
        self.readers = {}
        self.grp = grp if grp is not None else SemGroup(name)


class Sched:
    SEM_CAP = 24000

    def __init__(self, nc, stack, same_engine_sync=True):
        self.nc = nc
        self.stack = stack
        self.streams = {k: [] for k in ENGS}
        self.sem_pool = []
        self.n_sems = 0
        self.epoch = 0
        self.psem = {}
        self.pcount = {}
        for k in ENGS:
            self.psem[k], self.pcount[k] = self.get_sem()
        self.waited = {k: {} for k in ENGS}
        self.dma_sems = []
        self.same_engine_sync = same_engine_sync
        self.n_instr = 0
        self.n_wait = 0

    def get_sem(self):
        if self.sem_pool:
            return self.sem_pool.pop()
        self.n_sems += 1
        return (self.stack.enter_context(self.nc.semaphore("sm%d" % self.n_sems)), 0)

    def put_sem(self, h, v):
        if v < self.SEM_CAP:
            self.sem_pool.append((h, v))

    def _need(self, e, tok, waits):
        if tok is None:
            return
        if tok[0] == "c":
            _, pe_, idx, ep = tok
            if ep != self.epoch:
                return
            if pe_ == e and (e == "pe" or not self.same_engine_sync):
                return
            key = ("c", pe_)
            val = idx
            sem = self.psem[pe_]
        else:
            _, g, cnt = tok
            if g.dsem is None:
                return
            key = ("d", g.uid)
            val = g.dcount
            sem = g.dsem
        if self.waited[e].get(key, 0) >= val:
            return
        waits[key] = (sem, max(val, waits.get(key, (None, 0))[1]))

    def _deps(self, e, reads, writes):
        waits = {}
        for b in reads:
            self._need(e, b.last_w, waits)
        for b in writes:
            self._need(e, b.last_w, waits)
            for r in b.readers.values():
                self._need(e, r, waits)
        for key, (sem, val) in waits.items():
            self.waited[e][key] = val
        self.n_instr += 1
        self.n_wait += len(waits)
        return list(waits.values())

    def _mark(self, tok, reads, writes):
        for b in writes:
            b.last_w = tok
            b.readers = {}
        rk = (tok[0], tok[1]) if tok[0] == "c" else ("d", tok[1].uid)
        for b in reads:
            if b not in writes:
                b.readers[rk] = tok

    def op(self, e, fn, reads=(), writes=()):
        waits = self._deps(e, reads, writes)
        self.pcount[e] += 1
        tok = ("c", e, self.pcount[e], self.epoch)
        self.streams[e].append((waits, fn, self.psem[e], 1))
        self._mark(tok, reads, writes)

    def dma(self, q, fn, reads=(), writes=(), owner=None):
        waits = self._deps(q, reads, writes)
        if owner is None:
            owner = writes[0] if writes else reads[0]
        g = owner.grp
        if g.dsem is None:
            g.dsem, g.dcount = self.get_sem()
            self.dma_sems.append(g)
        g.dcount += 16
        tok = ("d", g, g.dcount)
        self.streams[q].append((waits, fn, g.dsem, 16))
        self._mark(tok, reads, writes)

    def barrier(self):
        for e in ENGS:
            waits = []
            for pe_ in ENGS:
                if pe_ == e and e in ("pe", "sp"):
                    continue
                v = self.pcount[pe_]
                if self.waited[e].get(("c", pe_), 0) < v:
                    waits.append((self.psem[pe_], v))
                    self.waited[e][("c", pe_)] = v
            for g in self.dma_sems:
                if g.dcount and self.waited[e].get(("d", g.uid), 0) < g.dcount:
                    waits.append((g.dsem, g.dcount))
                    self.waited[e][("d", g.uid)] = g.dcount
            if waits:
                self.streams[e].append((waits, None, None, 0))

    def end_phase(self):
        self.barrier()
        for g in self.dma_sems:
            self.put_sem(g.dsem, g.dcount)
            for e in ENGS:
                self.waited[e].pop(("d", g.uid), None)
            g.dsem = None
            g.dcount = 0
        self.dma_sems = []
        self.epoch += 1
        old = [(self.psem[k], self.pcount[k]) for k in ENGS]
        for k in ENGS:
            self.psem[k], self.pcount[k] = self.get_sem()
            for e in ENGS:
                self.waited[e][("c", k)] = self.pcount[k]
        for h, v in old:
            self.put_sem(h, v)

    def emit(self):
        nc = self.nc
        with nc.Block() as block:
            def mk(ekey):
                def body(engine):
                    for waits, fn, sem, inc in self.streams[ekey]:
                        for (s, v) in waits:
                            engine.wait_ge(s, v)
                        if fn is not None:
                            fn(engine).then_inc(sem, inc)
                return body
            block.tensor(mk("pe"))
            block.vector(mk("dve"))
            block.scalar(mk("act"))
            block.gpsimd(mk("pool"))
            block.sync(mk("sp"))


class TT:
    __slots__ = ("t", "b")

    def __init__(self, t, name, grp=None):
        self.t = t
        self.b = Buf(name, grp)

    def __getitem__(self, k):
        return self.t[k]


class Builder:
    def __init__(self):
        self.nc = bass.Bass("TRN2", target_bir_lowering=False)
        self.stack = ExitStack()
        self.S = None
        self.uid = 0

    def sb(self, ctx, name, shape, dt, grp=None):
        self.uid += 1
        nm = "%s_%d" % (name, self.uid)
        return TT(ctx.enter_context(self.nc.sbuf_tensor(nm, shape, dt)), nm, grp)

    def ps(self, ctx, name, shape, dt):
        self.uid += 1
        nm = "%s_%d" % (name, self.uid)
        return TT(ctx.enter_context(self.nc.psum_tensor(nm, shape, dt)), nm)

    def dram_in(self, name, shape, dt):
        return self.nc.dram_tensor(name, list(shape), dt, kind="ExternalInput").ap()

    def dram_out(self, name, shape, dt):
        return self.nc.dram_tensor(name, list(shape), dt, kind="ExternalOutput").ap()

    def dram_scr(self, name, shape, dt):
        return self.nc.dram_tensor(name, list(shape), dt, kind="Internal").ap()

    def ld(self, out, in_, w, r=(), q="sp", owner=None):
        self.S.dma(q, lambda e: e.dma_start(out=out, in_=in_), reads=list(r), writes=list(w), owner=owner)

    def mm(self, out, lhsT, rhs, start, stop, r, w):
        self.S.op("pe", lambda e: e.matmul(out, lhsT=lhsT, rhs=rhs, start=start, stop=stop), reads=r, writes=w)

    def tr(self, out, in_, ident, r, w):
        self.S.op("pe", lambda e: e.transpose(out=out, in_=in_, identity=ident), reads=r, writes=w)

    def act(self, out, in_, func, r, w, bias=None, scale=None, accum=None, eng="act"):
        kw = {}
        if bias is not None:
            kw["bias"] = bias
        if scale is not None:
            kw["scale"] = scale
        if accum is not None:
            kw["accum_out"] = accum
        self.S.op("act", lambda e: e.activation(out=out, in_=in_, func=func, **kw), reads=r, writes=w)

    def tt(self, eng, out, in0, in1, op, r, w):
        self.S.op(eng, lambda e: e.tensor_tensor(out=out, in0=in0, in1=in1, op=op), reads=r, writes=w)

    def ts(self, eng, out, in0, s1, op0, r, w, s2=None, op1=None, accum=None):
        kw = {}
        if op1 is not None:
            kw["op1"] = op1
        if accum is not None:
            kw["accum_out"] = accum
        self.S.op(eng, lambda e: e.tensor_scalar(out=out, in0=in0, scalar1=s1, scalar2=s2, op0=op0, **kw),
                  reads=r, writes=w)

    def stt(self, out, in0, scalar, in1, op0, op1, r, w):
        self.S.op("dve", lambda e: e.scalar_tensor_tensor(out=out, in0=in0, scalar=scalar, in1=in1, op0=op0, op1=op1),
                  reads=r, writes=w)

    def cp(self, eng, out, in_, r, w):
        if eng == "act":
            self.S.op("act", lambda e: e.copy(out=out, in_=in_), reads=r, writes=w)
        else:
            self.S.op(eng, lambda e: e.tensor_copy(out=out, in_=in_), reads=r, writes=w)

    def memset(self, eng, ap, val, w):
        self.S.op(eng, lambda e: e.memset(ap, val), reads=[], writes=w)


def ring(B, ctx, name, shape, dt, n, psum=False):
    return [(B.ps if psum else B.sb)(ctx, name, shape, dt) for _ in range(n)]


def macros():
    ms = [(t0, 4) for t0 in range(0, NTP, 4)]
    ms.append((NTP, 1))
    return ms


class Prog:
    def __init__(self):
        B = self.B = Builder()
        di, do, ds = B.dram_in, B.dram_out, B.dram_scr
        self.xin = di("xin", [TOK, D], F32)
        self.wqkv = di("wqkv", [4, D, 3 * D], F32)
        self.wf = di("wf", [2, D, 16], F32)
        self.bf = di("bf", [2, 16], F32)
        self.wqi = di("wqi", [D, 512], F32)
        self.wki = di("wki", [D, 64], F32)
        self.wwi = di("wwi", [D, 8], F32)
        self.wo = di("wo", [4, D, D], F32)
        self.wup = di("wup", [4, D, 4 * D], F32)
        self.wdn = di("wdn", [4, 4 * D, D], F32)
        self.g_mix = di("g_mix", [4, D], F32)
        self.g_ffn = di("g_ffn", [4, D], F32)
        self.g_fin = di("g_fin", [D], F32)
        self.lamv = di("lamv", [4, 64], F32)
        self.subg = di("subg", [128, 1], F32)
        self.a_kT = di("a_kT", [2, NSEQ, 8, 128, PAST], F32)
        self.a_v = di("a_v", [2, NSEQ, PAST, D], F32)
        self.a_lf = di("a_lf", [2, NSEQ, PAST, 16], F32)
        self.b_kT = di("b_kT", [NSEQ, 8, 128, PAST], F32)
        self.b_v = di("b_v", [NSEQ, PAST, D], F32)
        self.c_kT = di("c_kT", [NSEQ, 8, 128, PAST], F32)
        self.c_v = di("c_v", [NSEQ, PAST, D], F32)
        self.c_kiT = di("c_kiT", [NSEQ, 64, PAST], F32)
        self.c_identb = di("c_identb", [128, 128], BF16)
        self.c_onesb = di("c_onesb", [128, 128], BF16)
        self.c_negI = di("c_negI", [128, 128], BF16)
        self.c_ones32 = di("c_ones32", [128, 128], F32)
        self.c_tri = di("c_tri", [128, 128], F32)
        self.c_triS = di("c_triS", [128, 128], F32)
        self.c_blk = di("c_blk", [128, 4, 128], F32)
        self.c_rope = di("c_rope", [TOK, 128], F32)
        self.c_maskF = di("c_maskF", [128, 4, 512], BF16)
        self.c_maskC = di("c_maskC", [128, 4, 512], BF16)
        self.c_maskS = di("c_maskS", [32, 32], BF16)
        self.c_cmaskA = di("c_cmaskA", [128, 128], F32)
        self.c_colm = di("c_colm", [128, 4, 128], BF16)
        self.c_aug3 = di("c_aug3", [128, 128], BF16)
        self.OUT_K = do("OUT_K", [4, TOK, D], F32)
        self.OUT_V = do("OUT_V", [4, TOK, D], F32)
        self.OUT_LF = do("OUT_LF", [2, TOK, 16], F32)
        self.OUT_KI = do("OUT_KI", [TOK, 64], F32)
        self.OUT_Y = do("OUT_Y", [TOK, D], F32)
        self.HRES = ds("HRES", [TOK, D], F32)
        self.QT = ds("QT", [8, 128, TOK], BF16)
        self.KT = ds("KT", [8, 128, TOK], BF16)
        self.CQT = ds("CQT", [8, 128, TOK], BF16)
        self.VS = ds("VS", [TOK, 16 * 128], BF16)
        self.VSS = ds("VSS", [128, 16 * 65], BF16)
        self.NLS = ds("NLS", [1, 1], F32)
        self.OT = ds("OT", [D, TOK], BF16)
        self.XT2 = ds("XT2", [NT, 128, D], BF16)
        self.NCS = ds("NCS", [128, 16], F32)
        self.QITs = ds("QITs", [4, 128, TOK], BF16)
        self.KITs = ds("KITs", [128, TOK], BF16)

    def load_w(self, stg, dst_ap_fn, src_ap_fn, nchunks, W, eng="pool", bufs=None):
        B = self.B
        for i in range(nchunks):
            s = stg[self.stg_i % len(stg)]
            self.stg_i += 1
            src = src_ap_fn(i)
            shp = list(src.shape)
            n = 1
            for d_ in shp[1:]:
                n *= d_
            sview = s[:, 0:n]
            if len(shp) == 3:
                sview = sview.rearrange("p (a b) -> p a b", a=shp[1])
            B.ld(sview, src, w=[s.b])
            ce = ("act", "dve", "act", eng)[self.stg_i % 4]
            B.cp(ce, dst_ap_fn(i), sview, r=[s.b], w=[bufs[i] if bufs else W.b])

    def load_consts(self, ctx):
        B = self.B
        C = {}
        for nm, shp, dt in [("identb", [128, 128], BF16), ("onesb", [128, 128], BF16), ("negI", [128, 128], BF16),
                            ("ones32", [128, 128], F32), ("tri", [128, 128], F32), ("triS", [128, 128], F32),
                            ("blk", [128, 4, 128], F32), ("maskF", [128, 4, 512], BF16),
                            ("maskC", [128, 4, 512], BF16), ("cmaskA", [128, 128], F32),
                            ("colm", [128, 4, 128], BF16), ("aug3", [128, 128], BF16)]:
            t = B.sb(ctx, "c_" + nm, shp, dt)
            src = getattr(self, "c_" + nm)
            B.ld(t[:], src[tuple(slice(None) for _ in shp)], w=[t.b])
            C[nm] = t
        t = B.sb(ctx, "c_maskS", [32, 32], BF16)
        B.ld(t[:], self.c_maskS[:, :], w=[t.b])
        C["maskS"] = t
        self.C = C

    def rope(self, eng, x32, o_ap4, tabs, nh, scaled, tmp, r, w, split=None):
        B = self.B
        xv = x32[:, 0:nh * 64].rearrange("p (h t d) -> p h t d", h=nh, t=2)
        x1, x2 = xv[:, :, 0, :], xv[:, :, 1, :]
        ci, si = (2, 3) if scaled else (0, 1)
        cosb = tabs[:, ci, :].unsqueeze(1).to_broadcast([128, nh, 32])
        sinb = tabs[:, si, :].unsqueeze(1).to_broadcast([128, nh, 32])
        ta, tb = tmp
        tav = ta[:, 0:nh * 32].rearrange("p (h d) -> p h d", h=nh)
        tbv = tb[:, 0:nh * 32].rearrange("p (h d) -> p h d", h=nh)
        rr = list(r) + [x32.b, tabs.b]
        e2 = split if split is not None else eng
        B.tt(eng, tav, x1, cosb, ALU.mult, r=rr, w=[ta.b])
        B.tt(e2, tbv, x2, sinb, ALU.mult, r=rr, w=[tb.b])
        B.tt(eng, o_ap4[:, :, 0, :], tav, tbv, ALU.subtract, r=[ta.b, tb.b], w=w)
        B.tt(eng, tav, x2, cosb, ALU.mult, r=rr, w=[ta.b])
        B.tt(e2, tbv, x1, sinb, ALU.mult, r=rr, w=[tb.b])
        B.tt(eng, o_ap4[:, :, 1, :], tav, tbv, ALU.add, r=[ta.b, tb.b], w=w)

    def phase1(self, L, P):
        B, S, C = self.B, self.B.S, self.C
        kind, j = L % 3, L // 3
        extra = {0: 16, 1: 0, 2: 584}[kind]
        ncols = 3072 + extra
        ncb = (ncols + 511) // 512
        with ExitStack() as ctx:
            W = B.sb(ctx, "W", [128, 8, 3072 + 584], BF16)
            stg = ring(B, ctx, "stg", [128, 2048], F32, 3 if kind != 2 else 2)
            self.stg_i = 0
            g1 = B.sb(ctx, "g1", [128, D], F32)
            B.ld(g1[:], self.g_mix[L].partition_broadcast(128), w=[g1.b])
            hr = ring(B, ctx, "ht", [128, D], F32, 2)
            if kind != 0:
                tabr = ring(B, ctx, "tab", [128, 4, 32], F32, 2)
            B.ld(hr[0][:], (self.xin if L == 0 else self.HRES)[0:128, :], w=[hr[0].b])
            Wb = [Buf("Wc%d" % i) for i in range(16)]
            self.load_w(stg, lambda i: W[:, i // 2, (i % 2) * 1536:(i % 2 + 1) * 1536],
                        lambda i: self.wqkv[L, (i // 2) * 128:(i // 2 + 1) * 128, (i % 2) * 1536:(i % 2 + 1) * 1536],
                        16, W, bufs=Wb)
            if kind == 0:
                self.load_w(stg, lambda i: W[:, :, 3072:3088],
                            lambda i: self.wf[j].rearrange("(c p) n -> p c n", p=128), 1, W)
            if kind == 2:
                self.load_w(stg, lambda i: W[:, 4 * i:4 * i + 4, 3072:3584],
                            lambda i: self.wqi[i * 512:(i + 1) * 512, :].rearrange("(c p) n -> p c n", p=128), 2, W)
                self.load_w(stg, lambda i: W[:, :, 3584:3648],
                            lambda i: self.wki.rearrange("(c p) n -> p c n", p=128), 1, W)
                self.load_w(stg, lambda i: W[:, :, 3648:3656],
                            lambda i: self.wwi.rearrange("(c p) n -> p c n", p=128), 1, W)
            junk = B.sb(ctx, "junk", [128, D], BF16)
            ssr = ring(B, ctx, "ss", [128, 1], F32, 2)
            lnr = ring(B, ctx, "lnv", [128, 1], F32, 2)
            rsr = ring(B, ctx, "rstd", [128, 1], F32, 2)
            xnr = ring(B, ctx, "xn", [128, D], BF16, 2)
            xnTr = ring(B, ctx, "xnT", [128, 8, 128], BF16, 2)
            pT1 = B.ps(ctx, "pT1", [128, 8, 128], BF16)
            pA = ring(B, ctx, "pA", [128, 512], F32, 3, psum=True)
            pT2 = ring(B, ctx, "pT2", [128, 8, 128], BF16, 2, psum=True)
            pC = B.ps(ctx, "pC", [128, 512], F32)
            q32r = ring(B, ctx, "q32", [128, D], F32, 1) if kind != 0 else [None]
            k32r = ring(B, ctx, "k32", [128, D], F32, 1)
            v32r = ring(B, ctx, "v32", [128, D], F32, 1)
            qbr = ring(B, ctx, "qb", [128, D], BF16, 2)
            kbr = ring(B, ctx, "kb", [128, D], BF16, 2)
            vw = 16 * 128 if kind != 1 else D
            vbr = ring(B, ctx, "vb", [128, vw], BF16, 2)
            if kind != 1:
                for vb in vbr:
                    v4 = vb[:].rearrange("p (hp two e) -> p hp two e", two=2, e=128)
                    B.memset("pool", vb[:], 0.0, w=[vb.b])
                    B.memset("pool", v4[:, :, 0, 64:65], 1.0, w=[vb.b])
                    B.memset("pool", v4[:, :, 1, 0:1], 1.0, w=[vb.b])
                vbs = B.sb(ctx, "vbs", [128, 16 * 65], BF16)
                B.memset("pool", vbs[:].rearrange("p (h e) -> p h e", e=65)[:, :, 64:65], 1.0, w=[vbs.b])
            qTm = ring(B, ctx, "qTm", [128, 8, 128], BF16, 2)
            kTm = ring(B, ctx, "kTm", [128, 8, 128], BF16, 2)
            if kind != 0:
                k32o = ring(B, ctx, "k32o", [128, D], F32, 1)
                tmpA = [B.sb(ctx, "rtA", [128, 512], F32) for _ in range(2)]
                tmpB = [B.sb(ctx, "rtB", [128, 512], F32) for _ in range(2)]
            if kind == 0:
                cqTm = ring(B, ctx, "cqTm", [128, 8, 128], BF16, 2)
                bfb = B.sb(ctx, "bfb", [128, 16], F32)
                B.ld(bfb[:], self.bf[j].partition_broadcast(128), w=[bfb.b])
                xfr = ring(B, ctx, "xf", [128, 16], F32, 2)
                lfr = ring(B, ctx, "lf32", [128, 16], F32, 2)
                r1r = ring(B, ctx, "r1t", [128, 16], F32, 2)
                ACC = B.sb(ctx, "ACC", [128, 16], F32)
                B.memset("pool", ACC[:], 0.0, w=[ACC.b])
                CQ = B.sb(ctx, "CQ", [128, 16, 64], BF16)
                B.memset("pool", CQ[:], 0.0, w=[CQ.b])
                lfc = B.sb(ctx, "lfc", [128, NSEQ, NKP, 16], F32)
                for s in range(NSEQ):
                    B.ld(lfc[:, s, :, :], self.a_lf[j, s].rearrange("(kt p) h -> p kt h", p=128), w=[lfc.b])
                ACCP = B.sb(ctx, "ACCP", [128, NSEQ, 16], F32)
                B.memset("pool", ACCP[:], 0.0, w=[ACCP.b])
                NCUM, NCUMP = P["NCUM"], P["NCUMP"]
                for s in range(NSEQ):
                    for kt in range(NKP):
                        B.mm(pC[:, 0:16], C["tri"][:], lfc[:, s, kt, :], True, False, r=[C["tri"].b, lfc.b], w=[pC.b])
                        B.mm(pC[:, 0:16], C["ones32"][:], ACCP[:, s, :], False, True, r=[C["ones32"].b, ACCP.b], w=[pC.b])
                        B.ts("dve", NCUMP[:, s, kt, :], pC[:, 0:16], -1.0, ALU.mult, r=[pC.b], w=[NCUMP.b])
                        B.tt("pool", ACCP[:, s, :], ACCP[:, s, :], lfc[:, s, kt, :], ALU.add, r=[lfc.b, ACCP.b], w=[ACCP.b])
            if kind == 2:
                qi32r = ring(B, ctx, "qi32", [128, 512], F32, 2)
                qibr = ring(B, ctx, "qib", [128, 512], BF16, 2)
                kr32 = ring(B, ctx, "kr32", [128, 72], F32, 2)
                ki32r = ring(B, ctx, "ki32", [128, 64], F32, 2)
                kib2r = ring(B, ctx, "kib2", [128, 128], BF16, 2)
                pT3 = B.ps(ctx, "pT3", [128, 8, 128], BF16)
                qiTr = ring(B, ctx, "qiT", [128, 5, 128], BF16, 2)
                WI = P["WI"]
            elif kind == 0:
                pT3 = B.ps(ctx, "pT3", [128, 8, 128], BF16)
            idb = C["identb"]

            def prologue(t):
                rows = slice(t * 128, (t + 1) * 128)
                ht = hr[t % 2]
                if t > 0:
                    B.ld(ht[:], (self.xin if L == 0 else self.HRES)[rows, :], w=[ht.b])
                ss, lnv, rstd = ssr[t % 2], lnr[t % 2], rsr[t % 2]
                xn, xnT = xnr[t % 2], xnTr[t % 2]
                B.act(junk[:], ht[:], AF.Square, r=[ht.b], w=[junk.b, ss.b], accum=ss[:])
                B.act(lnv[:], ss[:], AF.Ln, r=[ss.b], w=[lnv.b], scale=1.0 / D, bias=EPS)
                B.act(rstd[:], lnv[:], AF.Exp, r=[lnv.b], w=[rstd.b], scale=-0.5)
                B.stt(xn[:], ht[:], rstd[:], g1[:], ALU.mult, ALU.mult, r=[ht.b, rstd.b, g1.b], w=[xn.b])
                for c in range(8):
                    B.tr(pT1[:, c, :], xn[:, c * 128:(c + 1) * 128], idb[:], r=[xn.b, idb.b], w=[pT1.b])
                B.cp("act", xnT[:], pT1[:], r=[pT1.b], w=[xnT.b])
                if kind != 0:
                    tabs = tabr[t % 2]
                    B.ld(tabs[:].rearrange("p a b -> p (a b)"), self.c_rope[rows, :], w=[tabs.b])

            prologue(0)
            for t in range(NT):
                rows = slice(t * 128, (t + 1) * 128)
                m, pos = t // 4, t % 4
                xn, xnT = xnr[t % 2], xnTr[t % 2]
                q32, k32, v32 = q32r[0], k32r[0], v32r[0]
                qb, kb, vb = qbr[t % 2], kbr[t % 2], vbr[t % 2]
                if kind != 0:
                    tabs = tabr[t % 2]
                for cb in range(ncb):
                    wcb = min(512, ncols - cb * 512)
                    pa = pA[cb % 3]
                    for c in range(8):
                        wbuf = Wb[2 * c + (1 if cb >= 3 else 0)] if cb < 6 else W.b
                        B.mm(pa[:, 0:wcb], xnT[:, c, :], W[:, c, cb * 512:cb * 512 + wcb], c == 0, c == 7,
                             r=[xnT.b, wbuf], w=[pa.b])
                    if cb == 2 and t + 1 < NT:
                        prologue(t + 1)
                    cs = slice((cb % 2) * 512, (cb % 2 + 1) * 512)
                    if cb < 2:
                        if kind == 0:
                            B.act(qb[:, cs], pa[:], AF.Copy, r=[pa.b], w=[qb.b], scale=0.125)
                        else:
                            B.cp("act", q32[:, cs], pa[:], r=[pa.b], w=[q32.b])
                    elif cb < 4:
                        B.cp("act", k32[:, cs], pa[:], r=[pa.b], w=[k32.b])
                    elif cb < 6:
                        B.cp("dve", v32[:, cs], pa[:], r=[pa.b], w=[v32.b])
                    elif kind == 0:
                        xf, lf32, r1t = xfr[t % 2], lfr[t % 2], r1r[t % 2]
                        B.tt("dve", xf[:], pa[:, 0:16], bfb[:], ALU.add, r=[pa.b, bfb.b], w=[xf.b])
                        B.act(xf[:], xf[:], AF.Exp, r=[xf.b], w=[xf.b], scale=-1.0)
                        B.act(xf[:], xf[:], AF.Ln, r=[xf.b], w=[xf.b], bias=1.0)
                        B.ts("dve", lf32[:], xf[:], -1.0, ALU.mult, r=[xf.b], w=[lf32.b])
                        B.ld(self.OUT_LF[j, rows, :], lf32[:], w=[], r=[lf32.b])
                    elif cb == 6:
                        qi32 = qi32r[t % 2]
                        B.cp("act", qi32[:], pa[:], r=[pa.b], w=[qi32.b])
                    else:
                        kr = kr32[t % 2]
                        B.cp("dve", kr[:], pa[:, 0:72], r=[pa.b], w=[kr.b])
                if kind == 0:
                    B.ld(self.OUT_K[L, rows, :], k32[:], w=[], r=[k32.b])
                    B.cp("pool", kb[:], k32[:], r=[k32.b], w=[kb.b])
                    ksrc = k32
                else:
                    ko = k32o[0]
                    self.rope("dve", q32, qb[:].rearrange("p (h t d) -> p h t d", h=16, t=2), tabs, 16, True,
                              (tmpA[0], tmpB[0]), r=[], w=[qb.b], split="pool")
                    self.rope("dve", k32, ko[:].rearrange("p (h t d) -> p h t d", h=16, t=2), tabs, 16, False,
                              (tmpA[1], tmpB[1]), r=[], w=[ko.b], split="pool")
                    B.ld(self.OUT_K[L, rows, :], ko[:], w=[], r=[ko.b])
                    B.cp("pool", kb[:], ko[:], r=[ko.b], w=[kb.b])
                B.ld(self.OUT_V[L, rows, :], v32[:], w=[], r=[v32.b])
                if kind != 1:
                    v4 = vb[:].rearrange("p (hp two e) -> p hp two e", two=2, e=128)
                    s4 = v32[:].rearrange("p (hp two e) -> p hp two e", two=2, e=64)
                    B.cp("pool", v4[:, :, 0, 0:64], s4[:, :, 0, :], r=[v32.b], w=[vb.b])
                    B.cp("pool", v4[:, :, 1, 64:128], s4[:, :, 1, :], r=[v32.b], w=[vb.b])
                    if t == NTP:
                        B.cp("pool", vbs[:].rearrange("p (h e) -> p h e", e=65)[:, :, 0:64],
                             v32[:].rearrange("p (h e) -> p h e", e=64), r=[v32.b], w=[vbs.b])
                        B.ld(self.VSS[:, :], vbs[:], w=[], r=[vbs.b])
                else:
                    B.cp("pool", vb[:], v32[:], r=[v32.b], w=[vb.b])
                B.ld(self.VS[rows, 0:vw], vb[:], w=[], r=[vb.b])
                qT, kT = qTm[t % 2], kTm[t % 2]
                for hp in range(8):
                    B.tr(pT2[0][:, hp, :], qb[:, hp * 128:(hp + 1) * 128], idb[:], r=[qb.b, idb.b], w=[pT2[0].b])
                B.cp("dve", qT[:], pT2[0][:], r=[pT2[0].b], w=[qT.b])
                for hp in range(8):
                    B.tr(pT2[1][:, hp, :], kb[:, hp * 128:(hp + 1) * 128], idb[:], r=[kb.b, idb.b], w=[pT2[1].b])
                B.cp("dve", kT[:], pT2[1][:], r=[pT2[1].b], w=[kT.b])
                if kind == 0:
                    lf32, r1t = lfr[t % 2], r1r[t % 2]
                    if t < NTP:
                        B.mm(pC[:, 0:16], C["tri"][:], lf32[:], True, False, r=[C["tri"].b, lf32.b], w=[pC.b])
                        B.mm(pC[:, 0:16], C["ones32"][:], ACC[:], False, True, r=[C["ones32"].b, ACC.b], w=[pC.b])
                        B.tt("pool", ACC[:], ACC[:], lf32[:], ALU.add, r=[lf32.b, ACC.b], w=[ACC.b])
                    else:
                        B.mm(pC[:, 0:16], C["triS"][:], lf32[:], True, False, r=[C["triS"].b, lf32.b], w=[pC.b])
                        for s in range(NSEQ):
                            B.mm(pC[:, 0:16], C["blk"][:, s, :], ACCP[:, s, :], False, s == NSEQ - 1,
                                 r=[C["blk"].b, ACCP.b], w=[pC.b])
                    B.ts("dve", NCUM[:, t, :], pC[:, 0:16], -1.0, ALU.mult, r=[pC.b], w=[NCUM.b])
                    if t == NTP:
                        B.ld(self.NCS[:, :], NCUM[:, t, :], w=[], r=[NCUM.b])
                    B.cp("dve", CQ[:, :, 0], pC[:, 0:16], r=[pC.b], w=[CQ.b])
                    B.tt("dve", r1t[:], pC[:, 0:16], CQ[:, :, 0], ALU.subtract, r=[pC.b, CQ.b], w=[r1t.b])
                    B.cp("dve", CQ[:, :, 1], r1t[:], r=[r1t.b], w=[CQ.b])
                    B.tt("dve", r1t[:], r1t[:], CQ[:, :, 1], ALU.subtract, r=[CQ.b, r1t.b], w=[r1t.b])
                    B.cp("dve", CQ[:, :, 2], r1t[:], r=[r1t.b], w=[CQ.b])
                    B.ts("dve", CQ[:, :, 3:6], CQ[:, :, 0:3], -1.0, ALU.mult, r=[CQ.b], w=[CQ.b])
                    cqT = cqTm[t % 2]
                    for hp in range(8):
                        B.tr(pT3[:, hp, :], CQ[:, 2 * hp:2 * hp + 2, :].rearrange("p a b -> p (a b)"), idb[:],
                             r=[CQ.b, idb.b], w=[pT3.b])
                    B.cp("act", cqT[:], pT3[:], r=[pT3.b], w=[cqT.b])
                if kind == 2:
                    qi32, qib, kr, ki32, kib2 = qi32r[t % 2], qibr[t % 2], kr32[t % 2], ki32r[t % 2], kib2r[t % 2]
                    self.rope("pool", qi32, qib[:].rearrange("p (h t d) -> p h t d", h=8, t=2), tabs, 8, False,
                              (tmpA[0], tmpB[0]), r=[], w=[qib.b])
                    self.rope("dve", kr, ki32[:].rearrange("p (h t d) -> p h t d", h=1, t=2), tabs, 1, False,
                              (tmpA[1], tmpB[1]), r=[], w=[ki32.b])
                    B.ld(self.OUT_KI[rows, :], ki32[:], w=[], r=[ki32.b])
                    B.cp("pool", kib2[:, 0:64], ki32[:], r=[ki32.b], w=[kib2.b])
                    B.cp("pool", kib2[:, 64:128], ki32[:], r=[ki32.b], w=[kib2.b])
                    B.ts("dve", WI[:, t, :], kr[:, 64:72], (8.0 ** -0.5) * 0.125, ALU.mult, r=[kr.b], w=[WI.b])
                    for pr in range(4):
                        B.tr(pT3[:, pr, :], qib[:, pr * 128:(pr + 1) * 128], idb[:], r=[qib.b, idb.b], w=[pT3.b])
                    B.tr(pT3[:, 4, :], kib2[:], idb[:], r=[kib2.b, idb.b], w=[pT3.b])
                    qiT = qiTr[t % 2]
                    B.cp("act", qiT[:], pT3[:, 0:5, :], r=[pT3.b], w=[qiT.b])
                    B.ld(self.QITs[:, :, rows].rearrange("h p n -> p h n"), qiT[:, 0:4, :], w=[], r=[qiT.b])
                    B.ld(self.KITs[:, rows], qiT[:, 4, :], w=[], r=[qiT.b])
                B.ld(self.QT[:, :, rows].rearrange("h p n -> p h n"), qT[:], w=[], r=[qT.b])
                B.ld(self.KT[:, :, rows].rearrange("h p n -> p h n"), kT[:], w=[], r=[kT.b])
                if kind == 0:
                    B.ld(self.CQT[:, :, rows].rearrange("h p n -> p h n"), cqT[:], w=[], r=[cqT.b])
            S.end_phase()

    def build(self):
        B = self.B
        with ExitStack() as top:
            B.S = Sched(B.nc, top)
            self.load_consts(top)
            for L in range(DEPTH):
                kind = L % 3
                with ExitStack() as lctx:
                    P = {}
                    if kind == 0:
                        P["NCUM"] = B.sb(lctx, "NCUM", [128, NT, 16], F32)
                        P["NCUMP"] = B.sb(lctx, "NCUMP", [128, NSEQ, NKP, 16], F32)
                    if kind == 2:
                        P["WI"] = B.sb(lctx, "WI", [128, NT, 8], F32)
                    self.phase1(L, P)
                    if STOP == "p1_%d" % L:
                        break
                    self.phase2(L, P)
                    if STOP == "p2_%d" % L:
                        break
                    self.phase3(L, P)
            B.S.end_phase()
            if os.environ.get("MK_VERBOSE"):
                print("instr per engine", {k: len(v) for k, v in B.S.streams.items()}, "waits", B.S.n_wait,
                      "sems", B.S.n_sems, flush=True)
            B.S.emit()
        return B.nc


def _bf(a):
    return np.asarray(a, np.float32).astype(ml_dtypes.bfloat16)


def make_consts():
    c = {}
    eye = np.eye(128, dtype=np.float32)
    c["c_identb"] = _bf(eye)
    c["c_onesb"] = _bf(np.ones((128, 128)))
    c["c_negI"] = _bf(NEG * eye)
    c["c_ones32"] = np.ones((128, 128), np.float32)
    kk = np.arange(128)
    c["c_tri"] = (kk[:, None] <= kk[None, :]).astype(np.float32)
    c["c_triS"] = ((kk[:, None] <= kk[None, :]) & (kk[:, None] // 32 == kk[None, :] // 32)).astype(np.float32)
    blk = np.zeros((128, 4, 128), np.float32)
    for s in range(4):
        blk[:, s, 32 * s:32 * s + 32] = 1.0
    c["c_blk"] = blk
    pos = np.concatenate([np.arange(PTOK), PAST + (np.arange(128) % 32)]).astype(np.float32)
    inv = (np.float32(10000.0) ** (-np.arange(32, dtype=np.float32) / np.float32(32))).astype(np.float32)
    ang = (pos[:, None] * inv[None, :]).astype(np.float32)
    cs, sn = np.cos(ang).astype(np.float32), np.sin(ang).astype(np.float32)
    c["c_rope"] = np.concatenate([cs, sn, cs * np.float32(0.125), sn * np.float32(0.125)], axis=1).astype(np.float32)
    a = np.arange(4)
    q = np.arange(512)
    kg = 128 * a[None, :, None] + kk[:, None, None]
    c["c_maskF"] = _bf((kg > q[None, None, :]).astype(np.float32))
    c["c_maskC"] = _bf(((kg // 64) > (q[None, None, :] // 64)).astype(np.float32))
    k32 = np.arange(32)
    c["c_maskS"] = _bf((k32[:, None] > k32[None, :]).astype(np.float32))
    c["c_cmaskA"] = np.where((kk[:, None] < 64) & (kk[None, :] >= 64), NEG, 0.0).astype(np.float32)
    colm = np.zeros((128, 4, 128), np.float32)
    for s in range(4):
        colm[:, s, 32 * s:32 * s + 32] = 1.0
    c["c_colm"] = _bf(colm)
    aug3 = np.zeros((128, 128), np.float32)
    aug3[0:3] = 1.0
    aug3[64:67] = 1.0
    c["c_aug3"] = _bf(aug3)
    return c


_PROG_CACHE = {}


def kernel(**inp):
    ncores = int(os.environ.get("MK_CORES", "8"))
    f32 = np.float32
    import time as _time
    _tp = _time.time()
    A = {k: np.asarray(v) for k, v in inp.items()}
    consts = make_consts()
    shared = {
        "wqkv": np.ascontiguousarray(np.stack([A["a_w_qkv"][0], A["b_w_qkv"][0], A["c_w_qkv"][0], A["a_w_qkv"][1]])),
        "wf": A["a_w_f"], "bf": A["a_b_f"],
        "wqi": A["c_w_qidx"][0], "wki": A["c_w_kidx"][0], "wwi": A["c_w_widx"][0],
        "wo": np.ascontiguousarray(np.stack([A["a_w_o"][0], A["b_w_o"][0], A["c_w_o"][0], A["a_w_o"][1]])),
        "wup": A["ffn_w_up"], "wdn": A["ffn_w_down"],
        "g_mix": A["norm_mix_g"], "g_ffn": A["norm_ffn_g"], "g_fin": A["norm_final_g"],
        "lamv": np.ascontiguousarray(np.stack([A["b_lam_q1"][0], A["b_lam_k1"][0], A["b_lam_q2"][0], A["b_lam_k2"][0]])),
        "subg": np.ascontiguousarray(A["b_subln_g"][0].reshape(128, 1)),
    }
    shared.update(consts)
    in_maps = []
    for c in range(ncores):
        sq = slice(4 * c, 4 * c + 4)
        m = dict(shared)
        m["xin"] = np.ascontiguousarray(np.concatenate(
            [A["x_prompt"][c % 4][:PTOK], A["x_sample"][sq].reshape(128, D)], axis=0))

        def kT(x):
            return np.ascontiguousarray(x.transpose(0, 2, 3, 1).reshape(NSEQ, 8, 128, PAST))
        m["a_kT"] = np.stack([kT(A["cache_a_k"][0, sq]), kT(A["cache_a_k"][1, sq])])
        m["a_v"] = np.ascontiguousarray(A["cache_a_v"][:, sq].reshape(2, NSEQ, PAST, D))
        m["a_lf"] = np.ascontiguousarray(A["cache_a_logf"][:, sq])
        m["b_kT"] = kT(A["cache_b_k"][0, sq])
        m["b_v"] = np.ascontiguousarray(A["cache_b_v"][0, sq].reshape(NSEQ, PAST, D))
        m["c_kT"] = kT(A["cache_c_k"][0, sq])
        m["c_v"] = np.ascontiguousarray(A["cache_c_v"][0, sq].reshape(NSEQ, PAST, D))
        m["c_kiT"] = np.ascontiguousarray(A["cache_c_kidx"][0, sq].transpose(0, 2, 1))
        in_maps.append(m)

    if os.environ.get("MK_VERBOSE"):
        print("prep time", _time.time() - _tp, flush=True)
    key = (NTP, DEPTH, STOP)
    _t0 = _time.time()
    if key not in _PROG_CACHE:
        _PROG_CACHE[key] = Prog().build()
    nc = _PROG_CACHE[key]
    if os.environ.get("MK_VERBOSE"):
        print("build time", _time.time() - _t0, flush=True)
    _t0 = _time.time()
    if os.environ.get("MK_TRACE"):
        res = run_bass_kernel_spmd(nc, in_maps, core_ids=list(range(ncores)), trace=True)
        print("exec_time_ns", res.exec_time_ns, flush=True)
    else:
        res = run_bass_kernel_spmd(nc, in_maps, core_ids=list(range(ncores)))
    if os.environ.get("MK_VERBOSE"):
        print("run time", _time.time() - _t0, flush=True)
    R = res.results

    def gp(name, layer=None):
        outs = []
        for b in range(4):
            x = R[b % ncores][name] if layer is None else R[b % ncores][name][layer]
            x = x[:PTOK]
            if PTOK < 4096:
                x = np.concatenate([x, np.zeros((4096 - PTOK,) + x.shape[1:], x.dtype)], 0)
            outs.append(x)
        return np.stack(outs)

    def gs(name, layer=None):
        outs = []
        for c in range(8):
            x = R[c % ncores][name] if layer is None else R[c % ncores][name][layer]
            outs.append(x[PTOK:PTOK + 128].reshape((4, 32) + x.shape[1:]))
        return np.concatenate(outs, 0)

    y_p = gp("OUT_Y").astype(f32)
    y_s = gs("OUT_Y").astype(f32)
    outs = [y_p, y_s]
    for g in (gp, gs):
        n = 4 if g is gp else 32
        t = 4096 if g is gp else 32
        outs.append(np.stack([g("OUT_K", 0), g("OUT_K", 3)]).reshape(2, n, t, 16, 64))
        outs.append(np.stack([g("OUT_V", 0), g("OUT_V", 3)]).reshape(2, n, t, 16, 64))
        outs.append(np.stack([g("OUT_LF", 0), g("OUT_LF", 1)]).reshape(2, n, t, 16))
        outs.append(g("OUT_K", 1).reshape(1, n, t, 16, 64))
        outs.append(g("OUT_V", 1).reshape(1, n, t, 8, 128))
        outs.append(g("OUT_K", 2).reshape(1, n, t, 16, 64))
        outs.append(g("OUT_V", 2).reshape(1, n, t, 16, 64))
        outs.append(g("OUT_KI").reshape(1, n, t, 64))
    return tuple(np.ascontiguousarray(o, dtype=f32) for o in outs)


class Unit:
    __slots__ = ("qk", "exp", "pv", "fin", "fin_b")

    def __init__(self, qk, exp, pv, fin=None, fin_b=None):
        self.qk, self.exp, self.pv, self.fin, self.fin_b = qk, exp, pv, fin, fin_b


def run_units_gen(units, pS, ptr, defer=3, la=1):
    pend = []
    n = len(units)

    def after_pv(u):
        if u.fin is not None:
            u.fin()
        if u.fin_b is not None:
            pend.append([defer, u.fin_b])

    def tick():
        for p in pend:
            p[0] -= 1
        while pend and pend[0][0] <= 0:
            pend.pop(0)[1]()

    for idx in range(n + la):
        if idx < n:
            units[idx].qk(pS[idx % len(pS)])
        j = idx - la
        if j >= 0:
            units[j].pv(ptr[j % len(ptr)])
            after_pv(units[j])
        tick()
        if idx < n:
            units[idx].exp(pS[idx % len(pS)], ptr[idx % len(ptr)])
        yield 1
    for p in pend:
        p[1]()
    yield 1


def run_units(units, pS, ptr, defer=3, la=1):
    for _ in run_units_gen(units, pS, ptr, defer, la):
        pass


def interleave(ga, ta, gb, tb):
    pa = pb = 0.0
    b_alive = gb is not None
    for wa in ga:
        pa += wa
        while b_alive and pb / tb < pa / ta:
            try:
                pb += next(gb)
            except StopIteration:
                b_alive = False
    while b_alive:
        try:
            next(gb)
        except StopIteration:
            b_alive = False


def _phase2_fox(self, L, P):
    B, S, C = self.B, self.B.S, self.C
    j = L // 3
    NCUM, NCUMP = P["NCUM"], P["NCUMP"]
    onesb, negI, ones32, maskF = C["onesb"], C["negI"], C["ones32"], C["maskF"]

    make_norm = self.make_norm

    def _unused(ctx, width):
        oar = ring(B, ctx, "oa", [65, width], F32, 2)
        RZ = B.sb(ctx, "RZ", [65, width], F32)
        B.memset("pool", RZ[:], 0.0, w=[RZ.b])
        pZ = B.ps(ctx, "pZ", [128, 512], F32)
        otr = ring(B, ctx, "ot", [64, width], BF16, 2)
        cnt = [0]

        def norm(pO, n, store):
            k = cnt[0]
            cnt[0] += 1
            oa, ot = oar[k % 2], otr[k % 2]
            B.cp("act", oa[0:65, 0:n], pO[0:65, 0:n], r=[pO.b], w=[oa.b])
            B.act(RZ[64:65, 0:n], oa[64:65, 0:n], AF.Ln, r=[oa.b], w=[RZ.b])
            B.act(RZ[64:65, 0:n], RZ[64:65, 0:n], AF.Exp, r=[RZ.b], w=[RZ.b], scale=-1.0)
            B.mm(pZ[0:64, 0:n], ones32[0:65, 0:64], RZ[0:65, 0:n], True, True, r=[ones32.b, RZ.b], w=[pZ.b])
            B.tt("dve", ot[0:64, 0:n], oa[0:64, 0:n], pZ[0:64, 0:n], ALU.mult, r=[oa.b, pZ.b], w=[ot.b])
            store(ot)
        return norm

    with ExitStack() as ctx:
        QTz = ring(B, ctx, "QTz", [128, 2, PTOK], BF16, 2)
        KTz = ring(B, ctx, "KTz", [128, 2, PTOK], BF16, 2)
        for t_ in QTz + KTz:
            B.memset("pool", t_[:], 0.0, w=[t_.b])
            B.memset("pool", t_[64:70, 0, :], 1.0, w=[t_.b])
            B.memset("pool", t_[0:6, 1, :], 1.0, w=[t_.b])
        Vh = ring(B, ctx, "Vh", [128, NTP, 2, 128], BF16, 2)
        pS = ring(B, ctx, "pS", [128, 1024], F32, 2, psum=True)
        pO = ring(B, ctx, "pO", [128, 512], F32, 3, psum=True)
        ptr = ring(B, ctx, "pt", [128, 1024], BF16, 3)
        norm = self.make_norm_pair(ctx, 512)
        VSv = self.VS[0:PTOK, :].rearrange("(kt p) (h e) -> p kt h e", p=128, e=128)
        g = 0
        for hp in range(8):
            qz, kz, vh = QTz[hp % 2], KTz[hp % 2], Vh[hp % 2]
            for i in range(2):
                pr = slice(64 * i, 64 * i + 64)
                a0 = 64 if i == 0 else 0
                c0 = 0 if i == 0 else 64
                B.ld(qz[pr, i, :], self.QT[hp, pr, 0:PTOK], w=[qz.b])
                B.ld(qz[a0:a0 + 3, i, :], self.CQT[hp, c0:c0 + 3, 0:PTOK], w=[qz.b])
                B.ld(kz[pr, i, :], self.KT[hp, pr, 0:PTOK], w=[kz.b])
                B.ld(kz[a0 + 3:a0 + 6, i, :], self.CQT[hp, c0 + 3:c0 + 6, 0:PTOK], w=[kz.b])
            for k0 in range(0, NTP, 8):
                k1 = min(NTP, k0 + 8)
                B.ld(vh[:, k0:k1, :, :], VSv[:, k0:k1, 2 * hp:2 * hp + 2, :], w=[vh.b])
            units = []
            for m in range(NTP // 4):
                nk = 4 * m + 4
                qs = slice(m * 512, (m + 1) * 512)
                for i in range(2):
                    po = pO[g % 3]
                    g += 1
                    for k2 in range(0, nk, 2):
                        def qk(ps, k2=k2, i=i, qs=qs, m=m):
                            for d in range(2):
                                kt = k2 + d
                                diag = kt >= 4 * m
                                B.mm(ps[:, d * 512:(d + 1) * 512], kz[:, i, kt * 128:(kt + 1) * 128], qz[:, i, qs],
                                     True, not diag, r=[kz.b, qz.b], w=[ps.b])
                                if diag:
                                    B.mm(ps[:, d * 512:(d + 1) * 512], negI[:, :], maskF[:, kt - 4 * m, :], False, True,
                                         r=[negI.b, maskF.b], w=[ps.b])

                        def ex(ps, pt):
                            B.act(pt[:, :], ps[:, :], AF.Exp, r=[ps.b], w=[pt.b])

                        def pv(pt, k2=k2, i=i, po=po, nk=nk):
                            for d in range(2):
                                kt = k2 + d
                                B.mm(po[:, :], vh[:, kt, i, :], pt[:, d * 512:(d + 1) * 512], kt == 0, kt == nk - 1,
                                     r=[vh.b, pt.b], w=[po.b])
                        fa = fb = None
                        if k2 == nk - 2:
                            fa, fb = norm(po, i, 512, lambda ot, hp=hp, qs=qs: B.ld(
                                self.OT[hp * 128:(hp + 1) * 128, qs], ot[:, :], w=[], r=[ot.b]))
                        units.append(Unit(qk, ex, pv, fa, fb))
            run_units(units, pS, ptr, defer=2)
        S.end_phase()

    self.sample65("fox", j, P, make_norm)


def _sample65(self, mode, j, P, make_norm):
    B, S, C = self.B, self.B.S, self.C
    onesb, negI, aug3 = C["onesb"], C["negI"], C["aug3"]
    fox = mode == "fox"
    if fox:
        NCUMP = P["NCUMP"]
        kT_src = lambda s, hp: self.a_kT[j, s, hp]
        v_src = lambda s, kt: self.a_v[j, s, kt * 128:(kt + 1) * 128, :]
    else:
        kT_src = lambda s, hp: self.c_kT[s, hp]
        v_src = lambda s, kt: self.c_v[s, kt * 128:(kt + 1) * 128, :]
    with ExitStack() as ctx:
        stg = ring(B, ctx, "stg", [128, 2048], F32, 3)
        KC = ring(B, ctx, "KC", [128, PAST], BF16, 2)
        VC = ring(B, ctx, "VC", [128, NKP, 16, 65], BF16, 2)
        for vc in VC:
            B.memset("pool", vc[:, :, :, 64:65], 1.0, w=[vc.b])
        QS = B.sb(ctx, "QS", [128, 8, 128], BF16)
        KS = B.sb(ctx, "KS", [128, 8, 128], BF16)
        sc = slice(PTOK, PTOK + 128)
        B.ld(QS[:], self.QT[:, :, sc].rearrange("h p n -> p h n"), w=[QS.b])
        B.ld(KS[:], self.KT[:, :, sc].rearrange("h p n -> p h n"), w=[KS.b])
        VN = B.sb(ctx, "VN", [32, NSEQ, 16 * 65], BF16)
        B.ld(VN[:], self.VSS.rearrange("(s p) x -> p s x", p=32), w=[VN.b])
        if fox:
            CQS = B.sb(ctx, "CQS", [128, 8, 128], BF16)
            B.ld(CQS[:], self.CQT[:, :, sc].rearrange("h p n -> p h n"), w=[CQS.b])
            NCSs = B.sb(ctx, "NCSs", [32, NSEQ, 16], F32)
            B.ld(NCSs[:], self.NCS.rearrange("(s p) h -> p s h", p=32), w=[NCSs.b])
            maskS = C["maskS"]
        if not fox:
            NMTS = self.dsa_sample_select(ctx, P)
        pS = ring(B, ctx, "pSs", [128, 512], F32, 3, psum=True)
        pO = ring(B, ctx, "pOs", [128, 512], F32, 2, psum=True)
        ptr = ring(B, ctx, "pts", [128, 32 if fox else 512], BF16, 4)
        norm = make_norm(ctx, 32)
        OTS = B.sb(ctx, "OTS", [64, 16, 128], BF16)
        si = 0
        g = 0
        for s in range(NSEQ):
            vc = VC[s % 2]
            for kt in range(NKP):
                st_ = stg[si % 3]
                si += 1
                B.ld(st_[:, 0:1024], v_src(s, kt), w=[st_.b])
                B.cp("pool", vc[:, kt, :, 0:64], st_[:, 0:1024].rearrange("p (h e) -> p h e", e=64), r=[st_.b], w=[vc.b])
            qcs = slice(32 * s, 32 * s + 32)
            for hp in range(8):
                kc = KC[(s * 8 + hp) % 2]
                st_ = stg[si % 3]
                si += 1
                B.ld(st_[:, :], kT_src(s, hp), w=[st_.b])
                B.cp("dve", kc[:, :], st_[:, :], r=[st_.b], w=[kc.b])
                units = []
                for i in range(2):
                    h = 2 * hp + i
                    pr = slice(64 * i, 64 * i + 64)
                    po = pO[g % 2]
                    g += 1
                    if not fox:
                        def qk(ps, pr=pr, kc=kc, hp=hp, qcs=qcs):
                            for kt in range(NKP):
                                B.mm(ps[:, kt * 32:(kt + 1) * 32], kc[pr, kt * 128:(kt + 1) * 128], QS[pr, hp, qcs],
                                     True, False, r=[kc.b, QS.b], w=[ps.b])
                                B.mm(ps[:, kt * 32:(kt + 1) * 32], negI[:, :], NMTS[:, kt, qcs], False, True,
                                     r=[negI.b, NMTS.b], w=[ps.b])

                        def ex(ps, pt):
                            B.act(pt[:, 0:512], ps[:, 0:512], AF.Exp, r=[ps.b], w=[pt.b])

                        def pv(pt, h=h, po=po, vc=vc):
                            for kt in range(NKP):
                                B.mm(po[0:65, 0:32], vc[:, kt, h, :], pt[:, kt * 32:(kt + 1) * 32], kt == 0, False,
                                     r=[vc.b, pt.b], w=[po.b])
                        units.append(Unit(qk, ex, pv, None))
                    for kt in range(NKP + 1):
                        new = kt == NKP
                        if not fox and not new:
                            continue
                        if not new:
                            def qk(ps, kt=kt, pr=pr, kc=kc, hp=hp, qcs=qcs):
                                B.mm(ps[:, 0:32], kc[pr, kt * 128:(kt + 1) * 128], QS[pr, hp, qcs], True, False,
                                     r=[kc.b, QS.b], w=[ps.b])
                                if fox:
                                    B.mm(ps[:, 0:32], aug3[pr, 0:128], CQS[pr, hp, qcs], False, True,
                                         r=[aug3.b, CQS.b], w=[ps.b])
                                else:
                                    B.mm(ps[:, 0:32], negI[:, :], NMTS[:, kt, qcs], False, True,
                                         r=[negI.b, NMTS.b], w=[ps.b])

                            def ex(ps, pt, kt=kt, h=h, s=s):
                                if fox:
                                    B.act(pt[:, 0:32], ps[:, 0:32], AF.Exp, r=[ps.b, NCUMP.b], w=[pt.b],
                                          bias=NCUMP[:, s, kt, h:h + 1])
                                else:
                                    B.act(pt[:, 0:32], ps[:, 0:32], AF.Exp, r=[ps.b], w=[pt.b])

                            def pv(pt, kt=kt, h=h, po=po, vc=vc):
                                B.mm(po[0:65, 0:32], vc[:, kt, h, :], pt[:, 0:32], kt == 0, False, r=[vc.b, pt.b], w=[po.b])
                            fin = None
                        else:
                            def qk(ps, pr=pr, hp=hp, qcs=qcs):
                                B.mm(ps[0:32, 0:32], KS[pr, hp, qcs], QS[pr, hp, qcs], True, False, r=[KS.b, QS.b], w=[ps.b])
                                if fox:
                                    B.mm(ps[0:32, 0:32], aug3[pr, 0:32], CQS[pr, hp, qcs], False, False,
                                         r=[aug3.b, CQS.b], w=[ps.b])
                                    B.mm(ps[0:32, 0:32], negI[0:32, 0:32], maskS[0:32, 0:32], False, True,
                                         r=[negI.b, maskS.b], w=[ps.b])
                                else:
                                    B.mm(ps[0:32, 0:32], negI[0:32, 0:32], NMTS[0:32, NKP, qcs], False, True,
                                         r=[negI.b, NMTS.b], w=[ps.b])

                            def ex(ps, pt, h=h, s=s):
                                if fox:
                                    B.act(pt[0:32, 0:32], ps[0:32, 0:32], AF.Exp, r=[ps.b, NCSs.b], w=[pt.b],
                                          bias=NCSs[0:32, s, h:h + 1])
                                else:
                                    B.act(pt[0:32, 0:32], ps[0:32, 0:32], AF.Exp, r=[ps.b], w=[pt.b])

                            def pv(pt, h=h, po=po, s=s):
                                B.mm(po[0:65, 0:32], VN[0:32, s, h * 65:(h + 1) * 65], pt[0:32, 0:32], False, True,
                                     r=[VN.b, pt.b], w=[po.b])

                            def fin(po=po, h=h, qcs=qcs):
                                norm(po, 32, lambda ot: B.cp("pool", OTS[:, h, qcs], ot[:, 0:32], r=[ot.b], w=[OTS.b]))
                        units.append(Unit(qk, ex, pv, fin))
                run_units(units, pS, ptr, la=2)
        B.ld(self.OT[:, sc].rearrange("(h e) n -> e h n", e=64), OTS[:], w=[], r=[OTS.b])
        S.end_phase()


Prog.sample65 = _sample65


Prog.phase2_fox = _phase2_fox


def _phase2(self, L, P):
    kind = L % 3
    if kind == 0:
        self.phase2_fox(L, P)
    elif kind == 1:
        self.phase2_diff(L, P)
    else:
        self.phase2_dsa(L, P)


Prog.phase2 = _phase2


def _phase3(self, L, P):
    B, S, C = self.B, self.B.S, self.C
    kind = L % 3
    idb = C["identb"]
    last_layer = (L == 3)

    def rms(ht, g, out, junk, ss, lnv, rstd):
        B.act(junk[:], ht[:], AF.Square, r=[ht.b], w=[junk.b, ss.b], accum=ss[:])
        B.act(lnv[:], ss[:], AF.Ln, r=[ss.b], w=[lnv.b], scale=1.0 / D, bias=EPS)
        B.act(rstd[:], lnv[:], AF.Exp, r=[lnv.b], w=[rstd.b], scale=-0.5)
        B.stt(out[:], ht[:], rstd[:], g[:], ALU.mult, ALU.mult, r=[ht.b, rstd.b, g.b], w=[out.b])

    with ExitStack() as ctx:
        stg = ring(B, ctx, "stg", [128, 2048], F32, 3)
        self.stg_i = 0
        WO = B.sb(ctx, "WO", [128, 8, D], BF16)
        WOb = [Buf("WOc%d" % i) for i in range(8)]
        nch, kp = 8, 128
        OTv = self.OT.rearrange("(c p) n -> p c n", p=128)
        hr = ring(B, ctx, "ht", [128, D], F32, 2)
        otr = ring(B, ctx, "OTt", [kp, nch, 128], BF16, 2)
        B.ld(hr[0][:], (self.xin if L == 0 else self.HRES)[0:128, :], w=[hr[0].b])
        B.ld(otr[0][:], OTv[:, :, 0:128], w=[otr[0].b])
        self.load_w(stg, lambda i: WO[:, i, :], lambda i: self.wo[L, i * 128:(i + 1) * 128, :], 8, WO, bufs=WOb)
        g2 = B.sb(ctx, "g2", [128, D], F32)
        B.ld(g2[:], self.g_ffn[L].partition_broadcast(128), w=[g2.b])
        h2r = ring(B, ctx, "h2", [128, D], F32, 2)
        junk = B.sb(ctx, "junk", [128, D], BF16)
        ssr = ring(B, ctx, "ss", [128, 1], F32, 2)
        lnr = ring(B, ctx, "lnv", [128, 1], F32, 2)
        rsr = ring(B, ctx, "rstd", [128, 1], F32, 2)
        xnr = ring(B, ctx, "xn", [128, D], BF16, 2)
        xTr = ring(B, ctx, "xT", [128, 8, 128], BF16, 2)
        pY = ring(B, ctx, "pY", [128, 512], F32, 4, psum=True)
        pT = ring(B, ctx, "pT", [128, 8, 128], BF16, 2, psum=True)
        def oproj(t):
            rows = slice(t * 128, (t + 1) * 128)
            ht, ott = hr[t % 2], otr[t % 2]
            if t > 0:
                B.ld(ht[:], (self.xin if L == 0 else self.HRES)[rows, :], w=[ht.b])
                B.ld(ott[:], OTv[:, :, rows], w=[ott.b])
            for nb in range(2):
                py = pY[(2 * t + nb) % 4]
                for c in range(nch):
                    B.mm(py[:, :], ott[:, c, :], WO[:, c, nb * 512:(nb + 1) * 512], c == 0, c == nch - 1,
                         r=[ott.b, WOb[c]], w=[py.b])

        def post(t):
            rows = slice(t * 128, (t + 1) * 128)
            ht, h2 = hr[t % 2], h2r[t % 2]
            for nb in range(2):
                py = pY[(2 * t + nb) % 4]
                B.tt("dve", h2[:, nb * 512:(nb + 1) * 512], py[:, :], ht[:, nb * 512:(nb + 1) * 512], ALU.add,
                     r=[py.b, ht.b], w=[h2.b])
            B.ld(self.HRES[rows, :], h2[:], w=[], r=[h2.b])
            xn, xT = xnr[t % 2], xTr[t % 2]
            rms(h2, g2, xn, junk, ssr[t % 2], lnr[t % 2], rsr[t % 2])
            for c in range(8):
                B.tr(pT[t % 2][:, c, :], xn[:, c * 128:(c + 1) * 128], idb[:], r=[xn.b, idb.b], w=[pT[t % 2].b])
            B.cp("act", xT[:], pT[t % 2][:], r=[pT[t % 2].b], w=[xT.b])
            B.ld(self.XT2[t].rearrange("p (c n) -> p c n", c=8), xT[:], w=[], r=[xT.b])

        oproj(0)
        for t in range(NT):
            if t + 1 < NT:
                oproj(t + 1)
            post(t)
        S.end_phase()

    for p in range(2):
        with ExitStack() as ctx:
            stg = ring(B, ctx, "stg", [128, 2048], F32, 3)
            self.stg_i = 0
            WU = B.sb(ctx, "WU", [128, 8, 2048], BF16)
            WD = B.sb(ctx, "WD", [128, 16, D], BF16)
            WUb = [Buf("WUc%d" % i) for i in range(8)]
            WDb = [Buf("WDc%d" % i) for i in range(16)]
            xmr = ring(B, ctx, "xT2m", [128, 8, 512], BF16, 2)
            for i in range(macros()[0][1]):
                B.ld(xmr[0][:, :, i * 128:(i + 1) * 128], self.XT2[i].rearrange("p (c n) -> p c n", c=8), w=[xmr[0].b])
            self.load_w(stg, lambda i: WU[:, i, :],
                        lambda i: self.wup[L, i * 128:(i + 1) * 128, p * 2048:(p + 1) * 2048], 8, WU, bufs=WUb)
            self.load_w(stg, lambda i: WD[:, i, :],
                        lambda i: self.wdn[L, (p * 16 + i) * 128:(p * 16 + i + 1) * 128, :], 16, WD, bufs=WDb)
            h1r = ring(B, ctx, "h1T", [128, 16, 512], BF16, 2)
            h1b = [[Buf("h1b%d_%d" % (a_, q_)) for q_ in range(4)] for a_ in range(2)]
            rr = ring(B, ctx, "relu", [128, 512], F32, 2)
            hr = ring(B, ctx, "ht", [128, D], F32, 3)
            pU = ring(B, ctx, "pU", [128, 512], F32, 3, psum=True)
            pY = ring(B, ctx, "pY", [128, 512], F32, 4, psum=True)
            fin = last_layer and p == 1
            if fin:
                gf = B.sb(ctx, "gf", [128, D], F32)
                B.ld(gf[:], self.g_fin.partition_broadcast(128), w=[gf.b])
                junk = B.sb(ctx, "junk", [128, D], BF16)
                ssr = ring(B, ctx, "ss", [128, 1], F32, 2)
                lnr = ring(B, ctx, "lnv", [128, 1], F32, 2)
                rsr = ring(B, ctx, "rstd", [128, 1], F32, 2)
                yr = ring(B, ctx, "y32", [128, D], F32, 2)
            ui = 0
            hi = 0
            for mi, (t0, nt) in enumerate(macros()):
                ntok = nt * 128
                xm, h1 = xmr[mi % 2], h1r[mi % 2]
                for i in range(nt):
                    if mi > 0:
                        B.ld(xm[:, :, i * 128:(i + 1) * 128], self.XT2[t0 + i].rearrange("p (c n) -> p c n", c=8), w=[xm.b])
                for fc in range(16):
                    pu = pU[ui % 3]
                    rl = rr[ui % 2]
                    ui += 1
                    for c in range(8):
                        B.mm(pu[:, 0:ntok], WU[:, c, fc * 128:(fc + 1) * 128], xm[:, c, 0:ntok], c == 0, c == 7,
                             r=[WUb[c], xm.b], w=[pu.b])
                    B.act(rl[:, 0:ntok], pu[:, 0:ntok], AF.Relu, r=[pu.b], w=[rl.b])
                    B.tt("pool", h1[:, fc, 0:ntok], rl[:, 0:ntok], rl[:, 0:ntok], ALU.mult, r=[rl.b],
                         w=[h1b[mi % 2][fc // 4]])
                for i in range(nt):
                    t = t0 + i
                    rows = slice(t * 128, (t + 1) * 128)
                    ht = hr[hi % 3]
                    hi += 1
                    B.ld(ht[:], self.HRES[rows, :], w=[ht.b])
                    for nb in range(2):
                        py = pY[(2 * t + nb) % 4]
                        for fc in range(16):
                            B.mm(py[:, :], h1[:, fc, i * 128:(i + 1) * 128], WD[:, fc, nb * 512:(nb + 1) * 512],
                                 fc == 0, fc == 15, r=[h1b[mi % 2][fc // 4], WDb[fc]], w=[py.b])
                        B.tt("dve", ht[:, nb * 512:(nb + 1) * 512], py[:, :], ht[:, nb * 512:(nb + 1) * 512], ALU.add,
                             r=[py.b, ht.b], w=[ht.b])
                    if not fin:
                        B.ld(self.HRES[rows, :], ht[:], w=[], r=[ht.b])
                    else:
                        y = yr[t % 2]
                        rms(ht, gf, y, junk, ssr[t % 2], lnr[t % 2], rsr[t % 2])
                        B.ld(self.OUT_Y[rows, :], y[:], w=[], r=[y.b])
            S.end_phase()


Prog.phase3 = _phase3


def _phase2_diff(self, L, P):
    B, S, C = self.B, self.B.S, self.C
    onesb, negI, ones32, maskC = C["onesb"], C["negI"], C["ones32"], C["maskC"]
    lam_init = 0.8 - 0.6 * math.exp(-0.3 * L)

    def setup_common(ctx, width, pZring=None):
        lv = B.sb(ctx, "lv", [1, 256], F32)
        B.ld(lv[:], self.lamv.rearrange("a b -> (a b)").unsqueeze(0), w=[lv.b])
        lt = B.sb(ctx, "lt", [1, 128], F32)
        B.tt("dve", lt[:, 0:64], lv[:, 0:64], lv[:, 64:128], ALU.mult, r=[lv.b], w=[lt.b])
        B.tt("dve", lt[:, 64:128], lv[:, 128:192], lv[:, 192:256], ALU.mult, r=[lv.b], w=[lt.b])
        l2 = B.sb(ctx, "l2", [1, 2], F32)
        S.op("dve", lambda e: e.tensor_reduce(out=l2[:], in_=lt[:].rearrange("p (a b) -> p a b", a=2), axis=AX.X, op=ALU.add),
             reads=[lt.b], writes=[l2.b])
        B.act(l2[:], l2[:], AF.Exp, r=[l2.b], w=[l2.b])
        nlam = B.sb(ctx, "nlam", [1, 1], F32)
        B.tt("dve", nlam[:], l2[:, 1:2], l2[:, 0:1], ALU.subtract, r=[l2.b], w=[nlam.b])
        B.ts("dve", nlam[:], nlam[:], -lam_init, ALU.add, r=[nlam.b], w=[nlam.b])
        GSC = B.sb(ctx, "GSC", [128, 1], F32)
        B.ld(GSC[:], self.subg[:, :], w=[GSC.b])
        B.ts("dve", GSC[:], GSC[:], 1.0 - lam_init, ALU.mult, r=[GSC.b], w=[GSC.b])
        NL = B.sb(ctx, "NL128", [128, 1], F32)
        pZb_own = B.ps(ctx, "pZb", [128, 512], F32) if pZring is None else None
        nls = Buf("nls")
        B.ld(self.NLS[0:1, 0:1], nlam[0:1, 0:1], w=[nls], r=[nlam.b])
        B.ld(NL[:], self.NLS[0].partition_broadcast(128), w=[NL.b], r=[nls])
        rzr = ring(B, ctx, "rz", [128, width], F32, 2)
        t1r = ring(B, ctx, "t1", [128, width], F32, 2)
        t2 = B.sb(ctx, "t2", [128, width], F32)
        sqr = ring(B, ctx, "sq", [128, width], F32, 2)
        rs = B.sb(ctx, "rs", [128, width], F32)
        otr = ring(B, ctx, "otd", [128, width], BF16, 2)
        cnt = [0]

        def fin(i, po, pz, n, store):
            k = cnt[0]
            cnt[0] += 1
            rz = rzr[k % 2]
            pZb = pz if pZb_own is None else pZb_own
            t1, sq = t1r[(k // 2) % 2], sqr[(k // 2) % 2]

            def stage_a():
                S.op("dve", lambda e: e.reciprocal(out=rz[:, 0:n], in_=pz[:, 0:n]), reads=[pz.b], writes=[rz.b])
                if i == 0:
                    B.tt("dve", t1[:, 0:n], po[:, 0:n], rz[:, 0:n], ALU.mult, r=[po.b, rz.b], w=[t1.b])
                    return
                B.tt("dve", t2[:, 0:n], po[:, 0:n], rz[:, 0:n], ALU.mult, r=[po.b, rz.b], w=[t2.b])
                B.stt(t1[:, 0:n], t2[:, 0:n], NL[:, 0:1], t1[:, 0:n], ALU.mult, ALU.add, r=[t2.b, NL.b, t1.b], w=[t1.b])
                B.tt("pool", sq[:, 0:n], t1[:, 0:n], t1[:, 0:n], ALU.mult, r=[t1.b], w=[sq.b])

            def stage_b():
                B.mm(pZb[:, 0:n], ones32[:, :], sq[:, 0:n], True, True, r=[ones32.b, sq.b], w=[pZb.b])
                B.act(rs[:, 0:n], pZb[:, 0:n], AF.Ln, r=[pZb.b], w=[rs.b], scale=1.0 / 128, bias=SUBLN_EPS)
                B.act(rs[:, 0:n], rs[:, 0:n], AF.Exp, r=[rs.b], w=[rs.b], scale=-0.5)
                ot = otr[(k // 2) % 2]
                B.stt(ot[:, 0:n], t1[:, 0:n], GSC[:, 0:1], rs[:, 0:n], ALU.mult, ALU.mult, r=[t1.b, GSC.b, rs.b], w=[ot.b])
                store(ot)
            return stage_a, (stage_b if i == 1 else None)
        return fin

    with ExitStack() as ctx:
        QTh = ring(B, ctx, "QTz", [128, 2, PTOK], BF16, 2)
        for t_ in QTh:
            B.memset("pool", t_[:], 0.0, w=[t_.b])
        KTh = ring(B, ctx, "KTh", [128, PTOK], BF16, 2)
        Vd = ring(B, ctx, "Vd", [128, NTP, 128], BF16, 2)
        pS = ring(B, ctx, "pS", [128, 1024], F32, 2, psum=True)
        pO = ring(B, ctx, "pO", [128, 512], F32, 2, psum=True)
        pZ = ring(B, ctx, "pZ", [128, 512], F32, 2, psum=True)
        ptr = ring(B, ctx, "pt", [128, 1024], BF16, 3)
        fin_ = setup_common(ctx, 512, pZ)
        VSv = self.VS[0:PTOK, 0:D].rearrange("(kt p) (h e) -> p kt h e", p=128, e=128)
        g = 0
        for hd in range(8):
            qt, kt_, vd = QTh[hd % 2], KTh[hd % 2], Vd[hd % 2]
            B.ld(qt[0:64, 0, :], self.QT[hd, 0:64, 0:PTOK], w=[qt.b])
            B.ld(qt[64:128, 1, :], self.QT[hd, 64:128, 0:PTOK], w=[qt.b])
            B.ld(kt_[:], self.KT[hd, :, 0:PTOK], w=[kt_.b])
            for k0 in range(0, NTP, 8):
                k1 = min(NTP, k0 + 8)
                B.ld(vd[:, k0:k1, :], VSv[:, k0:k1, hd, :], w=[vd.b])
            units = []
            for m in range(NTP // 4):
                nk = 4 * m + 4
                qs = slice(m * 512, (m + 1) * 512)
                for i in range(2):
                    po, pz = pO[g % 2], pZ[g % 2]
                    g += 1
                    for k2 in range(0, nk, 2):
                        def qk(ps, k2=k2, i=i, qs=qs, m=m):
                            for d in range(2):
                                kt = k2 + d
                                diag = kt >= 4 * m
                                B.mm(ps[:, d * 512:(d + 1) * 512], kt_[:, kt * 128:(kt + 1) * 128], qt[:, i, qs],
                                     True, not diag, r=[kt_.b, qt.b], w=[ps.b])
                                if diag:
                                    B.mm(ps[:, d * 512:(d + 1) * 512], negI[:, :], maskC[:, kt - 4 * m, :], False, True,
                                         r=[negI.b, maskC.b], w=[ps.b])

                        def ex(ps, pt):
                            B.act(pt[:, :], ps[:, :], AF.Exp, r=[ps.b], w=[pt.b])

                        def pv(pt, k2=k2, po=po, pz=pz, nk=nk):
                            for d in range(2):
                                kt = k2 + d
                                B.mm(po[:, :], vd[:, kt, :], pt[:, d * 512:(d + 1) * 512], kt == 0, kt == nk - 1,
                                     r=[vd.b, pt.b], w=[po.b])
                                B.mm(pz[:, :], onesb[:, :], pt[:, d * 512:(d + 1) * 512], kt == 0, kt == nk - 1,
                                     r=[onesb.b, pt.b], w=[pz.b])
                        fa = fb = None
                        if k2 == nk - 2:
                            fa, fb = fin_(i, po, pz, 512, lambda ot, hd=hd, qs=qs: B.ld(
                                self.OT[hd * 128:(hd + 1) * 128, qs], ot[:, :], w=[], r=[ot.b]))
                        units.append(Unit(qk, ex, pv, fa, fb))
            run_units(units, pS, ptr, defer=4)
        S.end_phase()

    with ExitStack() as ctx:
        stg = ring(B, ctx, "stg", [128, 2048], F32, 3)
        KC = ring(B, ctx, "KC", [128, PAST], BF16, 2)
        VC = ring(B, ctx, "VC", [128, NKP, D], BF16, 2)
        QS = B.sb(ctx, "QS", [128, 8, 128], BF16)
        KS = B.sb(ctx, "KS", [128, 8, 128], BF16)
        sc = slice(PTOK, PTOK + 128)
        B.ld(QS[:], self.QT[:, :, sc].rearrange("h p n -> p h n"), w=[QS.b])
        B.ld(KS[:], self.KT[:, :, sc].rearrange("h p n -> p h n"), w=[KS.b])
        VN = B.sb(ctx, "VN", [32, NSEQ, D], BF16)
        B.ld(VN[:], self.VS[sc, 0:D].rearrange("(s p) x -> p s x", p=32), w=[VN.b])
        pS = ring(B, ctx, "pSs", [128, 512], F32, 3, psum=True)
        pO = ring(B, ctx, "pOs", [128, 512], F32, 2, psum=True)
        pZ = ring(B, ctx, "pZs", [128, 512], F32, 2, psum=True)
        ptr = ring(B, ctx, "pts", [128, 512], BF16, 4)
        fin_ = setup_common(ctx, 32)
        OTS = B.sb(ctx, "OTS", [128, 8, 128], BF16)
        si = 0
        g = 0
        for s in range(NSEQ):
            vc = VC[s % 2]
            for kt in range(NKP):
                st_ = stg[si % 3]
                si += 1
                B.ld(st_[:, 0:1024], self.b_v[s, kt * 128:(kt + 1) * 128, :], w=[st_.b])
                B.cp("pool", vc[:, kt, :], st_[:, 0:1024], r=[st_.b], w=[vc.b])
            qcs = slice(32 * s, 32 * s + 32)
            for hd in range(8):
                kc = KC[(s * 8 + hd) % 2]
                st_ = stg[si % 3]
                si += 1
                B.ld(st_[:, :], self.b_kT[s, hd], w=[st_.b])
                B.cp("dve", kc[:, :], st_[:, :], r=[st_.b], w=[kc.b])
                units = []
                for i in range(2):
                    pr = slice(64 * i, 64 * i + 64)
                    po, pz = pO[g % 2], pZ[g % 2]
                    g += 1
                    es = slice(hd * 128, (hd + 1) * 128)
                    def qk(ps, pr=pr, kc=kc, hd=hd, qcs=qcs):
                        for kt in range(NKP):
                            B.mm(ps[:, kt * 32:(kt + 1) * 32], kc[pr, kt * 128:(kt + 1) * 128], QS[pr, hd, qcs],
                                 True, True, r=[kc.b, QS.b], w=[ps.b])

                    def ex(ps, pt):
                        B.act(pt[:, 0:512], ps[:, 0:512], AF.Exp, r=[ps.b], w=[pt.b])

                    def pv(pt, po=po, pz=pz, vc=vc, es=es):
                        for kt in range(NKP):
                            B.mm(po[:, 0:32], vc[:, kt, es], pt[:, kt * 32:(kt + 1) * 32], kt == 0, False,
                                 r=[vc.b, pt.b], w=[po.b])
                            B.mm(pz[:, 0:32], onesb[:, :], pt[:, kt * 32:(kt + 1) * 32], kt == 0, False,
                                 r=[onesb.b, pt.b], w=[pz.b])
                    units.append(Unit(qk, ex, pv, None))
                    for kt in range(NKP + 1):
                        new = kt == NKP
                        if not new:
                            continue
                        if not new:
                            def qk(ps, kt=kt, pr=pr, kc=kc, hd=hd, qcs=qcs):
                                B.mm(ps[:, 0:32], kc[pr, kt * 128:(kt + 1) * 128], QS[pr, hd, qcs], True, True,
                                     r=[kc.b, QS.b], w=[ps.b])

                            def ex(ps, pt):
                                B.act(pt[:, 0:32], ps[:, 0:32], AF.Exp, r=[ps.b], w=[pt.b])

                            def pv(pt, kt=kt, po=po, pz=pz, vc=vc, es=es):
                                B.mm(po[:, 0:32], vc[:, kt, es], pt[:, 0:32], kt == 0, False, r=[vc.b, pt.b], w=[po.b])
                                B.mm(pz[:, 0:32], onesb[:, :], pt[:, 0:32], kt == 0, False, r=[onesb.b, pt.b], w=[pz.b])
                            fin = None
                        else:
                            def qk(ps, pr=pr, hd=hd, qcs=qcs):
                                B.mm(ps[0:32, 0:32], KS[pr, hd, qcs], QS[pr, hd, qcs], True, True, r=[KS.b, QS.b], w=[ps.b])

                            def ex(ps, pt):
                                B.act(pt[0:32, 0:32], ps[0:32, 0:32], AF.Exp, r=[ps.b], w=[pt.b])

                            def pv(pt, po=po, pz=pz, s=s, es=es):
                                B.mm(po[:, 0:32], VN[0:32, s, es], pt[0:32, 0:32], False, True, r=[VN.b, pt.b], w=[po.b])
                                B.mm(pz[:, 0:32], onesb[0:32, :], pt[0:32, 0:32], False, True, r=[onesb.b, pt.b], w=[pz.b])

                            def fin(i=i, po=po, pz=pz, hd=hd, qcs=qcs):
                                fa_, fb_ = fin_(i, po, pz, 32,
                                                lambda ot: B.cp("pool", OTS[:, hd, qcs], ot[:, 0:32], r=[ot.b], w=[OTS.b]))
                                fa_()
                                if fb_ is not None:
                                    fb_()
                        units.append(Unit(qk, ex, pv, fin))
                run_units(units, pS, ptr, la=2)
        B.ld(self.OT[:, sc].rearrange("(c p) n -> p c n", p=128), OTS[:], w=[], r=[OTS.b])
        S.end_phase()


Prog.phase2_diff = _phase2_diff


def _make_norm(self, ctx, width):
    B, C = self.B, self.C
    ones32 = C["ones32"]
    oar = ring(B, ctx, "oa", [65, width], F32, 2)
    RZ = B.sb(ctx, "RZ", [65, width], F32)
    B.memset("pool", RZ[:], 0.0, w=[RZ.b])
    pZ = B.ps(ctx, "pZ", [128, 512], F32)
    otr = ring(B, ctx, "ot", [64, width], BF16, 2)
    cnt = [0]

    def norm(pO, n, store):
        k = cnt[0]
        cnt[0] += 1
        oa, ot = oar[k % 2], otr[k % 2]
        B.cp("act", oa[0:65, 0:n], pO[0:65, 0:n], r=[pO.b], w=[oa.b])
        B.act(RZ[64:65, 0:n], oa[64:65, 0:n], AF.Ln, r=[oa.b], w=[RZ.b])
        B.act(RZ[64:65, 0:n], RZ[64:65, 0:n], AF.Exp, r=[RZ.b], w=[RZ.b], scale=-1.0)
        B.mm(pZ[0:64, 0:n], ones32[0:65, 0:64], RZ[0:65, 0:n], True, True, r=[ones32.b, RZ.b], w=[pZ.b])
        B.tt("dve", ot[0:64, 0:n], oa[0:64, 0:n], pZ[0:64, 0:n], ALU.mult, r=[oa.b, pZ.b], w=[ot.b])
        store(ot)
    return norm


Prog.make_norm = _make_norm


def _make_norm_pair(self, ctx, width):
    B, C = self.B, self.C
    ones32 = C["ones32"]
    oar = ring(B, ctx, "oap", [128, width], F32, 2)
    RZ = [B.sb(ctx, "RZp%d" % i, [65, width], F32) for i in range(2)]
    for rz in RZ:
        B.memset("pool", rz[:], 0.0, w=[rz.b])
    pZ = B.ps(ctx, "pZp", [128, 512], F32)
    otr = ring(B, ctx, "otp", [128, width], BF16, 2)
    cnt = [0]

    def norm(pO, i, n, store):
        k = cnt[0]
        cnt[0] += 1
        oa, ot, rz = oar[k % 2], otr[(k // 2) % 2], RZ[i]
        if i == 0:
            zr, orows = slice(64, 65), slice(0, 64)
        else:
            zr, orows = slice(0, 1), slice(64, 128)

        def stage_a():
            B.cp("dve", rz[zr, 0:n], pO[zr, 0:n], r=[pO.b], w=[rz.b])

        def stage_b():
            B.mm(pZ[orows, 0:n], ones32[0:65, 0:64], rz[0:65, 0:n], True, True, r=[ones32.b, rz.b], w=[pZ.b])
            B.S.op("dve", lambda e: e.reciprocal(out=oa[orows, 0:n], in_=pZ[orows, 0:n]), reads=[pZ.b], writes=[oa.b])
            B.tt("dve", ot[orows, 0:n], pO[orows, 0:n], oa[orows, 0:n], ALU.mult, r=[oa.b, pO.b], w=[ot.b])
            if i == 1:
                store(ot)
        return stage_a, stage_b
    return norm


Prog.make_norm_pair = _make_norm_pair


def _dsa_select(self, I, NM, Nk, wi_ap, small):
    B, S = self.B, self.B.S
    hi, mid, cnt, u = small
    S.op("dve", lambda e: e.tensor_reduce(out=hi[:], in_=I[:, 0:Nk], axis=AX.X, op=ALU.max), reads=[I.b], writes=[hi.b])
    B.ts("dve", mid[:], hi[:], -0.5 * BIS_RANGE, ALU.add, r=[hi.b], w=[mid.b])
    yield 1
    for n in range(1, NBIS + 1):
        B.ts("dve", NM[:, 0:Nk], I[:, 0:Nk], mid[:, 0:1], ALU.is_ge, r=[I.b, mid.b], w=[NM.b, cnt.b],
             s2=0.0, op1=ALU.add, accum=cnt[:])
        wn = BIS_RANGE / (2 ** n)
        if n < NBIS:
            wn1 = wn / 2
            B.ts("dve", u[:], cnt[:], 255.5, ALU.is_ge, r=[cnt.b], w=[u.b], s2=2 * wn1, op1=ALU.mult)
            B.stt(mid[:], u[:], -wn1, mid[:], ALU.add, ALU.add, r=[u.b, mid.b], w=[mid.b])
        else:
            B.ts("dve", u[:], cnt[:], 255.5, ALU.is_ge, r=[cnt.b], w=[u.b], s2=wn, op1=ALU.mult)
            B.stt(mid[:], u[:], -wn, mid[:], ALU.add, ALU.add, r=[u.b, mid.b], w=[mid.b])
        yield 1
    B.ts("dve", NM[:, 0:Nk], I[:, 0:Nk], mid[:, 0:1], ALU.is_lt, r=[I.b, mid.b], w=[NM.b])
    yield 1


Prog.dsa_select = _dsa_select


def _phase2_dsa(self, L, P):
    B, S, C = self.B, self.B.S, self.C
    WI = P["WI"]
    negI, idb, cmaskA = C["negI"], C["identb"], C["cmaskA"]
    make_norm = self.make_norm
    with ExitStack() as ctx:
        KIT = B.sb(ctx, "KIT", [128, PTOK], BF16)
        B.ld(KIT[:], self.KITs[:, 0:PTOK], w=[KIT.b])
        QIr = ring(B, ctx, "QIt", [128, 4, 128], BF16, 2)
        I = B.sb(ctx, "Isc", [128, PTOK], F32)
        NM = B.sb(ctx, "NM", [128, PTOK], BF16)
        Rr = ring(B, ctx, "Rr", [128, 512], F32, 3)
        NMT = ring(B, ctx, "NMT", [128, NTP, 512], BF16, 2)
        small = [B.sb(ctx, "bs%d" % k, [128, 1], F32) for k in range(4)]
        pI = ring(B, ctx, "pI", [128, 512], F32, 2, psum=True)
        pTn = B.ps(ctx, "pTn", [128, 4, 128], BF16)
        qtr = ring(B, ctx, "qt", [128, 2, 512], BF16, 2)
        for t_ in qtr:
            B.memset("pool", t_[:], 0.0, w=[t_.b])
        ktr = ring(B, ctx, "kt", [128, PTOK], BF16, 2)
        vhr = ring(B, ctx, "vh", [128, NTP, 2, 128], BF16, 2)
        pS = ring(B, ctx, "pS", [128, 512], F32, 2, psum=True)
        pO = ring(B, ctx, "pO", [128, 512], F32, 2, psum=True)
        ptr = ring(B, ctx, "pt", [128, 512], BF16, 3)
        norm = self.make_norm_pair(ctx, 512)
        VSv = self.VS[0:PTOK, :].rearrange("(kt p) (h e) -> p kt h e", p=128, e=128)
        cnts = {"ri": 0, "li": 0, "g": 0}

        def step_a(i):
            m = i // 4
            nmt = NMT[m % 2]
            Nk = 128 * (i + 1)
            qi = QIr[i % 2]
            B.ld(qi[:], self.QITs[:, :, i * 128:(i + 1) * 128].rearrange("h p n -> p h n"), w=[qi.b])
            for kb in range((Nk + 511) // 512):
                w = min(512, Nk - 512 * kb)
                ks = slice(512 * kb, 512 * kb + w)
                for h8 in range(8):
                    pr = slice(64 * (h8 % 2), 64 * (h8 % 2) + 64)
                    pi, R = pI[cnts["ri"] % 2], Rr[cnts["ri"] % 3]
                    cnts["ri"] += 1
                    B.mm(pi[:, 0:w], qi[pr, h8 // 2, :], KIT[pr, ks], True, True, r=[qi.b, KIT.b], w=[pi.b])
                    B.act(R[:, 0:w], pi[:, 0:w], AF.Relu, r=[pi.b], w=[R.b])
                    if h8 == 0:
                        B.ts("dve", I[:, ks], R[:, 0:w], WI[:, i, 0:1], ALU.mult, r=[R.b, WI.b], w=[I.b])
                    else:
                        B.stt(I[:, ks], R[:, 0:w], WI[:, i, h8:h8 + 1], I[:, ks], ALU.mult, ALU.add,
                              r=[R.b, WI.b, I.b], w=[I.b])
                    yield 0.3 + w * 0.0011
            ds_ = slice(128 * i, 128 * i + 128)
            B.tt("dve", I[:, ds_], I[:, ds_], cmaskA[:, :], ALU.add, r=[I.b, cmaskA.b], w=[I.b])
            for _ in self.dsa_select(I, NM, Nk, None, small):
                yield 0.4 + Nk * 0.0015
            cs = slice((i % 4) * 128, (i % 4) * 128 + 128)
            for k0 in range(0, i + 1, 4):
                n = min(4, i + 1 - k0)
                for jj in range(n):
                    B.tr(pTn[:, jj, :], NM[:, (k0 + jj) * 128:(k0 + jj + 1) * 128], idb[:], r=[NM.b, idb.b], w=[pTn.b])
                B.cp("act", nmt[:, k0:k0 + n, cs], pTn[:, 0:n, :], r=[pTn.b], w=[nmt.b])
            for kt in range(i + 1, 4 * m + 4):
                B.memset("pool", nmt[:, kt, cs], 1.0, w=[nmt.b])
            yield 1.0

        def t_a(i):
            Nk = 128 * (i + 1)
            t = 1.0
            for kb in range((Nk + 511) // 512):
                w = min(512, Nk - 512 * kb)
                t += 8 * (0.3 + w * 0.0011)
            return t + (NBIS + 2) * (0.4 + Nk * 0.0015)

        def step_b(m, hps):
            for c0 in range(0, len(hps), 2):
                for _ in step_b2(m, hps[c0:c0 + 2]):
                    pass

        def step_b2(m, hps):
            nmt = NMT[m % 2]
            nk = 4 * m + 4
            qs = slice(m * 512, (m + 1) * 512)
            units = []
            for hp in hps:
                li = cnts["li"]
                cnts["li"] += 1
                qt, kt_, vh = qtr[li % 2], ktr[li % 2], vhr[li % 2]
                B.ld(qt[0:64, 0, :], self.QT[hp, 0:64, qs], w=[qt.b])
                B.ld(qt[64:128, 1, :], self.QT[hp, 64:128, qs], w=[qt.b])
                B.ld(kt_[:, 0:nk * 128], self.KT[hp, :, 0:nk * 128], w=[kt_.b])
                for k0 in range(0, nk, 8):
                    k1 = min(nk, k0 + 8)
                    B.ld(vh[:, k0:k1, :, :], VSv[:, k0:k1, 2 * hp:2 * hp + 2, :], w=[vh.b])
                for i in range(2):
                    po = pO[cnts["g"] % 2]
                    cnts["g"] += 1
                    for kt in range(nk):
                        def qk(ps, kt=kt, i=i, qt=qt, kt_=kt_):
                            B.mm(ps[:, :], kt_[:, kt * 128:(kt + 1) * 128], qt[:, i, :], True, False, r=[kt_.b, qt.b], w=[ps.b])
                            B.mm(ps[:, :], negI[:, :], nmt[:, kt, :], False, True, r=[negI.b, nmt.b], w=[ps.b])

                        def ex(ps, pt):
                            B.act(pt[:, :], ps[:, :], AF.Exp, r=[ps.b], w=[pt.b])

                        def pv(pt, kt=kt, i=i, po=po, vh=vh, nk=nk):
                            B.mm(po[:, :], vh[:, kt, i, :], pt[:, :], kt == 0, kt == nk - 1, r=[vh.b, pt.b], w=[po.b])
                        fa = fb = None
                        if kt == nk - 1:
                            fa, fb = norm(po, i, 512, lambda ot, hp=hp, qs=qs: B.ld(
                                self.OT[hp * 128:(hp + 1) * 128, qs], ot[:, :], w=[], r=[ot.b]))
                        units.append(Unit(qk, ex, pv, fa, fb))
            for wv in run_units_gen(units, pS, ptr, defer=2):
                yield wv

        nm_ = NTP // 4
        for i in range(4):
            for _ in step_a(i):
                pass
        for m in range(1, nm_):
            for c in range(4):
                nb = 4 * (4 * (m - 1) + 4) + 1
                interleave(step_a(4 * m + c), t_a(4 * m + c), step_b2(m - 1, [2 * c, 2 * c + 1]), float(nb))
        step_b(nm_ - 1, list(range(8)))
        S.end_phase()
    self.sample65("dsa", 0, P, make_norm)


def _dsa_sample_select(self, ctx, P):
    B, S, C = self.B, self.B.S, self.C
    WI = P["WI"]
    idb, colm = C["identb"], C["colm"]
    NKS = PAST + 32
    sc = slice(PTOK, PTOK + 128)
    stg = ring(B, ctx, "stgk", [128, 2048], F32, 2)
    KIC = B.sb(ctx, "KIC", [128, NSEQ, PAST], BF16)
    for s in range(NSEQ):
        st_ = stg[s % 2]
        B.ld(st_[0:64, :], self.c_kiT[s], w=[st_.b])
        B.ld(st_[64:128, :], self.c_kiT[s], w=[st_.b])
        B.cp("pool", KIC[:, s, :], st_[:, :], r=[st_.b], w=[KIC.b])
    QIS = B.sb(ctx, "QIS", [128, 4, 128], BF16)
    B.ld(QIS[:], self.QITs[:, :, sc].rearrange("h p n -> p h n"), w=[QIS.b])
    QISm = B.sb(ctx, "QISm", [128, NSEQ, 4, 128], BF16)
    for s in range(NSEQ):
        B.tt("pool", QISm[:, s, :, :], QIS[:], colm[:, s, :].unsqueeze(1).to_broadcast([128, 4, 128]), ALU.mult,
             r=[QIS.b, colm.b], w=[QISm.b])
    KIN = B.sb(ctx, "KIN", [128, 128], BF16)
    B.ld(KIN[:], self.KITs[:, sc], w=[KIN.b])
    I = B.sb(ctx, "IscS", [128, NKS], F32)
    NM = B.sb(ctx, "NMs", [128, NKS], BF16)
    Rr = ring(B, ctx, "RrS", [128, 512], F32, 2)
    small = [B.sb(ctx, "bss%d" % k, [128, 1], F32) for k in range(4)]
    NMTS = B.sb(ctx, "NMTS", [128, NKP + 1, 128], BF16)
    with ExitStack() as pctx:
        pI = ring(B, pctx, "pIs", [128, 512], F32, 2, psum=True)
        pTn = B.ps(pctx, "pTns", [128, 4, 128], BF16)
        ri = 0
        for kb in range(5):
            w = 512 if kb < 4 else 32
            ks = slice(512 * kb, 512 * kb + w)
            for h8 in range(8):
                pr = slice(64 * (h8 % 2), 64 * (h8 % 2) + 64)
                pi, R = pI[ri % 2], Rr[ri % 2]
                ri += 1
                for s in range(NSEQ):
                    rhs = KIC[pr, s, ks] if kb < 4 else KIN[pr, 32 * s:32 * s + 32]
                    B.mm(pi[:, 0:w], QISm[pr, s, h8 // 2, :], rhs, s == 0, s == NSEQ - 1,
                         r=[QISm.b, KIC.b, KIN.b], w=[pi.b])
                B.act(R[:, 0:w], pi[:, 0:w], AF.Relu, r=[pi.b], w=[R.b])
                if h8 == 0:
                    B.ts("dve", I[:, ks], R[:, 0:w], WI[:, NTP, 0:1], ALU.mult, r=[R.b, WI.b], w=[I.b])
                else:
                    B.stt(I[:, ks], R[:, 0:w], WI[:, NTP, h8:h8 + 1], I[:, ks], ALU.mult, ALU.add,
                          r=[R.b, WI.b, I.b], w=[I.b])
        for _ in self.dsa_select(I, NM, NKS, None, small):
            pass
        for k0 in range(0, NKP + 1, 4):
            n = min(4, NKP + 1 - k0)
            for jj in range(n):
                kt = k0 + jj
                if kt < NKP:
                    B.tr(pTn[:, jj, :], NM[:, kt * 128:(kt + 1) * 128], idb[:], r=[NM.b, idb.b], w=[pTn.b])
                else:
                    B.tr(pTn[0:32, jj, :], NM[:, PAST:PAST + 32], idb[:], r=[NM.b, idb.b], w=[pTn.b])
            if k0 + n <= NKP:
                B.cp("act", NMTS[:, k0:k0 + n, :], pTn[:, 0:n, :], r=[pTn.b], w=[NMTS.b])
            else:
                if n > 1:
                    B.cp("act", NMTS[:, k0:k0 + n - 1, :], pTn[:, 0:n - 1, :], r=[pTn.b], w=[NMTS.b])
                B.cp("act", NMTS[0:32, NKP, :], pTn[0:32, n - 1, :], r=[pTn.b], w=[NMTS.b])
    return NMTS


Prog.phase2_dsa = _phase2_dsa
Prog.dsa_sample_select = _dsa_sample_select
```

```python
import os
import math
from contextlib import ExitStack

import numpy as np
import ml_dtypes
import concourse.bass as bass
import concourse.mybir as mybir
from concourse.bass_utils import run_bass_kernel_spmd

F32 = mybir.dt.float32
BF16 = mybir.dt.bfloat16
ALU = mybir.AluOpType
AF = mybir.ActivationFunctionType
AX = mybir.AxisListType

ENGS = ("pe", "dve", "act", "pool", "sp")

D = 1024
NH = 16
HD = 64
NTP = int(os.environ.get("MK_NTP", "32"))
NT = NTP + 1
TOK = NT * 128
PTOK = NTP * 128
NSEQ = 4
TS = 32
PAST = 2048
NKP = PAST // 128
DEPTH = int(os.environ.get("MK_DEPTH", "4"))
STOP = os.environ.get("MK_STOP", "")
NEG = -29952.0
EPS = 1e-6
SUBLN_EPS = 1e-5
NBIS = 16
BIS_RANGE = 16.0


class SemGroup:
    __slots__ = ("name", "dsem", "dcount", "uid")
    _n = 0

    def __init__(self, name):
        self.name = name
        self.dsem = None
        self.dcount = 0
        SemGroup._n += 1
        self.uid = SemGroup._n


class Buf:
    __slots__ = ("name", "last_w", "readers", "grp")

    def __init__(self, name, grp=None):
        self.name = name
        self.last_w = None
        self.readers = {}
        self.grp = grp if grp is not None else SemGroup(name)


class Sched:
    SEM_CAP = 24000

    def __init__(self, nc, stack, same_engine_sync=True):
        self.nc = nc
        self.stack = stack
        self.streams = {k: [] for k in ENGS}
        self.sem_pool = []
        self.n_sems = 0
        self.epoch = 0
        self.psem = {}
        self.pcount = {}
        for k in ENGS:
            self.psem[k], self.pcount[k] = self.get_sem()
        self.waited = {k: {} for k in ENGS}
        self.dma_sems = []
        self.same_engine_sync = same_engine_sync
        self.n_instr = 0
        self.n_wait = 0

    def get_sem(self):
        if self.sem_pool:
            return self.sem_pool.pop()
        self.n_sems += 1
        return (self.stack.enter_context(self.nc.semaphore("sm%d" % self.n_sems)), 0)

    def put_sem(self, h, v):
        if v < self.SEM_CAP:
            self.sem_pool.append((h, v))

    def _need(self, e, tok, waits):
        if tok is None:
            return
        if tok[0] == "c":
            _, pe_, idx, ep = tok
            if ep != self.epoch:
                return
            if pe_ == e and (e == "pe" or not self.same_engine_sync):
                return
            key = ("c", pe_)
            val = idx
            sem = self.psem[pe_]
        else:
            _, g, cnt = tok
            if g.dsem is None:
                return
            key = ("d", g.uid)
            val = g.dcount
            sem = g.dsem
        if self.waited[e].get(key, 0) >= val:
            return
        waits[key] = (sem, max(val, waits.get(key, (None, 0))[1]))

    def _deps(self, e, reads, writes):
        waits = {}
        for b in reads:
            self._need(e, b.last_w, waits)
        for b in writes:
            self._need(e, b.last_w, waits)
            for r in b.readers.values():
                self._need(e, r, waits)
        for key, (sem, val) in waits.items():
            self.waited[e][key] = val
        self.n_instr += 1
        self.n_wait += len(waits)
        return list(waits.values())

    def _mark(self, tok, reads, writes):
        for b in writes:
            b.last_w = tok
            b.readers = {}
        rk = (tok[0], tok[1]) if tok[0] == "c" else ("d", tok[1].uid)
        for b in reads:
            if b not in writes:
                b.readers[rk] = tok

    def op(self, e, fn, reads=(), writes=()):
        waits = self._deps(e, reads, writes)
        self.pcount[e] += 1
        tok = ("c", e, self.pcount[e], self.epoch)
        self.streams[e].append((waits, fn, self.psem[e], 1))
        self._mark(tok, reads, writes)

    def dma(self, q, fn, reads=(), writes=(), owner=None):
        waits = self._deps(q, reads, writes)
        if owner is None:
            owner = writes[0] if writes else reads[0]
        g = owner.grp
        if g.dsem is None:
            g.dsem, g.dcount = self.get_sem()
            self.dma_sems.append(g)
        g.dcount += 16
        tok = ("d", g, g.dcount)
        self.streams[q].append((waits, fn, g.dsem, 16))
        self._mark(tok, reads, writes)

    def barrier(self):
        for e in ENGS:
            waits = []
            for pe_ in ENGS:
                if pe_ == e and e in ("pe", "sp"):
                    continue
                v = self.pcount[pe_]
                if self.waited[e].get(("c", pe_), 0) < v:
                    waits.append((self.psem[pe_], v))
                    self.waited[e][("c", pe_)] = v
            for g in self.dma_sems:
                if g.dcount and self.waited[e].get(("d", g.uid), 0) < g.dcount:
                    waits.append((g.dsem, g.dcount))
                    self.waited[e][("d", g.uid)] = g.dcount
            if waits:
                self.streams[e].append((waits, None, None, 0))

    def end_phase(self):
        self.barrier()
        for g in self.dma_sems:
            self.put_sem(g.dsem, g.dcount)
            for e in ENGS:
                self.waited[e].pop(("d", g.uid), None)
            g.dsem = None
            g.dcount = 0
        self.dma_sems = []
        self.epoch += 1
        old = [(self.psem[k], self.pcount[k]) for k in ENGS]
        for k in ENGS:
            self.psem[k], self.pcount[k] = self.get_sem()
            for e in ENGS:
                self.waited[e][("c", k)] = self.pcount[k]
        for h, v in old:
            self.put_sem(h, v)

    def emit(self):
        nc = self.nc
        with nc.Block() as block:
            def mk(ekey):
                def body(engine):
                    for waits, fn, sem, inc in self.streams[ekey]:
                        for (s, v) in waits:
                            engine.wait_ge(s, v)
                        if fn is not None:
                            fn(engine).then_inc(sem, inc)
                return body
            block.tensor(mk("pe"))
            block.vector(mk("dve"))
            block.scalar(mk("act"))
            block.gpsimd(mk("pool"))
            block.sync(mk("sp"))


class TT:
    __slots__ = ("t", "b")

    def __init__(self, t, name, grp=None):
        self.t = t
        self.b = Buf(name, grp)

    def __getitem__(self, k):
        return self.t[k]


class Builder:
    def __init__(self):
        self.nc = bass.Bass("TRN2", target_bir_lowering=False)
        self.stack = ExitStack()
        self.S = None
        self.uid = 0

    def sb(self, ctx, name, shape, dt, grp=None):
        self.uid += 1
        nm = "%s_%d" % (name, self.uid)
        return TT(ctx.enter_context(self.nc.sbuf_tensor(nm, shape, dt)), nm, grp)

    def ps(self, ctx, name, shape, dt):
        self.uid += 1
        nm = "%s_%d" % (name, self.uid)
        return TT(ctx.enter_context(self.nc.psum_tensor(nm, shape, dt)), nm)

    def dram_in(self, name, shape, dt):
        return self.nc.dram_tensor(name, list(shape), dt, kind="ExternalInput").ap()

    def dram_out(self, name, shape, dt):
        return self.nc.dram_tensor(name, list(shape), dt, kind="ExternalOutput").ap()

    def dram_scr(self, name, shape, dt):
        return self.nc.dram_tensor(name, list(shape), dt, kind="Internal").ap()

    def ld(self, out, in_, w, r=(), q="sp", owner=None):
        self.S.dma(q, lambda e: e.dma_start(out=out, in_=in_), reads=list(r), writes=list(w), owner=owner)

    def mm(self, out, lhsT, rhs, start, stop, r, w):
        self.S.op("pe", lambda e: e.matmul(out, lhsT=lhsT, rhs=rhs, start=start, stop=stop), reads=r, writes=w)

    def tr(self, out, in_, ident, r, w):
        self.S.op("pe", lambda e: e.transpose(out=out, in_=in_, identity=ident), reads=r, writes=w)

    def act(self, out, in_, func, r, w, bias=None, scale=None, accum=None, eng="act"):
        kw = {}
        if bias is not None:
            kw["bias"] = bias
        if scale is not None:
            kw["scale"] = scale
        if accum is not None:
            kw["accum_out"] = accum
        self.S.op("act", lambda e: e.activation(out=out, in_=in_, func=func, **kw), reads=r, writes=w)

    def tt(self, eng, out, in0, in1, op, r, w):
        self.S.op(eng, lambda e: e.tensor_tensor(out=out, in0=in0, in1=in1, op=op), reads=r, writes=w)

    def ts(self, eng, out, in0, s1, op0, r, w, s2=None, op1=None, accum=None):
        kw = {}
        if op1 is not None:
            kw["op1"] = op1
        if accum is not None:
            kw["accum_out"] = accum
        self.S.op(eng, lambda e: e.tensor_scalar(out=out, in0=in0, scalar1=s1, scalar2=s2, op0=op0, **kw),
                  reads=r, writes=w)

    def stt(self, out, in0, scalar, in1, op0, op1, r, w):
        self.S.op("dve", lambda e: e.scalar_tensor_tensor(out=out, in0=in0, scalar=scalar, in1=in1, op0=op0, op1=op1),
                  reads=r, writes=w)

    def cp(self, eng, out, in_, r, w):
        if eng == "act":
            self.S.op("act", lambda e: e.copy(out=out, in_=in_), reads=r, writes=w)
        else:
            self.S.op(eng, lambda e: e.tensor_copy(out=out, in_=in_), reads=r, writes=w)

    def memset(self, eng, ap, val, w):
        self.S.op(eng, lambda e: e.memset(ap, val), reads=[], writes=w)


def ring(B, ctx, name, shape, dt, n, psum=False):
    return [(B.ps if psum else B.sb)(ctx, name, shape, dt) for _ in range(n)]


def macros():
    ms = [(t0, 4) for t0 in range(0, NTP, 4)]
    ms.append((NTP, 1))
    return ms


class Prog:
    def __init__(self):
        B = self.B = Builder()
        di, do, ds = B.dram_in, B.dram_out, B.dram_scr
        self.xin = di("xin", [TOK, D], F32)
        self.wqkv = di("wqkv", [4, D, 3 * D], F32)
        self.wf = di("wf", [2, D, 16], F32)
        self.bf = di("bf", [2, 16], F32)
        self.wqi = di("wqi", [D, 512], F32)
        self.wki = di("wki", [D, 64], F32)
        self.wwi = di("wwi", [D, 8], F32)
        self.wo = di("wo", [4, D, D], F32)
        self.wup = di("wup", [4, D, 4 * D], F32)
        self.wdn = di("wdn", [4, 4 * D, D], F32)
        self.g_mix = di("g_mix", [4, D], F32)
        self.g_ffn = di("g_ffn", [4, D], F32)
        self.g_fin = di("g_fin", [D], F32)
        self.lamv = di("lamv", [4, 64], F32)
        self.subg = di("subg", [128, 1], F32)
        self.a_kT = di("a_kT", [2, NSEQ, 8, 128, PAST], F32)
        self.a_v = di("a_v", [2, NSEQ, PAST, D], F32)
        self.a_lf = di("a_lf", [2, NSEQ, PAST, 16], F32)
        self.b_kT = di("b_kT", [NSEQ, 8, 128, PAST], F32)
        self.b_v = di("b_v", [NSEQ, PAST, D], F32)
        self.c_kT = di("c_kT", [NSEQ, 8, 128, PAST], F32)
        self.c_v = di("c_v", [NSEQ, PAST, D], F32)
        self.c_kiT = di("c_kiT", [NSEQ, 64, PAST], F32)
        self.c_identb = di("c_identb", [128, 128], BF16)
        self.c_onesb = di("c_onesb", [128, 128], BF16)
        self.c_negI = di("c_negI", [128, 128], BF16)
        self.c_ones32 = di("c_ones32", [128, 128], F32)
        self.c_tri = di("c_tri", [128, 128], F32)
        self.c_triS = di("c_triS", [128, 128], F32)
        self.c_blk = di("c_blk", [128, 4, 128], F32)
        self.c_rope = di("c_rope", [TOK, 128], F32)
        self.c_maskF = di("c_maskF", [128, 4, 512], BF16)
        self.c_maskC = di("c_maskC", [128, 4, 512], BF16)
        self.c_maskS = di("c_maskS", [32, 32], BF16)
        self.c_cmaskA = di("c_cmaskA", [128, 128], F32)
        self.c_colm = di("c_colm", [128, 4, 128], BF16)
        self.c_aug3 = di("c_aug3", [128, 128], BF16)
        self.OUT_K = do("OUT_K", [4, TOK, D], F32)
        self.OUT_V = do("OUT_V", [4, TOK, D], F32)
        self.OUT_LF = do("OUT_LF", [2, TOK, 16], F32)
        self.OUT_KI = do("OUT_KI", [TOK, 64], F32)
        self.OUT_Y = do("OUT_Y", [TOK, D], F32)
        self.HRES = ds("HRES", [TOK, D], F32)
        self.QT = ds("QT", [8, 128, TOK], BF16)
        self.KT = ds("KT", [8, 128, TOK], BF16)
        self.CQT = ds("CQT", [8, 128, TOK], BF16)
        self.VS = ds("VS", [TOK, 16 * 128], BF16)
        self.VSS = ds("VSS", [128, 16 * 65], BF16)
        self.NLS = ds("NLS", [1, 1], F32)
        self.OT = ds("OT", [D, TOK], BF16)
        self.XT2 = ds("XT2", [NT, 128, D], BF16)
        self.NCS = ds("NCS", [128, 16], F32)
        self.QITs = ds("QITs", [4, 128, TOK], BF16)
        self.KITs = ds("KITs", [128, TOK], BF16)

    def load_w(self, stg, dst_ap_fn, src_ap_fn, nchunks, W, eng="pool", bufs=None):
        B = self.B
        for i in range(nchunks):
            s = stg[self.stg_i % len(stg)]
            self.stg_i += 1
            src = src_ap_fn(i)
            shp = list(src.shape)
            n = 1
            for d_ in shp[1:]:
                n *= d_
            sview = s[:, 0:n]
            if len(shp) == 3:
                sview = sview.rearrange("p (a b) -> p a b", a=shp[1])
            B.ld(sview, src, w=[s.b])
            ce = ("act", "dve", "act", eng)[self.stg_i % 4]
            B.cp(ce, dst_ap_fn(i), sview, r=[s.b], w=[bufs[i] if bufs else W.b])

    def load_consts(self, ctx):
        B = self.B
        C = {}
        for nm, shp, dt in [("identb", [128, 128], BF16), ("onesb", [128, 128], BF16), ("negI", [128, 128], BF16),
                            ("ones32", [128, 128], F32), ("tri", [128, 128], F32), ("triS", [128, 128], F32),
                            ("blk", [128, 4, 128], F32), ("maskF", [128, 4, 512], BF16),
                            ("maskC", [128, 4, 512], BF16), ("cmaskA", [128, 128], F32),
                            ("colm", [128, 4, 128], BF16), ("aug3", [128, 128], BF16)]:
            t = B.sb(ctx, "c_" + nm, shp, dt)
            src = getattr(self, "c_" + nm)
            B.ld(t[:], src[tuple(slice(None) for _ in shp)], w=[t.b])
            C[nm] = t
        t = B.sb(ctx, "c_maskS", [32, 32], BF16)
        B.ld(t[:], self.c_maskS[:, :], w=[t.b])
        C["maskS"] = t
        self.C = C

    def rope(self, eng, x32, o_ap4, tabs, nh, scaled, tmp, r, w, split=None):
        B = self.B
        xv = x32[:, 0:nh * 64].rearrange("p (h t d) -> p h t d", h=nh, t=2)
        x1, x2 = xv[:, :, 0, :], xv[:, :, 1, :]
        ci, si = (2, 3) if scaled else (0, 1)
        cosb = tabs[:, ci, :].unsqueeze(1).to_broadcast([128, nh, 32])
        sinb = tabs[:, si, :].unsqueeze(1).to_broadcast([128, nh, 32])
        ta, tb = tmp
        tav = ta[:, 0:nh * 32].rearrange("p (h d) -> p h d", h=nh)
        tbv = tb[:, 0:nh * 32].rearrange("p (h d) -> p h d", h=nh)
        rr = list(r) + [x32.b, tabs.b]
        e2 = split if split is not None else eng
        B.tt(eng, tav, x1, cosb, ALU.mult, r=rr, w=[ta.b])
        B.tt(e2, tbv, x2, sinb, ALU.mult, r=rr, w=[tb.b])
        B.tt(eng, o_ap4[:, :, 0, :], tav, tbv, ALU.subtract, r=[ta.b, tb.b], w=w)
        B.tt(eng, tav, x2, cosb, ALU.mult, r=rr, w=[ta.b])
        B.tt(e2, tbv, x1, sinb, ALU.mult, r=rr, w=[tb.b])
        B.tt(eng, o_ap4[:, :, 1, :], tav, tbv, ALU.add, r=[ta.b, tb.b], w=w)

    def phase1(self, L, P):
        B, S, C = self.B, self.B.S, self.C
        kind, j = L % 3, L // 3
        extra = {0: 16, 1: 0, 2: 584}[kind]
        ncols = 3072 + extra
        ncb = (ncols + 511) // 512
        with ExitStack() as ctx:
            W = B.sb(ctx, "W", [128, 8, 3072 + 584], BF16)
            stg = ring(B, ctx, "stg", [128, 2048], F32, 3 if kind != 2 else 2)
            self.stg_i = 0
            g1 = B.sb(ctx, "g1", [128, D], F32)
            B.ld(g1[:], self.g_mix[L].partition_broadcast(128), w=[g1.b])
            hr = ring(B, ctx, "ht", [128, D], F32, 2)
            if kind != 0:
                tabr = ring(B, ctx, "tab", [128, 4, 32], F32, 2)
            B.ld(hr[0][:], (self.xin if L == 0 else self.HRES)[0:128, :], w=[hr[0].b])
            Wb = [Buf("Wc%d" % i) for i in range(16)]
            self.load_w(stg, lambda i: W[:, i // 2, (i % 2) * 1536:(i % 2 + 1) * 1536],
                        lambda i: self.wqkv[L, (i // 2) * 128:(i // 2 + 1) * 128, (i % 2) * 1536:(i % 2 + 1) * 1536],
                        16, W, bufs=Wb)
            if kind == 0:
                self.load_w(stg, lambda i: W[:, :, 3072:3088],
                            lambda i: self.wf[j].rearrange("(c p) n -> p c n", p=128), 1, W)
            if kind == 2:
                self.load_w(stg, lambda i: W[:, 4 * i:4 * i + 4, 3072:3584],
                            lambda i: self.wqi[i * 512:(i + 1) * 512, :].rearrange("(c p) n -> p c n", p=128), 2, W)
                self.load_w(stg, lambda i: W[:, :, 3584:3648],
                            lambda i: self.wki.rearrange("(c p) n -> p c n", p=128), 1, W)
                self.load_w(stg, lambda i: W[:, :, 3648:3656],
                            lambda i: self.wwi.rearrange("(c p) n -> p c n", p=128), 1, W)
            junk = B.sb(ctx, "junk", [128, D], BF16)
            ssr = ring(B, ctx, "ss", [128, 1], F32, 2)
            lnr = ring(B, ctx, "lnv", [128, 1], F32, 2)
            rsr = ring(B, ctx, "rstd", [128, 1], F32, 2)
            xnr = ring(B, ctx, "xn", [128, D], BF16, 2)
            xnTr = ring(B, ctx, "xnT", [128, 8, 128], BF16, 2)
            pT1 = B.ps(ctx, "pT1", [128, 8, 128], BF16)
            pA = ring(B, ctx, "pA", [128, 512], F32, 3, psum=True)
            pT2 = ring(B, ctx, "pT2", [128, 8, 128], BF16, 2, psum=True)
            pC = B.ps(ctx, "pC", [128, 512], F32)
            q32r = ring(B, ctx, "q32", [128, D], F32, 1) if kind != 0 else [None]
            k32r = ring(B, ctx, "k32", [128, D], F32, 1)
            v32r = ring(B, ctx, "v32", [128, D], F32, 1)
            qbr = ring(B, ctx, "qb", [128, D], BF16, 2)
            kbr = ring(B, ctx, "kb", [128, D], BF16, 2)
            vw = 16 * 128 if kind != 1 else D
            vbr = ring(B, ctx, "vb", [128, vw], BF16, 2)
            if kind != 1:
                for vb in vbr:
                    v4 = vb[:].rearrange("p (hp two e) -> p hp two e", two=2, e=128)
                    B.memset("pool", vb[:], 0.0, w=[vb.b])
                    B.memset("pool", v4[:, :, 0, 64:65], 1.0, w=[vb.b])
                    B.memset("pool", v4[:, :, 1, 0:1], 1.0, w=[vb.b])
                vbs = B.sb(ctx, "vbs", [128, 16 * 65], BF16)
                B.memset("pool", vbs[:].rearrange("p (h e) -> p h e", e=65)[:, :, 64:65], 1.0, w=[vbs.b])
            qTm = ring(B, ctx, "qTm", [128, 8, 128], BF16, 2)
            kTm = ring(B, ctx, "kTm", [128, 8, 128], BF16, 2)
            if kind != 0:
                k32o = ring(B, ctx, "k32o", [128, D], F32, 1)
                tmpA = [B.sb(ctx, "rtA", [128, 512], F32) for _ in range(2)]
                tmpB = [B.sb(ctx, "rtB", [128, 512], F32) for _ in range(2)]
            if kind == 0:
                cqTm = ring(B, ctx, "cqTm", [128, 8, 128], BF16, 2)
                bfb = B.sb(ctx, "bfb", [128, 16], F32)
                B.ld(bfb[:], self.bf[j].partition_broadcast(128), w=[bfb.b])
                xfr = ring(B, ctx, "xf", [128, 16], F32, 2)
                lfr = ring(B, ctx, "lf32", [128, 16], F32, 2)
                r1r = ring(B, ctx, "r1t", [128, 16], F32, 2)
                ACC = B.sb(ctx, "ACC", [128, 16], F32)
                B.memset("pool", ACC[:], 0.0, w=[ACC.b])
                CQ = B.sb(ctx, "CQ", [128, 16, 64], BF16)
                B.memset("pool", CQ[:], 0.0, w=[CQ.b])
                lfc = B.sb(ctx, "lfc", [128, NSEQ, NKP, 16], F32)
                for s in range(NSEQ):
                    B.ld(lfc[:, s, :, :], self.a_lf[j, s].rearrange("(kt p) h -> p kt h", p=128), w=[lfc.b])
                ACCP = B.sb(ctx, "ACCP", [128, NSEQ, 16], F32)
                B.memset("pool", ACCP[:], 0.0, w=[ACCP.b])
                NCUM, NCUMP = P["NCUM"], P["NCUMP"]
                for s in range(NSEQ):
                    for kt in range(NKP):
                        B.mm(pC[:, 0:16], C["tri"][:], lfc[:, s, kt, :], True, False, r=[C["tri"].b, lfc.b], w=[pC.b])
                        B.mm(pC[:, 0:16], C["ones32"][:], ACCP[:, s, :], False, True, r=[C["ones32"].b, ACCP.b], w=[pC.b])
                        B.ts("dve", NCUMP[:, s, kt, :], pC[:, 0:16], -1.0, ALU.mult, r=[pC.b], w=[NCUMP.b])
                        B.tt("pool", ACCP[:, s, :], ACCP[:, s, :], lfc[:, s, kt, :], ALU.add, r=[lfc.b, ACCP.b], w=[ACCP.b])
            if kind == 2:
                qi32r = ring(B, ctx, "qi32", [128, 512], F32, 2)
                qibr = ring(B, ctx, "qib", [128, 512], BF16, 2)
                kr32 = ring(B, ctx, "kr32", [128, 72], F32, 2)
                ki32r = ring(B, ctx, "ki32", [128, 64], F32, 2)
                kib2r = ring(B, ctx, "kib2", [128, 128], BF16, 2)
                pT3 = B.ps(ctx, "pT3", [128, 8, 128], BF16)
                qiTr = ring(B, ctx, "qiT", [128, 5, 128], BF16, 2)
                WI = P["WI"]
            elif kind == 0:
                pT3 = B.ps(ctx, "pT3", [128, 8, 128], BF16)
            idb = C["identb"]

            def prologue(t):
                rows = slice(t * 128, (t + 1) * 128)
                ht = hr[t % 2]
                if t > 0:
                    B.ld(ht[:], (self.xin if L == 0 else self.HRES)[rows, :], w=[ht.b])
                ss, lnv, rstd = ssr[t % 2], lnr[t % 2], rsr[t % 2]
                xn, xnT = xnr[t % 2], xnTr[t % 2]
                B.act(junk[:], ht[:], AF.Square, r=[ht.b], w=[junk.b, ss.b], accum=ss[:])
                B.act(lnv[:], ss[:], AF.Ln, r=[ss.b], w=[lnv.b], scale=1.0 / D, bias=EPS)
                B.act(rstd[:], lnv[:], AF.Exp, r=[lnv.b], w=[rstd.b], scale=-0.5)
                B.stt(xn[:], ht[:], rstd[:], g1[:], ALU.mult, ALU.mult, r=[ht.b, rstd.b, g1.b], w=[xn.b])
                for c in range(8):
                    B.tr(pT1[:, c, :], xn[:, c * 128:(c + 1) * 128], idb[:], r=[xn.b, idb.b], w=[pT1.b])
                B.cp("act", xnT[:], pT1[:], r=[pT1.b], w=[xnT.b])
                if kind != 0:
                    tabs = tabr[t % 2]
                    B.ld(tabs[:].rearrange("p a b -> p (a b)"), self.c_rope[rows, :], w=[tabs.b])

            prologue(0)
            for t in range(NT):
                rows = slice(t * 128, (t + 1) * 128)
                m, pos = t // 4, t % 4
                xn, xnT = xnr[t % 2], xnTr[t % 2]
                q32, k32, v32 = q32r[0], k32r[0], v32r[0]
                qb, kb, vb = qbr[t % 2], kbr[t % 2], vbr[t % 2]
                if kind != 0:
                    tabs = tabr[t % 2]
                for cb in range(ncb):
                    wcb = min(512, ncols - cb * 512)
                    pa = pA[cb % 3]
                    for c in range(8):
                        wbuf = Wb[2 * c + (1 if cb >= 3 else 0)] if cb < 6 else W.b
                        B.mm(pa[:, 0:wcb], xnT[:, c, :], W[:, c, cb * 512:cb * 512 + wcb], c == 0, c == 7,
                             r=[xnT.b, wbuf], w=[pa.b])
                    if cb == 2 and t + 1 < NT:
                        prologue(t + 1)
                    cs = slice((cb % 2) * 512, (cb % 2 + 1) * 512)
                    if cb < 2:
                        if kind == 0:
                            B.act(qb[:, cs], pa[:], AF.Copy, r=[pa.b], w=[qb.b], scale=0.125)
                        else:
                            B.cp("act", q32[:, cs], pa[:], r=[pa.b], w=[q32.b])
                    elif cb < 4:
                        B.cp("act", k32[:, cs], pa[:], r=[pa.b], w=[k32.b])
                    elif cb < 6:
                        B.cp("dve", v32[:, cs], pa[:], r=[pa.b], w=[v32.b])
                    elif kind == 0:
                        xf, lf32, r1t = xfr[t % 2], lfr[t % 2], r1r[t % 2]
                        B.tt("dve", xf[:], pa[:, 0:16], bfb[:], ALU.add, r=[pa.b, bfb.b], w=[xf.b])
                        B.act(xf[:], xf[:], AF.Exp, r=[xf.b], w=[xf.b], scale=-1.0)
                        B.act(xf[:], xf[:], AF.Ln, r=[xf.b], w=[xf.b], bias=1.0)
                        B.ts("dve", lf32[:], xf[:], -1.0, ALU.mult, r=[xf.b], w=[lf32.b])
                        B.ld(self.OUT_LF[j, rows, :], lf32[:], w=[], r=[lf32.b])
                    elif cb == 6:
                        qi32 = qi32r[t % 2]
                        B.cp("act", qi32[:], pa[:], r=[pa.b], w=[qi32.b])
                    else:
                        kr = kr32[t % 2]
                        B.cp("dve", kr[:], pa[:, 0:72], r=[pa.b], w=[kr.b])
                if kind == 0:
                    B.ld(self.OUT_K[L, rows, :], k32[:], w=[], r=[k32.b])
                    B.cp("pool", kb[:], k32[:], r=[k32.b], w=[kb.b])
                    ksrc = k32
                else:
                    ko = k32o[0]
                    self.rope("dve", q32, qb[:].rearrange("p (h t d) -> p h t d", h=16, t=2), tabs, 16, True,
                              (tmpA[0], tmpB[0]), r=[], w=[qb.b])
                    self.rope("pool", k32, ko[:].rearrange("p (h t d) -> p h t d", h=16, t=2), tabs, 16, False,
                              (tmpA[1], tmpB[1]), r=[], w=[ko.b])
                    B.ld(self.OUT_K[L, rows, :], ko[:], w=[], r=[ko.b])
                    B.cp("pool", kb[:], ko[:], r=[ko.b], w=[kb.b])
                B.ld(self.OUT_V[L, rows, :], v32[:], w=[], r=[v32.b])
                if kind != 1:
                    v4 = vb[:].rearrange("p (hp two e) -> p hp two e", two=2, e=128)
                    s4 = v32[:].rearrange("p (hp two e) -> p hp two e", two=2, e=64)
                    B.cp("pool", v4[:, :, 0, 0:64], s4[:, :, 0, :], r=[v32.b], w=[vb.b])
                    B.cp("pool", v4[:, :, 1, 64:128], s4[:, :, 1, :], r=[v32.b], w=[vb.b])
                    if t == NTP:
                        B.cp("pool", vbs[:].rearrange("p (h e) -> p h e", e=65)[:, :, 0:64],
                             v32[:].rearrange("p (h e) -> p h e", e=64), r=[v32.b], w=[vbs.b])
                        B.ld(self.VSS[:, :], vbs[:], w=[], r=[vbs.b])
                else:
                    B.cp("pool", vb[:], v32[:], r=[v32.b], w=[vb.b])
                B.ld(self.VS[rows, 0:vw], vb[:], w=[], r=[vb.b])
                qT, kT = qTm[t % 2], kTm[t % 2]
                for hp in range(8):
                    B.tr(pT2[0][:, hp, :], qb[:, hp * 128:(hp + 1) * 128], idb[:], r=[qb.b, idb.b], w=[pT2[0].b])
                B.cp("dve", qT[:], pT2[0][:], r=[pT2[0].b], w=[qT.b])
                for hp in range(8):
                    B.tr(pT2[1][:, hp, :], kb[:, hp * 128:(hp + 1) * 128], idb[:], r=[kb.b, idb.b], w=[pT2[1].b])
                B.cp("dve", kT[:], pT2[1][:], r=[pT2[1].b], w=[kT.b])
                if kind == 0:
                    lf32, r1t = lfr[t % 2], r1r[t % 2]
                    if t < NTP:
                        B.mm(pC[:, 0:16], C["tri"][:], lf32[:], True, False, r=[C["tri"].b, lf32.b], w=[pC.b])
                        B.mm(pC[:, 0:16], C["ones32"][:], ACC[:], False, True, r=[C["ones32"].b, ACC.b], w=[pC.b])
                        B.tt("pool", ACC[:], ACC[:], lf32[:], ALU.add, r=[lf32.b, ACC.b], w=[ACC.b])
                    else:
                        B.mm(pC[:, 0:16], C["triS"][:], lf32[:], True, False, r=[C["triS"].b, lf32.b], w=[pC.b])
                        for s in range(NSEQ):
                            B.mm(pC[:, 0:16], C["blk"][:, s, :], ACCP[:, s, :], False, s == NSEQ - 1,
                                 r=[C["blk"].b, ACCP.b], w=[pC.b])
                    B.ts("dve", NCUM[:, t, :], pC[:, 0:16], -1.0, ALU.mult, r=[pC.b], w=[NCUM.b])
                    if t == NTP:
                        B.ld(self.NCS[:, :], NCUM[:, t, :], w=[], r=[NCUM.b])
                    B.cp("dve", CQ[:, :, 0], pC[:, 0:16], r=[pC.b], w=[CQ.b])
                    B.tt("dve", r1t[:], pC[:, 0:16], CQ[:, :, 0], ALU.subtract, r=[pC.b, CQ.b], w=[r1t.b])
                    B.cp("dve", CQ[:, :, 1], r1t[:], r=[r1t.b], w=[CQ.b])
                    B.tt("dve", r1t[:], r1t[:], CQ[:, :, 1], ALU.subtract, r=[CQ.b, r1t.b], w=[r1t.b])
                    B.cp("dve", CQ[:, :, 2], r1t[:], r=[r1t.b], w=[CQ.b])
                    B.ts("dve", CQ[:, :, 3:6], CQ[:, :, 0:3], -1.0, ALU.mult, r=[CQ.b], w=[CQ.b])
                    cqT = cqTm[t % 2]
                    for hp in range(8):
                        B.tr(pT3[:, hp, :], CQ[:, 2 * hp:2 * hp + 2, :].rearrange("p a b -> p (a b)"), idb[:],
                             r=[CQ.b, idb.b], w=[pT3.b])
                    B.cp("act", cqT[:], pT3[:], r=[pT3.b], w=[cqT.b])
                if kind == 2:
                    qi32, qib, kr, ki32, kib2 = qi32r[t % 2], qibr[t % 2], kr32[t % 2], ki32r[t % 2], kib2r[t % 2]
                    self.rope("pool", qi32, qib[:].rearrange("p (h t d) -> p h t d", h=8, t=2), tabs, 8, False,
                              (tmpA[0], tmpB[0]), r=[], w=[qib.b])
                    self.rope("dve", kr, ki32[:].rearrange("p (h t d) -> p h t d", h=1, t=2), tabs, 1, False,
                              (tmpA[1], tmpB[1]), r=[], w=[ki32.b])
                    B.ld(self.OUT_KI[rows, :], ki32[:], w=[], r=[ki32.b])
                    B.cp("pool", kib2[:, 0:64], ki32[:], r=[ki32.b], w=[kib2.b])
                    B.cp("pool", kib2[:, 64:128], ki32[:], r=[ki32.b], w=[kib2.b])
                    B.ts("dve", WI[:, t, :], kr[:, 64:72], (8.0 ** -0.5) * 0.125, ALU.mult, r=[kr.b], w=[WI.b])
                    for pr in range(4):
                        B.tr(pT3[:, pr, :], qib[:, pr * 128:(pr + 1) * 128], idb[:], r=[qib.b, idb.b], w=[pT3.b])
                    B.tr(pT3[:, 4, :], kib2[:], idb[:], r=[kib2.b, idb.b], w=[pT3.b])
                    qiT = qiTr[t % 2]
                    B.cp("act", qiT[:], pT3[:, 0:5, :], r=[pT3.b], w=[qiT.b])
                    B.ld(self.QITs[:, :, rows].rearrange("h p n -> p h n"), qiT[:, 0:4, :], w=[], r=[qiT.b])
                    B.ld(self.KITs[:, rows], qiT[:, 4, :], w=[], r=[qiT.b])
                B.ld(self.QT[:, :, rows].rearrange("h p n -> p h n"), qT[:], w=[], r=[qT.b])
                B.ld(self.KT[:, :, rows].rearrange("h p n -> p h n"), kT[:], w=[], r=[kT.b])
                if kind == 0:
                    B.ld(self.CQT[:, :, rows].rearrange("h p n -> p h n"), cqT[:], w=[], r=[cqT.b])
            S.end_phase()

    def build(self):
        B = self.B
        with ExitStack() as top:
            B.S = Sched(B.nc, top)
            self.load_consts(top)
            for L in range(DEPTH):
                kind = L % 3
                with ExitStack() as lctx:
                    P = {}
                    if kind == 0:
                        P["NCUM"] = B.sb(lctx, "NCUM", [128, NT, 16], F32)
                        P["NCUMP"] = B.sb(lctx, "NCUMP", [128, NSEQ, NKP, 16], F32)
                    if kind == 2:
                        P["WI"] = B.sb(lctx, "WI", [128, NT, 8], F32)
                    self.phase1(L, P)
                    if STOP == "p1_%d" % L:
                        break
                    self.phase2(L, P)
                    if STOP == "p2_%d" % L:
                        break
                    self.phase3(L, P)
            B.S.end_phase()
            if os.environ.get("MK_VERBOSE"):
                print("instr per engine", {k: len(v) for k, v in B.S.streams.items()}, "waits", B.S.n_wait,
                      "sems", B.S.n_sems, flush=True)
            B.S.emit()
        return B.nc


def _bf(a):
    return np.asarray(a, np.float32).astype(ml_dtypes.bfloat16)


def make_consts():
    c = {}
    eye = np.eye(128, dtype=np.float32)
    c["c_identb"] = _bf(eye)
    c["c_onesb"] = _bf(np.ones((128, 128)))
    c["c_negI"] = _bf(NEG * eye)
    c["c_ones32"] = np.ones((128, 128), np.float32)
    kk = np.arange(128)
    c["c_tri"] = (kk[:, None] <= kk[None, :]).astype(np.float32)
    c["c_triS"] = ((kk[:, None] <= kk[None, :]) & (kk[:, None] // 32 == kk[None, :] // 32)).astype(np.float32)
    blk = np.zeros((128, 4, 128), np.float32)
    for s in range(4):
        blk[:, s, 32 * s:32 * s + 32] = 1.0
    c["c_blk"] = blk
    pos = np.concatenate([np.arange(PTOK), PAST + (np.arange(128) % 32)]).astype(np.float32)
    inv = (np.float32(10000.0) ** (-np.arange(32, dtype=np.float32) / np.float32(32))).astype(np.float32)
    ang = (pos[:, None] * inv[None, :]).astype(np.float32)
    cs, sn = np.cos(ang).astype(np.float32), np.sin(ang).astype(np.float32)
    c["c_rope"] = np.concatenate([cs, sn, cs * np.float32(0.125), sn * np.float32(0.125)], axis=1).astype(np.float32)
    a = np.arange(4)
    q = np.arange(512)
    kg = 128 * a[None, :, None] + kk[:, None, None]
    c["c_maskF"] = _bf((kg > q[None, None, :]).astype(np.float32))
    c["c_maskC"] = _bf(((kg // 64) > (q[None, None, :] // 64)).astype(np.float32))
    k32 = np.arange(32)
    c["c_maskS"] = _bf((k32[:, None] > k32[None, :]).astype(np.float32))
    c["c_cmaskA"] = np.where((kk[:, None] < 64) & (kk[None, :] >= 64), NEG, 0.0).astype(np.float32)
    colm = np.zeros((128, 4, 128), np.float32)
    for s in range(4):
        colm[:, s, 32 * s:32 * s + 32] = 1.0
    c["c_colm"] = _bf(colm)
    aug3 = np.zeros((128, 128), np.float32)
    aug3[0:3] = 1.0
    aug3[64:67] = 1.0
    c["c_aug3"] = _bf(aug3)
    return c


_PROG_CACHE = {}


def kernel(**inp):
    ncores = int(os.environ.get("MK_CORES", "8"))
    f32 = np.float32
    import time as _time
    _tp = _time.time()
    A = {k: np.asarray(v) for k, v in inp.items()}
    consts = make_consts()
    shared = {
        "wqkv": np.ascontiguousarray(np.stack([A["a_w_qkv"][0], A["b_w_qkv"][0], A["c_w_qkv"][0], A["a_w_qkv"][1]])),
        "wf": A["a_w_f"], "bf": A["a_b_f"],
        "wqi": A["c_w_qidx"][0], "wki": A["c_w_kidx"][0], "wwi": A["c_w_widx"][0],
        "wo": np.ascontiguousarray(np.stack([A["a_w_o"][0], A["b_w_o"][0], A["c_w_o"][0], A["a_w_o"][1]])),
        "wup": A["ffn_w_up"], "wdn": A["ffn_w_down"],
        "g_mix": A["norm_mix_g"], "g_ffn": A["norm_ffn_g"], "g_fin": A["norm_final_g"],
        "lamv": np.ascontiguousarray(np.stack([A["b_lam_q1"][0], A["b_lam_k1"][0], A["b_lam_q2"][0], A["b_lam_k2"][0]])),
        "subg": np.ascontiguousarray(A["b_subln_g"][0].reshape(128, 1)),
    }
    shared.update(consts)
    in_maps = []
    for c in range(ncores):
        sq = slice(4 * c, 4 * c + 4)
        m = dict(shared)
        m["xin"] = np.ascontiguousarray(np.concatenate(
            [A["x_prompt"][c % 4][:PTOK], A["x_sample"][sq].reshape(128, D)], axis=0))

        def kT(x):
            return np.ascontiguousarray(x.transpose(0, 2, 3, 1).reshape(NSEQ, 8, 128, PAST))
        m["a_kT"] = np.stack([kT(A["cache_a_k"][0, sq]), kT(A["cache_a_k"][1, sq])])
        m["a_v"] = np.ascontiguousarray(A["cache_a_v"][:, sq].reshape(2, NSEQ, PAST, D))
        m["a_lf"] = np.ascontiguousarray(A["cache_a_logf"][:, sq])
        m["b_kT"] = kT(A["cache_b_k"][0, sq])
        m["b_v"] = np.ascontiguousarray(A["cache_b_v"][0, sq].reshape(NSEQ, PAST, D))
        m["c_kT"] = kT(A["cache_c_k"][0, sq])
        m["c_v"] = np.ascontiguousarray(A["cache_c_v"][0, sq].reshape(NSEQ, PAST, D))
        m["c_kiT"] = np.ascontiguousarray(A["cache_c_kidx"][0, sq].transpose(0, 2, 1))
        in_maps.append(m)

    if os.environ.get("MK_VERBOSE"):
        print("prep time", _time.time() - _tp, flush=True)
    key = (NTP, DEPTH, STOP)
    _t0 = _time.time()
    if key not in _PROG_CACHE:
        _PROG_CACHE[key] = Prog().build()
    nc = _PROG_CACHE[key]
    if os.environ.get("MK_VERBOSE"):
        print("build time", _time.time() - _t0, flush=True)
    _t0 = _time.time()
    if os.environ.get("MK_TRACE"):
        res = run_bass_kernel_spmd(nc, in_maps, core_ids=list(range(ncores)), trace=True)
        print("exec_time_ns", res.exec_time_ns, flush=True)
    else:
        res = run_bass_kernel_spmd(nc, in_maps, core_ids=list(range(ncores)))
    if os.environ.get("MK_VERBOSE"):
        print("run time", _time.time() - _t0, flush=True)
    R = res.results

    def gp(name, layer=None):
        outs = []
        for b in range(4):
            x = R[b % ncores][name] if layer is None else R[b % ncores][name][layer]
            x = x[:PTOK]
            if PTOK < 4096:
                x = np.concatenate([x, np.zeros((4096 - PTOK,) + x.shape[1:], x.dtype)], 0)
            outs.append(x)
        return np.stack(outs)

    def gs(name, layer=None):
        outs = []
        for c in range(8):
            x = R[c % ncores][name] if layer is None else R[c % ncores][name][layer]
            outs.append(x[PTOK:PTOK + 128].reshape((4, 32) + x.shape[1:]))
        return np.concatenate(outs, 0)

    y_p = gp("OUT_Y").astype(f32)
    y_s = gs("OUT_Y").astype(f32)
    outs = [y_p, y_s]
    for g in (gp, gs):
        n = 4 if g is gp else 32
        t = 4096 if g is gp else 32
        outs.append(np.stack([g("OUT_K", 0), g("OUT_K", 3)]).reshape(2, n, t, 16, 64))
        outs.append(np.stack([g("OUT_V", 0), g("OUT_V", 3)]).reshape(2, n, t, 16, 64))
        outs.append(np.stack([g("OUT_LF", 0), g("OUT_LF", 1)]).reshape(2, n, t, 16))
        outs.append(g("OUT_K", 1).reshape(1, n, t, 16, 64))
        outs.append(g("OUT_V", 1).reshape(1, n, t, 8, 128))
        outs.append(g("OUT_K", 2).reshape(1, n, t, 16, 64))
        outs.append(g("OUT_V", 2).reshape(1, n, t, 16, 64))
        outs.append(g("OUT_KI").reshape(1, n, t, 64))
    return tuple(np.ascontiguousarray(o, dtype=f32) for o in outs)


class Unit:
    __slots__ = ("qk", "exp", "pv", "fin", "fin_b")

    def __init__(self, qk, exp, pv, fin=None, fin_b=None):
        self.qk, self.exp, self.pv, self.fin, self.fin_b = qk, exp, pv, fin, fin_b


def run_units_gen(units, pS, ptr, defer=3, la=1):
    pend = []
    n = len(units)

    def after_pv(u):
        if u.fin is not None:
            u.fin()
        if u.fin_b is not None:
            pend.append([defer, u.fin_b])

    def tick():
        for p in pend:
            p[0] -= 1
        while pend and pend[0][0] <= 0:
            pend.pop(0)[1]()

    for idx in range(n + la):
        if idx < n:
            units[idx].qk(pS[idx % len(pS)])
        j = idx - la
        if j >= 0:
            units[j].pv(ptr[j % len(ptr)])
            after_pv(units[j])
        tick()
        if idx < n:
            units[idx].exp(pS[idx % len(pS)], ptr[idx % len(ptr)])
        yield 1
    for p in pend:
        p[1]()
    yield 1


def run_units(units, pS, ptr, defer=3, la=1):
    for _ in run_units_gen(units, pS, ptr, defer, la):
        pass


def interleave(ga, ta, gb, tb):
    pa = pb = 0.0
    b_alive = gb is not None
    for wa in ga:
        pa += wa
        while b_alive and pb / tb < pa / ta:
            try:
                pb += next(gb)
            except StopIteration:
                b_alive = False
    while b_alive:
        try:
            next(gb)
        except StopIteration:
            b_alive = False


def _phase2_fox(self, L, P):
    B, S, C = self.B, self.B.S, self.C
    j = L // 3
    NCUM, NCUMP = P["NCUM"], P["NCUMP"]
    onesb, negI, ones32, maskF = C["onesb"], C["negI"], C["ones32"], C["maskF"]

    make_norm = self.make_norm

    def _unused(ctx, width):
        oar = ring(B, ctx, "oa", [65, width], F32, 2)
        RZ = B.sb(ctx, "RZ", [65, width], F32)
        B.memset("pool", RZ[:], 0.0, w=[RZ.b])
        pZ = B.ps(ctx, "pZ", [128, 512], F32)
        otr = ring(B, ctx, "ot", [64, width], BF16, 2)
        cnt = [0]

        def norm(pO, n, store):
            k = cnt[0]
            cnt[0] += 1
            oa, ot = oar[k % 2], otr[k % 2]
            B.cp("act", oa[0:65, 0:n], pO[0:65, 0:n], r=[pO.b], w=[oa.b])
            B.act(RZ[64:65, 0:n], oa[64:65, 0:n], AF.Ln, r=[oa.b], w=[RZ.b])
            B.act(RZ[64:65, 0:n], RZ[64:65, 0:n], AF.Exp, r=[RZ.b], w=[RZ.b], scale=-1.0)
            B.mm(pZ[0:64, 0:n], ones32[0:65, 0:64], RZ[0:65, 0:n], True, True, r=[ones32.b, RZ.b], w=[pZ.b])
            B.tt("dve", ot[0:64, 0:n], oa[0:64, 0:n], pZ[0:64, 0:n], ALU.mult, r=[oa.b, pZ.b], w=[ot.b])
            store(ot)
        return norm

    with ExitStack() as ctx:
        QTz = ring(B, ctx, "QTz", [128, 2, PTOK], BF16, 2)
        KTz = ring(B, ctx, "KTz", [128, 2, PTOK], BF16, 2)
        for t_ in QTz + KTz:
            B.memset("pool", t_[:], 0.0, w=[t_.b])
            B.memset("pool", t_[64:70, 0, :], 1.0, w=[t_.b])
            B.memset("pool", t_[0:6, 1, :], 1.0, w=[t_.b])
        Vh = ring(B, ctx, "Vh", [128, NTP, 2, 128], BF16, 2)
        pS = ring(B, ctx, "pS", [128, 1024], F32, 2, psum=True)
        pO = ring(B, ctx, "pO", [128, 512], F32, 3, psum=True)
        ptr = ring(B, ctx, "pt", [128, 1024], BF16, 3)
        norm = self.make_norm_pair(ctx, 512)
        VSv = self.VS[0:PTOK, :].rearrange("(kt p) (h e) -> p kt h e", p=128, e=128)
        g = 0
        for hp in range(8):
            qz, kz, vh = QTz[hp % 2], KTz[hp % 2], Vh[hp % 2]
            for i in range(2):
                pr = slice(64 * i, 64 * i + 64)
                a0 = 64 if i == 0 else 0
                c0 = 0 if i == 0 else 64
                B.ld(qz[pr, i, :], self.QT[hp, pr, 0:PTOK], w=[qz.b])
                B.ld(qz[a0:a0 + 3, i, :], self.CQT[hp, c0:c0 + 3, 0:PTOK], w=[qz.b])
                B.ld(kz[pr, i, :], self.KT[hp, pr, 0:PTOK], w=[kz.b])
                B.ld(kz[a0 + 3:a0 + 6, i, :], self.CQT[hp, c0 + 3:c0 + 6, 0:PTOK], w=[kz.b])
            for k0 in range(0, NTP, 8):
                k1 = min(NTP, k0 + 8)
                B.ld(vh[:, k0:k1, :, :], VSv[:, k0:k1, 2 * hp:2 * hp + 2, :], w=[vh.b])
            units = []
            for m in range(NTP // 4):
                nk = 4 * m + 4
                qs = slice(m * 512, (m + 1) * 512)
                for i in range(2):
                    po = pO[g % 3]
                    g += 1
                    for k2 in range(0, nk, 2):
                        def qk(ps, k2=k2, i=i, qs=qs, m=m):
                            for d in range(2):
                                kt = k2 + d
                                diag = kt >= 4 * m
                                B.mm(ps[:, d * 512:(d + 1) * 512], kz[:, i, kt * 128:(kt + 1) * 128], qz[:, i, qs],
                                     True, not diag, r=[kz.b, qz.b], w=[ps.b])
                                if diag:
                                    B.mm(ps[:, d * 512:(d + 1) * 512], negI[:, :], maskF[:, kt - 4 * m, :], False, True,
                                         r=[negI.b, maskF.b], w=[ps.b])

                        def ex(ps, pt):
                            B.act(pt[:, :], ps[:, :], AF.Exp, r=[ps.b], w=[pt.b])

                        def pv(pt, k2=k2, i=i, po=po, nk=nk):
                            for d in range(2):
                                kt = k2 + d
                                B.mm(po[:, :], vh[:, kt, i, :], pt[:, d * 512:(d + 1) * 512], kt == 0, kt == nk - 1,
                                     r=[vh.b, pt.b], w=[po.b])
                        fa = fb = None
                        if k2 == nk - 2:
                            fa, fb = norm(po, i, 512, lambda ot, hp=hp, qs=qs: B.ld(
                                self.OT[hp * 128:(hp + 1) * 128, qs], ot[:, :], w=[], r=[ot.b]))
                        units.append(Unit(qk, ex, pv, fa, fb))
            run_units(units, pS, ptr, defer=2)
        S.end_phase()

    self.sample65("fox", j, P, make_norm)


def _sample65(self, mode, j, P, make_norm):
    B, S, C = self.B, self.B.S, self.C
    onesb, negI, aug3 = C["onesb"], C["negI"], C["aug3"]
    fox = mode == "fox"
    if fox:
        NCUMP = P["NCUMP"]
        kT_src = lambda s, hp: self.a_kT[j, s, hp]
        v_src = lambda s, kt: self.a_v[j, s, kt * 128:(kt + 1) * 128, :]
    else:
        kT_src = lambda s, hp: self.c_kT[s, hp]
        v_src = lambda s, kt: self.c_v[s, kt * 128:(kt + 1) * 128, :]
    with ExitStack() as ctx:
        stg = ring(B, ctx, "stg", [128, 2048], F32, 3)
        KC = ring(B, ctx, "KC", [128, PAST], BF16, 2)
        VC = ring(B, ctx, "VC", [128, NKP, 16, 65], BF16, 2)
        for vc in VC:
            B.memset("pool", vc[:, :, :, 64:65], 1.0, w=[vc.b])
        QS = B.sb(ctx, "QS", [128, 8, 128], BF16)
        KS = B.sb(ctx, "KS", [128, 8, 128], BF16)
        sc = slice(PTOK, PTOK + 128)
        B.ld(QS[:], self.QT[:, :, sc].rearrange("h p n -> p h n"), w=[QS.b])
        B.ld(KS[:], self.KT[:, :, sc].rearrange("h p n -> p h n"), w=[KS.b])
        VN = B.sb(ctx, "VN", [32, NSEQ, 16 * 65], BF16)
        B.ld(VN[:], self.VSS.rearrange("(s p) x -> p s x", p=32), w=[VN.b])
        if fox:
            CQS = B.sb(ctx, "CQS", [128, 8, 128], BF16)
            B.ld(CQS[:], self.CQT[:, :, sc].rearrange("h p n -> p h n"), w=[CQS.b])
            NCSs = B.sb(ctx, "NCSs", [32, NSEQ, 16], F32)
            B.ld(NCSs[:], self.NCS.rearrange("(s p) h -> p s h", p=32), w=[NCSs.b])
            maskS = C["maskS"]
        if not fox:
            NMTS = self.dsa_sample_select(ctx, P)
        pS = ring(B, ctx, "pSs", [128, 512], F32, 3, psum=True)
        pO = ring(B, ctx, "pOs", [128, 512], F32, 2, psum=True)
        ptr = ring(B, ctx, "pts", [128, 32 if fox else 512], BF16, 4)
        norm = make_norm(ctx, 32)
        OTS = B.sb(ctx, "OTS", [64, 16, 128], BF16)
        si = 0
        g = 0
        for s in range(NSEQ):
            vc = VC[s % 2]
            for kt in range(NKP):
                st_ = stg[si % 3]
                si += 1
                B.ld(st_[:, 0:1024], v_src(s, kt), w=[st_.b])
                B.cp("act" if (not fox and kt % 2 == 1) else "pool", vc[:, kt, :, 0:64],
                     st_[:, 0:1024].rearrange("p (h e) -> p h e", e=64), r=[st_.b], w=[vc.b])
            qcs = slice(32 * s, 32 * s + 32)
            for hp in range(8):
                kc = KC[(s * 8 + hp) % 2]
                st_ = stg[si % 3]
                si += 1
                B.ld(st_[:, :], kT_src(s, hp), w=[st_.b])
                B.cp("act" if (not fox and hp % 2 == 1) else "dve", kc[:, :], st_[:, :], r=[st_.b], w=[kc.b])
                units = []
                for i in range(2):
                    h = 2 * hp + i
                    pr = slice(64 * i, 64 * i + 64)
                    po = pO[g % 2]
                    g += 1
                    if not fox:
                        def qk(ps, pr=pr, kc=kc, hp=hp, qcs=qcs):
                            for kt in range(NKP):
                                B.mm(ps[:, kt * 32:(kt + 1) * 32], kc[pr, kt * 128:(kt + 1) * 128], QS[pr, hp, qcs],
                                     True, False, r=[kc.b, QS.b], w=[ps.b])
                                B.mm(ps[:, kt * 32:(kt + 1) * 32], negI[:, :], NMTS[:, kt, qcs], False, True,
                                     r=[negI.b, NMTS.b], w=[ps.b])

                        def ex(ps, pt):
                            B.act(pt[:, 0:512], ps[:, 0:512], AF.Exp, r=[ps.b], w=[pt.b])

                        def pv(pt, h=h, po=po, vc=vc):
                            for kt in range(NKP):
                                B.mm(po[0:65, 0:32], vc[:, kt, h, :], pt[:, kt * 32:(kt + 1) * 32], kt == 0, False,
                                     r=[vc.b, pt.b], w=[po.b])
                        units.append(Unit(qk, ex, pv, None))
                    for kt in range(NKP + 1):
                        new = kt == NKP
                        if not fox and not new:
                            continue
                        if not new:
                            def qk(ps, kt=kt, pr=pr, kc=kc, hp=hp, qcs=qcs):
                                B.mm(ps[:, 0:32], kc[pr, kt * 128:(kt + 1) * 128], QS[pr, hp, qcs], True, False,
                                     r=[kc.b, QS.b], w=[ps.b])
                                if fox:
                                    B.mm(ps[:, 0:32], aug3[pr, 0:128], CQS[pr, hp, qcs], False, True,
                                         r=[aug3.b, CQS.b], w=[ps.b])
                                else:
                                    B.mm(ps[:, 0:32], negI[:, :], NMTS[:, kt, qcs], False, True,
                                         r=[negI.b, NMTS.b], w=[ps.b])

                            def ex(ps, pt, kt=kt, h=h, s=s):
                                if fox:
                                    B.act(pt[:, 0:32], ps[:, 0:32], AF.Exp, r=[ps.b, NCUMP.b], w=[pt.b],
                                          bias=NCUMP[:, s, kt, h:h + 1])
                                else:
                                    B.act(pt[:, 0:32], ps[:, 0:32], AF.Exp, r=[ps.b], w=[pt.b])

                            def pv(pt, kt=kt, h=h, po=po, vc=vc):
                                B.mm(po[0:65, 0:32], vc[:, kt, h, :], pt[:, 0:32], kt == 0, False, r=[vc.b, pt.b], w=[po.b])
                            fin = None
                        else:
                            def qk(ps, pr=pr, hp=hp, qcs=qcs):
                                B.mm(ps[0:32, 0:32], KS[pr, hp, qcs], QS[pr, hp, qcs], True, False, r=[KS.b, QS.b], w=[ps.b])
                                if fox:
                                    B.mm(ps[0:32, 0:32], aug3[pr, 0:32], CQS[pr, hp, qcs], False, False,
                                         r=[aug3.b, CQS.b], w=[ps.b])
                                    B.mm(ps[0:32, 0:32], negI[0:32, 0:32], maskS[0:32, 0:32], False, True,
                                         r=[negI.b, maskS.b], w=[ps.b])
                                else:
                                    B.mm(ps[0:32, 0:32], negI[0:32, 0:32], NMTS[0:32, NKP, qcs], False, True,
                                         r=[negI.b, NMTS.b], w=[ps.b])

                            def ex(ps, pt, h=h, s=s):
                                if fox:
                                    B.act(pt[0:32, 0:32], ps[0:32, 0:32], AF.Exp, r=[ps.b, NCSs.b], w=[pt.b],
                                          bias=NCSs[0:32, s, h:h + 1])
                                else:
                                    B.act(pt[0:32, 0:32], ps[0:32, 0:32], AF.Exp, r=[ps.b], w=[pt.b])

                            def pv(pt, h=h, po=po, s=s):
                                B.mm(po[0:65, 0:32], VN[0:32, s, h * 65:(h + 1) * 65], pt[0:32, 0:32], False, True,
                                     r=[VN.b, pt.b], w=[po.b])

                            def fin(po=po, h=h, qcs=qcs):
                                norm(po, 32, lambda ot: B.cp("pool", OTS[:, h, qcs], ot[:, 0:32], r=[ot.b], w=[OTS.b]))
                        units.append(Unit(qk, ex, pv, fin))
                run_units(units, pS, ptr, la=2)
        B.ld(self.OT[:, sc].rearrange("(h e) n -> e h n", e=64), OTS[:], w=[], r=[OTS.b])
        S.end_phase()


Prog.sample65 = _sample65


Prog.phase2_fox = _phase2_fox


def _phase2(self, L, P):
    kind = L % 3
    if kind == 0:
        self.phase2_fox(L, P)
    elif kind == 1:
        self.phase2_diff(L, P)
    else:
        self.phase2_dsa(L, P)


Prog.phase2 = _phase2


def _phase3(self, L, P):
    B, S, C = self.B, self.B.S, self.C
    kind = L % 3
    idb = C["identb"]
    last_layer = (L == 3)

    def rms(ht, g, out, junk, ss, lnv, rstd):
        B.act(junk[:], ht[:], AF.Square, r=[ht.b], w=[junk.b, ss.b], accum=ss[:])
        B.act(lnv[:], ss[:], AF.Ln, r=[ss.b], w=[lnv.b], scale=1.0 / D, bias=EPS)
        B.act(rstd[:], lnv[:], AF.Exp, r=[lnv.b], w=[rstd.b], scale=-0.5)
        B.stt(out[:], ht[:], rstd[:], g[:], ALU.mult, ALU.mult, r=[ht.b, rstd.b, g.b], w=[out.b])

    with ExitStack() as ctx:
        stg = ring(B, ctx, "stg", [128, 2048], F32, 3)
        self.stg_i = 0
        WO = B.sb(ctx, "WO", [128, 8, D], BF16)
        WOb = [Buf("WOc%d" % i) for i in range(8)]
        nch, kp = 8, 128
        OTv = self.OT.rearrange("(c p) n -> p c n", p=128)
        hr = ring(B, ctx, "ht", [128, D], F32, 2)
        otr = ring(B, ctx, "OTt", [kp, nch, 128], BF16, 2)
        B.ld(hr[0][:], (self.xin if L == 0 else self.HRES)[0:128, :], w=[hr[0].b])
        B.ld(otr[0][:], OTv[:, :, 0:128], w=[otr[0].b])
        self.load_w(stg, lambda i: WO[:, i, :], lambda i: self.wo[L, i * 128:(i + 1) * 128, :], 8, WO, bufs=WOb)
        g2 = B.sb(ctx, "g2", [128, D], F32)
        B.ld(g2[:], self.g_ffn[L].partition_broadcast(128), w=[g2.b])
        h2r = ring(B, ctx, "h2", [128, D], F32, 2)
        junk = B.sb(ctx, "junk", [128, D], BF16)
        ssr = ring(B, ctx, "ss", [128, 1], F32, 2)
        lnr = ring(B, ctx, "lnv", [128, 1], F32, 2)
        rsr = ring(B, ctx, "rstd", [128, 1], F32, 2)
        xnr = ring(B, ctx, "xn", [128, D], BF16, 2)
        xTr = ring(B, ctx, "xT", [128, 8, 128], BF16, 2)
        pY = ring(B, ctx, "pY", [128, 512], F32, 4, psum=True)
        pT = ring(B, ctx, "pT", [128, 8, 128], BF16, 2, psum=True)
        def oproj(t):
            rows = slice(t * 128, (t + 1) * 128)
            ht, ott = hr[t % 2], otr[t % 2]
            if t > 0:
                B.ld(ht[:], (self.xin if L == 0 else self.HRES)[rows, :], w=[ht.b])
                B.ld(ott[:], OTv[:, :, rows], w=[ott.b])
            for nb in range(2):
                py = pY[(2 * t + nb) % 4]
                for c in range(nch):
                    B.mm(py[:, :], ott[:, c, :], WO[:, c, nb * 512:(nb + 1) * 512], c == 0, c == nch - 1,
                         r=[ott.b, WOb[c]], w=[py.b])

        def post(t):
            rows = slice(t * 128, (t + 1) * 128)
            ht, h2 = hr[t % 2], h2r[t % 2]
            for nb in range(2):
                py = pY[(2 * t + nb) % 4]
                B.tt("dve", h2[:, nb * 512:(nb + 1) * 512], py[:, :], ht[:, nb * 512:(nb + 1) * 512], ALU.add,
                     r=[py.b, ht.b], w=[h2.b])
            B.ld(self.HRES[rows, :], h2[:], w=[], r=[h2.b])
            xn, xT = xnr[t % 2], xTr[t % 2]
            rms(h2, g2, xn, junk, ssr[t % 2], lnr[t % 2], rsr[t % 2])
            for c in range(8):
                B.tr(pT[t % 2][:, c, :], xn[:, c * 128:(c + 1) * 128], idb[:], r=[xn.b, idb.b], w=[pT[t % 2].b])
            B.cp("act", xT[:], pT[t % 2][:], r=[pT[t % 2].b], w=[xT.b])
            B.ld(self.XT2[t].rearrange("p (c n) -> p c n", c=8), xT[:], w=[], r=[xT.b])

        oproj(0)
        for t in range(NT):
            if t + 1 < NT:
                oproj(t + 1)
            post(t)
        S.end_phase()

    for p in range(2):
        with ExitStack() as ctx:
            stg = ring(B, ctx, "stg", [128, 2048], F32, 3)
            self.stg_i = 0
            WU = B.sb(ctx, "WU", [128, 8, 2048], BF16)
            WD = B.sb(ctx, "WD", [128, 16, D], BF16)
            WUb = [Buf("WUc%d" % i) for i in range(8)]
            WDb = [Buf("WDc%d" % i) for i in range(16)]
            xmr = ring(B, ctx, "xT2m", [128, 8, 512], BF16, 2)
            for i in range(macros()[0][1]):
                B.ld(xmr[0][:, :, i * 128:(i + 1) * 128], self.XT2[i].rearrange("p (c n) -> p c n", c=8), w=[xmr[0].b])
            self.load_w(stg, lambda i: WU[:, i, :],
                        lambda i: self.wup[L, i * 128:(i + 1) * 128, p * 2048:(p + 1) * 2048], 8, WU, bufs=WUb)
            self.load_w(stg, lambda i: WD[:, i, :],
                        lambda i: self.wdn[L, (p * 16 + i) * 128:(p * 16 + i + 1) * 128, :], 16, WD, bufs=WDb)
            h1r = ring(B, ctx, "h1T", [128, 16, 512], BF16, 2)
            h1b = [[Buf("h1b%d_%d" % (a_, q_)) for q_ in range(4)] for a_ in range(2)]
            rr = ring(B, ctx, "relu", [128, 512], F32, 2)
            hr = ring(B, ctx, "ht", [128, D], F32, 3)
            pU = ring(B, ctx, "pU", [128, 512], F32, 3, psum=True)
            pY = ring(B, ctx, "pY", [128, 512], F32, 4, psum=True)
            fin = last_layer and p == 1
            if fin:
                gf = B.sb(ctx, "gf", [128, D], F32)
                B.ld(gf[:], self.g_fin.partition_broadcast(128), w=[gf.b])
                junk = B.sb(ctx, "junk", [128, D], BF16)
                ssr = ring(B, ctx, "ss", [128, 1], F32, 2)
                lnr = ring(B, ctx, "lnv", [128, 1], F32, 2)
                rsr = ring(B, ctx, "rstd", [128, 1], F32, 2)
                yr = ring(B, ctx, "y32", [128, D], F32, 2)
            ui = 0
            hi = 0
            for mi, (t0, nt) in enumerate(macros()):
                ntok = nt * 128
                xm, h1 = xmr[mi % 2], h1r[mi % 2]
                for i in range(nt):
                    if mi > 0:
                        B.ld(xm[:, :, i * 128:(i + 1) * 128], self.XT2[t0 + i].rearrange("p (c n) -> p c n", c=8), w=[xm.b])
                for fc in range(16):
                    pu = pU[ui % 3]
                    rl = rr[ui % 2]
                    ui += 1
                    for c in range(8):
                        B.mm(pu[:, 0:ntok], WU[:, c, fc * 128:(fc + 1) * 128], xm[:, c, 0:ntok], c == 0, c == 7,
                             r=[WUb[c], xm.b], w=[pu.b])
                    B.act(rl[:, 0:ntok], pu[:, 0:ntok], AF.Relu, r=[pu.b], w=[rl.b])
                    B.tt("pool", h1[:, fc, 0:ntok], rl[:, 0:ntok], rl[:, 0:ntok], ALU.mult, r=[rl.b],
                         w=[h1b[mi % 2][fc // 4]])
                for i in range(nt):
                    t = t0 + i
                    rows = slice(t * 128, (t + 1) * 128)
                    ht = hr[hi % 3]
                    hi += 1
                    B.ld(ht[:], self.HRES[rows, :], w=[ht.b])
                    for nb in range(2):
                        py = pY[(2 * t + nb) % 4]
                        for fc in range(16):
                            B.mm(py[:, :], h1[:, fc, i * 128:(i + 1) * 128], WD[:, fc, nb * 512:(nb + 1) * 512],
                                 fc == 0, fc == 15, r=[h1b[mi % 2][fc // 4], WDb[fc]], w=[py.b])
                        B.tt("dve", ht[:, nb * 512:(nb + 1) * 512], py[:, :], ht[:, nb * 512:(nb + 1) * 512], ALU.add,
                             r=[py.b, ht.b], w=[ht.b])
                    if not fin:
                        B.ld(self.HRES[rows, :], ht[:], w=[], r=[ht.b])
                    else:
                        y = yr[t % 2]
                        rms(ht, gf, y, junk, ssr[t % 2], lnr[t % 2], rsr[t % 2])
                        B.ld(self.OUT_Y[rows, :], y[:], w=[], r=[y.b])
            S.end_phase()


Prog.phase3 = _phase3


def _phase2_diff(self, L, P):
    B, S, C = self.B, self.B.S, self.C
    onesb, negI, ones32, maskC = C["onesb"], C["negI"], C["ones32"], C["maskC"]
    lam_init = 0.8 - 0.6 * math.exp(-0.3 * L)

    def setup_common(ctx, width, pZring=None):
        lv = B.sb(ctx, "lv", [1, 256], F32)
        B.ld(lv[:], self.lamv.rearrange("a b -> (a b)").unsqueeze(0), w=[lv.b])
        lt = B.sb(ctx, "lt", [1, 128], F32)
        B.tt("dve", lt[:, 0:64], lv[:, 0:64], lv[:, 64:128], ALU.mult, r=[lv.b], w=[lt.b])
        B.tt("dve", lt[:, 64:128], lv[:, 128:192], lv[:, 192:256], ALU.mult, r=[lv.b], w=[lt.b])
        l2 = B.sb(ctx, "l2", [1, 2], F32)
        S.op("dve", lambda e: e.tensor_reduce(out=l2[:], in_=lt[:].rearrange("p (a b) -> p a b", a=2), axis=AX.X, op=ALU.add),
             reads=[lt.b], writes=[l2.b])
        B.act(l2[:], l2[:], AF.Exp, r=[l2.b], w=[l2.b])
        nlam = B.sb(ctx, "nlam", [1, 1], F32)
        B.tt("dve", nlam[:], l2[:, 1:2], l2[:, 0:1], ALU.subtract, r=[l2.b], w=[nlam.b])
        B.ts("dve", nlam[:], nlam[:], -lam_init, ALU.add, r=[nlam.b], w=[nlam.b])
        GSC = B.sb(ctx, "GSC", [128, 1], F32)
        B.ld(GSC[:], self.subg[:, :], w=[GSC.b])
        B.ts("dve", GSC[:], GSC[:], 1.0 - lam_init, ALU.mult, r=[GSC.b], w=[GSC.b])
        NL = B.sb(ctx, "NL128", [128, 1], F32)
        pZb_own = B.ps(ctx, "pZb", [128, 512], F32) if pZring is None else None
        nls = Buf("nls")
        B.ld(self.NLS[0:1, 0:1], nlam[0:1, 0:1], w=[nls], r=[nlam.b])
        B.ld(NL[:], self.NLS[0].partition_broadcast(128), w=[NL.b], r=[nls])
        rzr = ring(B, ctx, "rz", [128, width], F32, 2)
        t1r = ring(B, ctx, "t1", [128, width], F32, 2)
        t2 = B.sb(ctx, "t2", [128, width], F32)
        sqr = ring(B, ctx, "sq", [128, width], F32, 2)
        rs = B.sb(ctx, "rs", [128, width], F32)
        otr = ring(B, ctx, "otd", [128, width], BF16, 2)
        cnt = [0]

        def fin(i, po, pz, n, store):
            k = cnt[0]
            cnt[0] += 1
            rz = rzr[k % 2]
            pZb = pz if pZb_own is None else pZb_own
            t1, sq = t1r[(k // 2) % 2], sqr[(k // 2) % 2]

            def stage_a():
                S.op("dve", lambda e: e.reciprocal(out=rz[:, 0:n], in_=pz[:, 0:n]), reads=[pz.b], writes=[rz.b])
                if i == 0:
                    B.tt("dve", t1[:, 0:n], po[:, 0:n], rz[:, 0:n], ALU.mult, r=[po.b, rz.b], w=[t1.b])
                    return
                B.tt("dve", t2[:, 0:n], po[:, 0:n], rz[:, 0:n], ALU.mult, r=[po.b, rz.b], w=[t2.b])
                B.stt(t1[:, 0:n], t2[:, 0:n], NL[:, 0:1], t1[:, 0:n], ALU.mult, ALU.add, r=[t2.b, NL.b, t1.b], w=[t1.b])
                B.tt("pool", sq[:, 0:n], t1[:, 0:n], t1[:, 0:n], ALU.mult, r=[t1.b], w=[sq.b])

            def stage_b():
                B.mm(pZb[:, 0:n], ones32[:, :], sq[:, 0:n], True, True, r=[ones32.b, sq.b], w=[pZb.b])
                B.act(rs[:, 0:n], pZb[:, 0:n], AF.Ln, r=[pZb.b], w=[rs.b], scale=1.0 / 128, bias=SUBLN_EPS)
                B.act(rs[:, 0:n], rs[:, 0:n], AF.Exp, r=[rs.b], w=[rs.b], scale=-0.5)
                ot = otr[(k // 2) % 2]
                B.stt(ot[:, 0:n], t1[:, 0:n], GSC[:, 0:1], rs[:, 0:n], ALU.mult, ALU.mult, r=[t1.b, GSC.b, rs.b], w=[ot.b])
                store(ot)
            return stage_a, (stage_b if i == 1 else None)
        return fin

    with ExitStack() as ctx:
        QTh = ring(B, ctx, "QTz", [128, 2, PTOK], BF16, 2)
        for t_ in QTh:
            B.memset("pool", t_[:], 0.0, w=[t_.b])
        KTh = ring(B, ctx, "KTh", [128, PTOK], BF16, 2)
        Vd = ring(B, ctx, "Vd", [128, NTP, 128], BF16, 2)
        pS = ring(B, ctx, "pS", [128, 1024], F32, 2, psum=True)
        pO = ring(B, ctx, "pO", [128, 512], F32, 2, psum=True)
        pZ = ring(B, ctx, "pZ", [128, 512], F32, 2, psum=True)
        ptr = ring(B, ctx, "pt", [128, 1024], BF16, 3)
        fin_ = setup_common(ctx, 512, pZ)
        VSv = self.VS[0:PTOK, 0:D].rearrange("(kt p) (h e) -> p kt h e", p=128, e=128)
        g = 0
        for hd in range(8):
            qt, kt_, vd = QTh[hd % 2], KTh[hd % 2], Vd[hd % 2]
            B.ld(qt[0:64, 0, :], self.QT[hd, 0:64, 0:PTOK], w=[qt.b])
            B.ld(qt[64:128, 1, :], self.QT[hd, 64:128, 0:PTOK], w=[qt.b])
            B.ld(kt_[:], self.KT[hd, :, 0:PTOK], w=[kt_.b])
            for k0 in range(0, NTP, 8):
                k1 = min(NTP, k0 + 8)
                B.ld(vd[:, k0:k1, :], VSv[:, k0:k1, hd, :], w=[vd.b])
            units = []
            for m in range(NTP // 4):
                nk = 4 * m + 4
                qs = slice(m * 512, (m + 1) * 512)
                for i in range(2):
                    po, pz = pO[g % 2], pZ[g % 2]
                    g += 1
                    for k2 in range(0, nk, 2):
                        def qk(ps, k2=k2, i=i, qs=qs, m=m):
                            for d in range(2):
                                kt = k2 + d
                                diag = kt >= 4 * m
                                B.mm(ps[:, d * 512:(d + 1) * 512], kt_[:, kt * 128:(kt + 1) * 128], qt[:, i, qs],
                                     True, not diag, r=[kt_.b, qt.b], w=[ps.b])
                                if diag:
                                    B.mm(ps[:, d * 512:(d + 1) * 512], negI[:, :], maskC[:, kt - 4 * m, :], False, True,
                                         r=[negI.b, maskC.b], w=[ps.b])

                        def ex(ps, pt):
                            B.act(pt[:, :], ps[:, :], AF.Exp, r=[ps.b], w=[pt.b])

                        def pv(pt, k2=k2, po=po, pz=pz, nk=nk):
                            for d in range(2):
                                kt = k2 + d
                                B.mm(po[:, :], vd[:, kt, :], pt[:, d * 512:(d + 1) * 512], kt == 0, kt == nk - 1,
                                     r=[vd.b, pt.b], w=[po.b])
                                B.mm(pz[:, :], onesb[:, :], pt[:, d * 512:(d + 1) * 512], kt == 0, kt == nk - 1,
                                     r=[onesb.b, pt.b], w=[pz.b])
                        fa = fb = None
                        if k2 == nk - 2:
                            fa, fb = fin_(i, po, pz, 512, lambda ot, hd=hd, qs=qs: B.ld(
                                self.OT[hd * 128:(hd + 1) * 128, qs], ot[:, :], w=[], r=[ot.b]))
                        units.append(Unit(qk, ex, pv, fa, fb))
            run_units(units, pS, ptr, defer=4)
        S.end_phase()

    with ExitStack() as ctx:
        stg = ring(B, ctx, "stg", [128, 2048], F32, 3)
        KC = ring(B, ctx, "KC", [128, PAST], BF16, 2)
        VC = ring(B, ctx, "VC", [128, NKP, D], BF16, 2)
        QS = B.sb(ctx, "QS", [128, 8, 128], BF16)
        KS = B.sb(ctx, "KS", [128, 8, 128], BF16)
        sc = slice(PTOK, PTOK + 128)
        B.ld(QS[:], self.QT[:, :, sc].rearrange("h p n -> p h n"), w=[QS.b])
        B.ld(KS[:], self.KT[:, :, sc].rearrange("h p n -> p h n"), w=[KS.b])
        VN = B.sb(ctx, "VN", [32, NSEQ, D], BF16)
        B.ld(VN[:], self.VS[sc, 0:D].rearrange("(s p) x -> p s x", p=32), w=[VN.b])
        pS = ring(B, ctx, "pSs", [128, 512], F32, 3, psum=True)
        pO = ring(B, ctx, "pOs", [128, 512], F32, 2, psum=True)
        pZ = ring(B, ctx, "pZs", [128, 512], F32, 2, psum=True)
        ptr = ring(B, ctx, "pts", [128, 512], BF16, 4)
        fin_ = setup_common(ctx, 32)
        OTS = B.sb(ctx, "OTS", [128, 8, 128], BF16)
        si = 0
        g = 0
        for s in range(NSEQ):
            vc = VC[s % 2]
            for kt in range(NKP):
                st_ = stg[si % 3]
                si += 1
                B.ld(st_[:, 0:1024], self.b_v[s, kt * 128:(kt + 1) * 128, :], w=[st_.b])
                B.cp("act" if kt % 2 == 1 else "pool", vc[:, kt, :], st_[:, 0:1024], r=[st_.b], w=[vc.b])
            qcs = slice(32 * s, 32 * s + 32)
            for hd in range(8):
                kc = KC[(s * 8 + hd) % 2]
                st_ = stg[si % 3]
                si += 1
                B.ld(st_[:, :], self.b_kT[s, hd], w=[st_.b])
                B.cp("act" if hd % 2 == 1 else "dve", kc[:, :], st_[:, :], r=[st_.b], w=[kc.b])
                units = []
                for i in range(2):
                    pr = slice(64 * i, 64 * i + 64)
                    po, pz = pO[g % 2], pZ[g % 2]
                    g += 1
                    es = slice(hd * 128, (hd + 1) * 128)
                    def qk(ps, pr=pr, kc=kc, hd=hd, qcs=qcs):
                        for kt in range(NKP):
                            B.mm(ps[:, kt * 32:(kt + 1) * 32], kc[pr, kt * 128:(kt + 1) * 128], QS[pr, hd, qcs],
                                 True, True, r=[kc.b, QS.b], w=[ps.b])

                    def ex(ps, pt):
                        B.act(pt[:, 0:512], ps[:, 0:512], AF.Exp, r=[ps.b], w=[pt.b])

                    def pv(pt, po=po, pz=pz, vc=vc, es=es):
                        for kt in range(NKP):
                            B.mm(po[:, 0:32], vc[:, kt, es], pt[:, kt * 32:(kt + 1) * 32], kt == 0, False,
                                 r=[vc.b, pt.b], w=[po.b])
                            B.mm(pz[:, 0:32], onesb[:, :], pt[:, kt * 32:(kt + 1) * 32], kt == 0, False,
                                 r=[onesb.b, pt.b], w=[pz.b])
                    units.append(Unit(qk, ex, pv, None))
                    for kt in range(NKP + 1):
                        new = kt == NKP
                        if not new:
                            continue
                        if not new:
                            def qk(ps, kt=kt, pr=pr, kc=kc, hd=hd, qcs=qcs):
                                B.mm(ps[:, 0:32], kc[pr, kt * 128:(kt + 1) * 128], QS[pr, hd, qcs], True, True,
                                     r=[kc.b, QS.b], w=[ps.b])

                            def ex(ps, pt):
                                B.act(pt[:, 0:32], ps[:, 0:32], AF.Exp, r=[ps.b], w=[pt.b])

                            def pv(pt, kt=kt, po=po, pz=pz, vc=vc, es=es):
                                B.mm(po[:, 0:32], vc[:, kt, es], pt[:, 0:32], kt == 0, False, r=[vc.b, pt.b], w=[po.b])
                                B.mm(pz[:, 0:32], onesb[:, :], pt[:, 0:32], kt == 0, False, r=[onesb.b, pt.b], w=[pz.b])
                            fin = None
                        else:
                            def qk(ps, pr=pr, hd=hd, qcs=qcs):
                                B.mm(ps[0:32, 0:32], KS[pr, hd, qcs], QS[pr, hd, qcs], True, True, r=[KS.b, QS.b], w=[ps.b])

                            def ex(ps, pt):
                                B.act(pt[0:32, 0:32], ps[0:32, 0:32], AF.Exp, r=[ps.b], w=[pt.b])

                            def pv(pt, po=po, pz=pz, s=s, es=es):
                                B.mm(po[:, 0:32], VN[0:32, s, es], pt[0:32, 0:32], False, True, r=[VN.b, pt.b], w=[po.b])
                                B.mm(pz[:, 0:32], onesb[0:32, :], pt[0:32, 0:32], False, True, r=[onesb.b, pt.b], w=[pz.b])

                            def fin(i=i, po=po, pz=pz, hd=hd, qcs=qcs):
                                fa_, fb_ = fin_(i, po, pz, 32,
                                                lambda ot: B.cp("pool", OTS[:, hd, qcs], ot[:, 0:32], r=[ot.b], w=[OTS.b]))
                                fa_()
                                if fb_ is not None:
                                    fb_()
                        units.append(Unit(qk, ex, pv, fin))
                run_units(units, pS, ptr, la=2)
        B.ld(self.OT[:, sc].rearrange("(c p) n -> p c n", p=128), OTS[:], w=[], r=[OTS.b])
        S.end_phase()


Prog.phase2_diff = _phase2_diff


def _make_norm(self, ctx, width):
    B, C = self.B, self.C
    ones32 = C["ones32"]
    oar = ring(B, ctx, "oa", [65, width], F32, 2)
    RZ = B.sb(ctx, "RZ", [65, width], F32)
    B.memset("pool", RZ[:], 0.0, w=[RZ.b])
    pZ = B.ps(ctx, "pZ", [128, 512], F32)
    otr = ring(B, ctx, "ot", [64, width], BF16, 2)
    cnt = [0]

    def norm(pO, n, store):
        k = cnt[0]
        cnt[0] += 1
        oa, ot = oar[k % 2], otr[k % 2]
        B.cp("act", oa[0:65, 0:n], pO[0:65, 0:n], r=[pO.b], w=[oa.b])
        B.act(RZ[64:65, 0:n], oa[64:65, 0:n], AF.Ln, r=[oa.b], w=[RZ.b])
        B.act(RZ[64:65, 0:n], RZ[64:65, 0:n], AF.Exp, r=[RZ.b], w=[RZ.b], scale=-1.0)
        B.mm(pZ[0:64, 0:n], ones32[0:65, 0:64], RZ[0:65, 0:n], True, True, r=[ones32.b, RZ.b], w=[pZ.b])
        B.tt("dve", ot[0:64, 0:n], oa[0:64, 0:n], pZ[0:64, 0:n], ALU.mult, r=[oa.b, pZ.b], w=[ot.b])
        store(ot)
    return norm


Prog.make_norm = _make_norm


def _make_norm_pair(self, ctx, width):
    B, C = self.B, self.C
    ones32 = C["ones32"]
    oar = ring(B, ctx, "oap", [128, width], F32, 2)
    RZ = [B.sb(ctx, "RZp%d" % i, [65, width], F32) for i in range(2)]
    for rz in RZ:
        B.memset("pool", rz[:], 0.0, w=[rz.b])
    pZ = B.ps(ctx, "pZp", [128, 512], F32)
    otr = ring(B, ctx, "otp", [128, width], BF16, 2)
    cnt = [0]

    def norm(pO, i, n, store):
        k = cnt[0]
        cnt[0] += 1
        oa, ot, rz = oar[k % 2], otr[(k // 2) % 2], RZ[i]
        if i == 0:
            zr, orows = slice(64, 65), slice(0, 64)
        else:
            zr, orows = slice(0, 1), slice(64, 128)

        def stage_a():
            B.cp("dve", rz[zr, 0:n], pO[zr, 0:n], r=[pO.b], w=[rz.b])

        def stage_b():
            B.mm(pZ[orows, 0:n], ones32[0:65, 0:64], rz[0:65, 0:n], True, True, r=[ones32.b, rz.b], w=[pZ.b])
            B.S.op("dve", lambda e: e.reciprocal(out=oa[orows, 0:n], in_=pZ[orows, 0:n]), reads=[pZ.b], writes=[oa.b])
            B.tt("dve", ot[orows, 0:n], pO[orows, 0:n], oa[orows, 0:n], ALU.mult, r=[oa.b, pO.b], w=[ot.b])
            if i == 1:
                store(ot)
        return stage_a, stage_b
    return norm


Prog.make_norm_pair = _make_norm_pair


def _dsa_select(self, I, NM, Nk, wi_ap, small):
    B, S = self.B, self.B.S
    hi, mid, cnt, u = small
    S.op("dve", lambda e: e.tensor_reduce(out=hi[:], in_=I[:, 0:Nk], axis=AX.X, op=ALU.max), reads=[I.b], writes=[hi.b])
    B.ts("dve", mid[:], hi[:], -0.5 * BIS_RANGE, ALU.add, r=[hi.b], w=[mid.b])
    yield 1
    for n in range(1, NBIS + 1):
        B.ts("dve", NM[:, 0:Nk], I[:, 0:Nk], mid[:, 0:1], ALU.is_ge, r=[I.b, mid.b], w=[NM.b, cnt.b],
             s2=0.0, op1=ALU.add, accum=cnt[:])
        wn = BIS_RANGE / (2 ** n)
        if n < NBIS:
            wn1 = wn / 2
            B.ts("dve", u[:], cnt[:], 255.5, ALU.is_ge, r=[cnt.b], w=[u.b], s2=2 * wn1, op1=ALU.mult)
            B.stt(mid[:], u[:], -wn1, mid[:], ALU.add, ALU.add, r=[u.b, mid.b], w=[mid.b])
        else:
            B.ts("dve", u[:], cnt[:], 255.5, ALU.is_ge, r=[cnt.b], w=[u.b], s2=wn, op1=ALU.mult)
            B.stt(mid[:], u[:], -wn, mid[:], ALU.add, ALU.add, r=[u.b, mid.b], w=[mid.b])
        yield 1
    B.ts("dve", NM[:, 0:Nk], I[:, 0:Nk], mid[:, 0:1], ALU.is_lt, r=[I.b, mid.b], w=[NM.b])
    yield 1


Prog.dsa_select = _dsa_select


def _phase2_dsa(self, L, P):
    B, S, C = self.B, self.B.S, self.C
    WI = P["WI"]
    negI, idb, cmaskA = C["negI"], C["identb"], C["cmaskA"]
    make_norm = self.make_norm
    with ExitStack() as ctx:
        KIT = B.sb(ctx, "KIT", [128, PTOK], BF16)
        B.ld(KIT[:], self.KITs[:, 0:PTOK], w=[KIT.b])
        QIr = ring(B, ctx, "QIt", [128, 4, 128], BF16, 2)
        I = B.sb(ctx, "Isc", [128, PTOK], F32)
        NM = B.sb(ctx, "NM", [128, PTOK], BF16)
        Rr = ring(B, ctx, "Rr", [128, 512], F32, 3)
        NMT = ring(B, ctx, "NMT", [128, NTP, 512], BF16, 2)
        small = [B.sb(ctx, "bs%d" % k, [128, 1], F32) for k in range(4)]
        pI = ring(B, ctx, "pI", [128, 512], F32, 2, psum=True)
        pTn = B.ps(ctx, "pTn", [128, 4, 128], BF16)
        qtr = ring(B, ctx, "qt", [128, 2, 512], BF16, 2)
        for t_ in qtr:
            B.memset("pool", t_[:], 0.0, w=[t_.b])
        ktr = ring(B, ctx, "kt", [128, PTOK], BF16, 2)
        vhr = ring(B, ctx, "vh", [128, NTP, 2, 128], BF16, 2)
        pS = ring(B, ctx, "pS", [128, 512], F32, 2, psum=True)
        pO = ring(B, ctx, "pO", [128, 512], F32, 2, psum=True)
        ptr = ring(B, ctx, "pt", [128, 512], BF16, 3)
        norm = self.make_norm_pair(ctx, 512)
        VSv = self.VS[0:PTOK, :].rearrange("(kt p) (h e) -> p kt h e", p=128, e=128)
        cnts = {"ri": 0, "li": 0, "g": 0}

        def step_a(i):
            m = i // 4
            nmt = NMT[m % 2]
            Nk = 128 * (i + 1)
            qi = QIr[i % 2]
            B.ld(qi[:], self.QITs[:, :, i * 128:(i + 1) * 128].rearrange("h p n -> p h n"), w=[qi.b])
            for kb in range((Nk + 511) // 512):
                w = min(512, Nk - 512 * kb)
                ks = slice(512 * kb, 512 * kb + w)
                for h8 in range(8):
                    pr = slice(64 * (h8 % 2), 64 * (h8 % 2) + 64)
                    pi, R = pI[cnts["ri"] % 2], Rr[cnts["ri"] % 3]
                    cnts["ri"] += 1
                    B.mm(pi[:, 0:w], qi[pr, h8 // 2, :], KIT[pr, ks], True, True, r=[qi.b, KIT.b], w=[pi.b])
                    B.act(R[:, 0:w], pi[:, 0:w], AF.Relu, r=[pi.b], w=[R.b])
                    if h8 == 0:
                        B.ts("dve", I[:, ks], R[:, 0:w], WI[:, i, 0:1], ALU.mult, r=[R.b, WI.b], w=[I.b])
                    else:
                        B.stt(I[:, ks], R[:, 0:w], WI[:, i, h8:h8 + 1], I[:, ks], ALU.mult, ALU.add,
                              r=[R.b, WI.b, I.b], w=[I.b])
                    yield 0.3 + w * 0.0011
            ds_ = slice(128 * i, 128 * i + 128)
            B.tt("dve", I[:, ds_], I[:, ds_], cmaskA[:, :], ALU.add, r=[I.b, cmaskA.b], w=[I.b])
            for _ in self.dsa_select(I, NM, Nk, None, small):
                yield 0.4 + Nk * 0.0015
            cs = slice((i % 4) * 128, (i % 4) * 128 + 128)
            for k0 in range(0, i + 1, 4):
                n = min(4, i + 1 - k0)
                for jj in range(n):
                    B.tr(pTn[:, jj, :], NM[:, (k0 + jj) * 128:(k0 + jj + 1) * 128], idb[:], r=[NM.b, idb.b], w=[pTn.b])
                B.cp("act", nmt[:, k0:k0 + n, cs], pTn[:, 0:n, :], r=[pTn.b], w=[nmt.b])
            for kt in range(i + 1, 4 * m + 4):
                B.memset("pool", nmt[:, kt, cs], 1.0, w=[nmt.b])
            yield 1.0

        def t_a(i):
            Nk = 128 * (i + 1)
            t = 1.0
            for kb in range((Nk + 511) // 512):
                w = min(512, Nk - 512 * kb)
                t += 8 * (0.3 + w * 0.0011)
            return t + (NBIS + 2) * (0.4 + Nk * 0.0015)

        def step_b(m, hps):
            for c0 in range(0, len(hps), 2):
                for _ in step_b2(m, hps[c0:c0 + 2]):
                    pass

        def step_b2(m, hps):
            nmt = NMT[m % 2]
            nk = 4 * m + 4
            qs = slice(m * 512, (m + 1) * 512)
            units = []
            for hp in hps:
                li = cnts["li"]
                cnts["li"] += 1
                qt, kt_, vh = qtr[li % 2], ktr[li % 2], vhr[li % 2]
                B.ld(qt[0:64, 0, :], self.QT[hp, 0:64, qs], w=[qt.b])
                B.ld(qt[64:128, 1, :], self.QT[hp, 64:128, qs], w=[qt.b])
                B.ld(kt_[:, 0:nk * 128], self.KT[hp, :, 0:nk * 128], w=[kt_.b])
                for k0 in range(0, nk, 8):
                    k1 = min(nk, k0 + 8)
                    B.ld(vh[:, k0:k1, :, :], VSv[:, k0:k1, 2 * hp:2 * hp + 2, :], w=[vh.b])
                for i in range(2):
                    po = pO[cnts["g"] % 2]
                    cnts["g"] += 1
                    for kt in range(nk):
                        def qk(ps, kt=kt, i=i, qt=qt, kt_=kt_):
                            B.mm(ps[:, :], kt_[:, kt * 128:(kt + 1) * 128], qt[:, i, :], True, False, r=[kt_.b, qt.b], w=[ps.b])
                            B.mm(ps[:, :], negI[:, :], nmt[:, kt, :], False, True, r=[negI.b, nmt.b], w=[ps.b])

                        def ex(ps, pt):
                            B.act(pt[:, :], ps[:, :], AF.Exp, r=[ps.b], w=[pt.b])

                        def pv(pt, kt=kt, i=i, po=po, vh=vh, nk=nk):
                            B.mm(po[:, :], vh[:, kt, i, :], pt[:, :], kt == 0, kt == nk - 1, r=[vh.b, pt.b], w=[po.b])
                        fa = fb = None
                        if kt == nk - 1:
                            fa, fb = norm(po, i, 512, lambda ot, hp=hp, qs=qs: B.ld(
                                self.OT[hp * 128:(hp + 1) * 128, qs], ot[:, :], w=[], r=[ot.b]))
                        units.append(Unit(qk, ex, pv, fa, fb))
            for wv in run_units_gen(units, pS, ptr, defer=2):
                yield wv

        nm_ = NTP // 4
        for i in range(4):
            for _ in step_a(i):
                pass
        for m in range(1, nm_):
            for c in range(4):
                nb = 4 * (4 * (m - 1) + 4) + 1
                interleave(step_a(4 * m + c), t_a(4 * m + c), step_b2(m - 1, [2 * c, 2 * c + 1]), float(nb))
        step_b(nm_ - 1, list(range(8)))
        S.end_phase()
    self.sample65("dsa", 0, P, make_norm)


def _dsa_sample_select(self, ctx, P):
    B, S, C = self.B, self.B.S, self.C
    WI = P["WI"]
    idb, colm = C["identb"], C["colm"]
    NKS = PAST + 32
    sc = slice(PTOK, PTOK + 128)
    stg = ring(B, ctx, "stgk", [128, 2048], F32, 2)
    KIC = B.sb(ctx, "KIC", [128, NSEQ, PAST], BF16)
    for s in range(NSEQ):
        st_ = stg[s % 2]
        B.ld(st_[0:64, :], self.c_kiT[s], w=[st_.b])
        B.ld(st_[64:128, :], self.c_kiT[s], w=[st_.b])
        B.cp("pool", KIC[:, s, :], st_[:, :], r=[st_.b], w=[KIC.b])
    QIS = B.sb(ctx, "QIS", [128, 4, 128], BF16)
    B.ld(QIS[:], self.QITs[:, :, sc].rearrange("h p n -> p h n"), w=[QIS.b])
    QISm = B.sb(ctx, "QISm", [128, NSEQ, 4, 128], BF16)
    for s in range(NSEQ):
        B.tt("pool", QISm[:, s, :, :], QIS[:], colm[:, s, :].unsqueeze(1).to_broadcast([128, 4, 128]), ALU.mult,
             r=[QIS.b, colm.b], w=[QISm.b])
    KIN = B.sb(ctx, "KIN", [128, 128], BF16)
    B.ld(KIN[:], self.KITs[:, sc], w=[KIN.b])
    I = B.sb(ctx, "IscS", [128, NKS], F32)
    NM = B.sb(ctx, "NMs", [128, NKS], BF16)
    Rr = ring(B, ctx, "RrS", [128, 512], F32, 2)
    small = [B.sb(ctx, "bss%d" % k, [128, 1], F32) for k in range(4)]
    NMTS = B.sb(ctx, "NMTS", [128, NKP + 1, 128], BF16)
    with ExitStack() as pctx:
        pI = ring(B, pctx, "pIs", [128, 512], F32, 2, psum=True)
        pTn = B.ps(pctx, "pTns", [128, 4, 128], BF16)
        ri = 0
        for kb in range(5):
            w = 512 if kb < 4 else 32
            ks = slice(512 * kb, 512 * kb + w)
            for h8 in range(8):
                pr = slice(64 * (h8 % 2), 64 * (h8 % 2) + 64)
                pi, R = pI[ri % 2], Rr[ri % 2]
                ri += 1
                for s in range(NSEQ):
                    rhs = KIC[pr, s, ks] if kb < 4 else KIN[pr, 32 * s:32 * s + 32]
                    B.mm(pi[:, 0:w], QISm[pr, s, h8 // 2, :], rhs, s == 0, s == NSEQ - 1,
                         r=[QISm.b, KIC.b, KIN.b], w=[pi.b])
                B.act(R[:, 0:w], pi[:, 0:w], AF.Relu, r=[pi.b], w=[R.b])
                if h8 == 0:
                    B.ts("dve", I[:, ks], R[:, 0:w], WI[:, NTP, 0:1], ALU.mult, r=[R.b, WI.b], w=[I.b])
                else:
                    B.stt(I[:, ks], R[:, 0:w], WI[:, NTP, h8:h8 + 1], I[:, ks], ALU.mult, ALU.add,
                          r=[R.b, WI.b, I.b], w=[I.b])
        for _ in self.dsa_select(I, NM, NKS, None, small):
            pass
        for k0 in range(0, NKP + 1, 4):
            n = min(4, NKP + 1 - k0)
            for jj in range(n):
                kt = k0 + jj
                if kt < NKP:
                    B.tr(pTn[:, jj, :], NM[:, kt * 128:(kt + 1) * 128], idb[:], r=[NM.b, idb.b], w=[pTn.b])
                else:
                    B.tr(pTn[0:32, jj, :], NM[:, PAST:PAST + 32], idb[:], r=[NM.b, idb.b], w=[pTn.b])
            if k0 + n <= NKP:
                B.cp("act", NMTS[:, k0:k0 + n, :], pTn[:, 0:n, :], r=[pTn.b], w=[NMTS.b])
            else:
                if n > 1:
                    B.cp("act", NMTS[:, k0:k0 + n - 1, :], pTn[:, 0:n - 1, :], r=[pTn.b], w=[NMTS.b])
                B.cp("act", NMTS[0:32, NKP, :], pTn[0:32, n - 1, :], r=[pTn.b], w=[NMTS.b])
    return NMTS


Prog.phase2_dsa = _phase2_dsa
Prog.dsa_sample_select = _dsa_sample_select
```
